# Optimizing a Trainium2 kernel written in Bass

```python
import jax, jax.numpy as jnp
from jax import lax
import numpy as np

D_MODEL = 1024
BATCH = 16
SEQ = 4096
DEPTH = 2

EXPAND = 2
D_INNER = EXPAND * D_MODEL
EPS = 1e-6

CONV_WIDTH = D_INNER // 2
ATTN_WIDTH = D_INNER - CONV_WIDTH
SB_HEAD_DIM = 128
SB_HEADS = ATTN_WIDTH // SB_HEAD_DIM
CONF_KERNEL = 31
Q_BLOCK = 128
EVEN_SPLITS = [CONV_WIDTH, 2 * CONV_WIDTH, 3 * CONV_WIDTH,
               3 * CONV_WIDTH + ATTN_WIDTH, 3 * CONV_WIDTH + 2 * ATTN_WIDTH,
               3 * CONV_WIDTH + 3 * ATTN_WIDTH]
IN_EVEN = 3 * CONV_WIDTH + 4 * ATTN_WIDTH

SSM_HEAD_DIM = 64
SSM_HEADS = D_INNER // SSM_HEAD_DIM
SSM_GROUPS = 4
SSM_STATE = 128
SSM_CONV = 4
SSM_CHUNK = 128
XBC_WIDTH = D_INNER + 2 * SSM_GROUPS * SSM_STATE
IN_ODD = D_INNER + XBC_WIDTH + SSM_HEADS
DT_MIN = 0.001
DT_MAX = 0.1

N_EVEN = (DEPTH + 1) // 2
N_ODD = DEPTH // 2

kernel_name = "hybrid_conformer_stickbreak_ssd"


def rmsnorm(x, w):
    xf = x.astype(jnp.float32)
    y = xf * lax.rsqrt(jnp.mean(xf * xf, axis=-1, keepdims=True) + EPS)
    return (y * w.astype(jnp.float32)).astype(x.dtype)


def layernorm(x, w, b):
    xf = x.astype(jnp.float32)
    mu = jnp.mean(xf, axis=-1, keepdims=True)
    xc = xf - mu
    var = jnp.mean(xc * xc, axis=-1, keepdims=True)
    return xc * lax.rsqrt(var + EPS) * w.astype(jnp.float32) + b.astype(jnp.float32)


def causal_dwconv(x, w, b):
    k_width = w.shape[0]
    out = lax.conv_general_dilated(
        x, w[:, None, :].astype(x.dtype), window_strides=(1,),
        padding=((k_width - 1, 0),), dimension_numbers=('NWC', 'WIO', 'NWC'),
        feature_group_count=x.shape[-1])
    return out + b.astype(x.dtype)


def stick_breaking_attention(q, k, v):
    S = q.shape[1]
    dh = q.shape[-1]
    qf = q.astype(jnp.float32) * (dh ** -0.5)
    kf = k.astype(jnp.float32)
    vf = v.astype(jnp.float32)
    outs = []
    for i in range(S // Q_BLOCK):
        t0 = i * Q_BLOCK
        kl = t0 + Q_BLOCK
        z = jnp.einsum('bthd,bshd->bhts', qf[:, t0:kl], kf[:, :kl])
        t_idx = t0 + jnp.arange(Q_BLOCK)
        s_idx = jnp.arange(kl)
        mask = s_idx[None, :] < t_idx[:, None]
        log_keep = jnp.where(mask, jax.nn.log_sigmoid(-z), 0.0)
        later = lax.cumsum(log_keep, axis=3, reverse=True) - log_keep
        wts = jnp.where(mask, jnp.exp(jax.nn.log_sigmoid(z) + later), 0.0)
        outs.append(jnp.einsum('bhts,bshd->bthd', wts, vf[:, :kl]))
    return jnp.concatenate(outs, axis=1).astype(q.dtype)


def ssd_scan(x, dt, a, bm, cm):
    bsz, S, H, P = x.shape
    G, N = bm.shape[2], bm.shape[3]
    R = H // G
    L = SSM_CHUNK
    nc = S // L
    xs = (x * dt[..., None]).reshape(bsz, nc, L, G, R, P).transpose(1, 0, 2, 3, 4, 5)
    la = (dt * a).reshape(bsz, nc, L, G, R).transpose(1, 0, 3, 4, 2)
    bc = bm.reshape(bsz, nc, L, G, N).transpose(1, 0, 2, 3, 4)
    cc = cm.reshape(bsz, nc, L, G, N).transpose(1, 0, 2, 3, 4)
    causal = jnp.tril(jnp.ones((L, L), dtype=bool))

    def step(state, inp):
        xck, ack, bck, cck = inp
        cs = jnp.cumsum(ack, axis=-1)
        seg = cs[..., :, None] - cs[..., None, :]
        decay = jnp.exp(jnp.where(causal, seg, -jnp.inf))
        cb = jnp.einsum('blgn,bsgn->bgls', cck, bck)
        y = jnp.einsum('bgls,bgrls,bsgrp->blgrp', cb, decay, xck)
        y = y + jnp.einsum('blgn,bgrpn,bgrl->blgrp', cck, state, jnp.exp(cs))
        tail = jnp.exp(cs[..., -1:] - cs)
        new_state = (state * jnp.exp(cs[..., -1])[..., None, None]
                     + jnp.einsum('bsgn,bgrs,bsgrp->bgrpn', bck, tail, xck))
        return new_state, y

    init = jnp.zeros((bsz, G, R, P, N), jnp.float32)
    _, ys = lax.scan(step, init, (xs, la, bc, cc))
    return ys.transpose(1, 0, 2, 3, 4, 5).reshape(bsz, S, H, P)


def conv_attn_mixer(h, w_in, dw_w, dw_b, ln_w, ln_b, w_out):
    bsz, S, _ = h.shape
    proj = h @ w_in
    glu_a, glu_b, gate_c, q, k, v, gate_a = jnp.split(proj, EVEN_SPLITS, axis=-1)
    u = glu_a * jax.nn.sigmoid(glu_b)
    u = causal_dwconv(u, dw_w, dw_b)
    u = jax.nn.silu(layernorm(u, ln_w, ln_b))
    y_conv = (u * jax.nn.silu(gate_c.astype(jnp.float32))).astype(h.dtype)
    shp = (bsz, S, SB_HEADS, SB_HEAD_DIM)
    o = stick_breaking_attention(q.reshape(shp), k.reshape(shp), v.reshape(shp))
    y_attn = (o.reshape(bsz, S, ATTN_WIDTH).astype(jnp.float32)
              * jax.nn.silu(gate_a.astype(jnp.float32))).astype(h.dtype)
    return jnp.concatenate([y_conv, y_attn], axis=-1) @ w_out


def mamba2_mixer(h, w_in, conv_w, conv_b, dt_bias, a_log, d_skip, norm_w, w_out):
    bsz, S, _ = h.shape
    gn = SSM_GROUPS * SSM_STATE
    proj = h @ w_in
    z, xbc, dt = jnp.split(proj, [D_INNER, D_INNER + XBC_WIDTH], axis=-1)
    xbc = jax.nn.silu(causal_dwconv(xbc, conv_w, conv_b))
    xs, bm, cm = jnp.split(xbc, [D_INNER, D_INNER + gn], axis=-1)
    xs = xs.astype(jnp.float32).reshape(bsz, S, SSM_HEADS, SSM_HEAD_DIM)
    bm = bm.astype(jnp.float32).reshape(bsz, S, SSM_GROUPS, SSM_STATE)
    cm = cm.astype(jnp.float32).reshape(bsz, S, SSM_GROUPS, SSM_STATE)
    dt = jax.nn.softplus(dt.astype(jnp.float32) + dt_bias.astype(jnp.float32))
    a = -jnp.exp(a_log.astype(jnp.float32))
    y = ssd_scan(xs, dt, a, bm, cm)
    y = y + d_skip.astype(jnp.float32)[:, None] * xs
    y = y.reshape(bsz, S, D_INNER) * jax.nn.silu(z.astype(jnp.float32))
    yg = y.reshape(bsz, S, SSM_GROUPS, D_INNER // SSM_GROUPS)
    yg = yg * lax.rsqrt(jnp.mean(yg * yg, axis=-1, keepdims=True) + EPS)
    y = yg.reshape(bsz, S, D_INNER) * norm_w.astype(jnp.float32)
    return y.astype(h.dtype) @ w_out


def setup_inputs(seed: int = 0) -> dict:
    key = jax.random.key(seed)
    ks = jax.random.split(key, 20)
    f32 = jnp.float32
    nrm = lambda k, shp, s: jax.random.normal(k, shp, f32) * s
    x = jax.random.normal(ks[0], (BATCH, SEQ, D_MODEL), f32)
    ev_norm_w = 1.0 + nrm(ks[1], (N_EVEN, D_MODEL), 0.02)
    ev_w_in = nrm(ks[2], (N_EVEN, D_MODEL, IN_EVEN), D_MODEL ** -0.5)
    ev_dw_w = nrm(ks[3], (N_EVEN, CONF_KERNEL, CONV_WIDTH), CONF_KERNEL ** -0.5)
    ev_dw_b = nrm(ks[4], (N_EVEN, CONV_WIDTH), 0.02)
    ev_ln_w = 1.0 + nrm(ks[5], (N_EVEN, CONV_WIDTH), 0.02)
    ev_ln_b = nrm(ks[6], (N_EVEN, CONV_WIDTH), 0.02)
    ev_w_out = nrm(ks[7], (N_EVEN, D_INNER, D_MODEL), D_INNER ** -0.5)
    od_norm_w = 1.0 + nrm(ks[8], (N_ODD, D_MODEL), 0.02)
    od_w_in = nrm(ks[9], (N_ODD, D_MODEL, IN_ODD), D_MODEL ** -0.5)
    od_conv_w = nrm(ks[10], (N_ODD, SSM_CONV, XBC_WIDTH), SSM_CONV ** -0.5)
    od_conv_b = nrm(ks[11], (N_ODD, XBC_WIDTH), 0.02)
    u = jax.random.uniform(ks[12], (N_ODD, SSM_HEADS), f32)
    dt0 = jnp.exp(u * (np.log(DT_MAX) - np.log(DT_MIN)) + np.log(DT_MIN))
    od_dt_bias = dt0 + jnp.log(-jnp.expm1(-dt0))
    od_a_log = jnp.log(jax.random.uniform(ks[13], (N_ODD, SSM_HEADS), f32, 1.0, 16.0))
    od_d = 1.0 + nrm(ks[14], (N_ODD, SSM_HEADS), 0.1)
    od_gnorm_w = 1.0 + nrm(ks[15], (N_ODD, D_INNER), 0.02)
    od_w_out = nrm(ks[16], (N_ODD, D_INNER, D_MODEL), D_INNER ** -0.5)
    final_norm_w = 1.0 + nrm(ks[17], (D_MODEL,), 0.02)
    return {"x": x, "ev_norm_w": ev_norm_w, "ev_w_in": ev_w_in, "ev_dw_w": ev_dw_w,
            "ev_dw_b": ev_dw_b, "ev_ln_w": ev_ln_w, "ev_ln_b": ev_ln_b, "ev_w_out": ev_w_out,
            "od_norm_w": od_norm_w, "od_w_in": od_w_in, "od_conv_w": od_conv_w,
            "od_conv_b": od_conv_b, "od_dt_bias": od_dt_bias, "od_a_log": od_a_log,
            "od_d": od_d, "od_gnorm_w": od_gnorm_w, "od_w_out": od_w_out,
            "final_norm_w": final_norm_w}


def reference(x, ev_norm_w, ev_w_in, ev_dw_w, ev_dw_b, ev_ln_w, ev_ln_b, ev_w_out,
              od_norm_w, od_w_in, od_conv_w, od_conv_b, od_dt_bias, od_a_log, od_d,
              od_gnorm_w, od_w_out, final_norm_w):
    h = x
    for layer in range(DEPTH):
        i = layer // 2
        if layer % 2 == 0:
            h = h + conv_attn_mixer(rmsnorm(h, ev_norm_w[i]), ev_w_in[i], ev_dw_w[i],
                                    ev_dw_b[i], ev_ln_w[i], ev_ln_b[i], ev_w_out[i])
        else:
            h = h + mamba2_mixer(rmsnorm(h, od_norm_w[i]), od_w_in[i], od_conv_w[i],
                                 od_conv_b[i], od_dt_bias[i], od_a_log[i], od_d[i],
                                 od_gnorm_w[i], od_w_out[i])
    return rmsnorm(h, final_norm_w)
```

```python
import numpy as np
from contextlib import ExitStack
import concourse.bass as bass
import concourse.mybir as mybir
from concourse.bass_utils import run_bass_kernel_spmd

F32 = mybir.dt.float32
BF16 = mybir.dt.bfloat16
AF = mybir.ActivationFunctionType
ALU = mybir.AluOpType
AX = mybir.AxisListType

T = 4096
D = 1024
NCORES = 8
EPS = 1e-6
IN_EVEN = 7168
IN_ODD = 5152


class _Op:
    __slots__ = ("eng", "fn", "deps", "sig", "sigval", "isdma", "dma_i")


class Sched:
    ENGS = ("sp", "act", "dve", "pool", "pe")

    def __init__(self, nc, ring=8):
        self.nc = nc
        self.ops = {e: [] for e in self.ENGS}
        self.res = {}
        self.ndma = {e: 0 for e in self.ENGS}
        self.ring = ring
        self.pend = {e: [] for e in self.ENGS}
        self.alldma = []

    def barrier(self):
        lst = []
        for e in self.ENGS:
            for op in reversed(self.ops[e]):
                if not op.isdma:
                    lst.append(op)
                    break
        lst.extend(self.alldma)
        self.alldma = []
        for e in self.ENGS:
            self.pend[e] = list(lst)

    def _add(self, eng, fn, reads, writes, pwrites, isdma):
        op = _Op()
        op.eng, op.fn, op.isdma, op.sig, op.sigval, op.dma_i = eng, fn, isdma, False, 0, -1
        deps = {}
        for k in reads:
            r = self.res.get(k)
            if r:
                for w in r["w"]:
                    deps[id(w)] = w
        for k in list(writes) + list(pwrites):
            r = self.res.get(k)
            if r:
                for w in r["w"]:
                    deps[id(w)] = w
                for w in r["rc"].values():
                    deps[id(w)] = w
                for w in r["rd"]:
                    deps[id(w)] = w
        for w in self.pend[eng]:
            deps[id(w)] = w
        self.pend[eng] = []
        deps.pop(id(op), None)
        op.deps = list(deps.values())
        for d in op.deps:
            if not d.isdma:
                d.sig = True
        for k in reads:
            r = self.res.setdefault(k, {"w": [], "rc": {}, "rd": []})
            if isdma:
                r["rd"].append(op)
            else:
                r["rc"][eng] = op
        for k in writes:
            self.res[k] = {"w": [op], "rc": {}, "rd": []}
        for k in pwrites:
            r = self.res.get(k)
            if r is None or r["rc"] or r["rd"]:
                self.res[k] = {"w": [op], "rc": {}, "rd": []}
            else:
                r["w"].append(op)
        if isdma:
            op.dma_i = self.ndma[eng]
            self.ndma[eng] += 1
            self.alldma.append(op)
        self.ops[eng].append(op)
        return op

    def op(self, eng, fn, reads=(), writes=(), pwrites=()):
        return self._add(eng, fn, reads, writes, pwrites, False)

    def dma(self, eng, out, in_, reads=(), writes=(), pwrites=()):
        return self._add(eng, lambda e: e.dma_start(out=out, in_=in_), reads, writes, pwrites, True)

    def emit(self):
        nc = self.nc
        for e in self.ENGS:
            c = 0
            for op in self.ops[e]:
                if (not op.isdma) and op.sig:
                    c += 1
                    op.sigval = c
        with ExitStack() as st:
            psem = {e: st.enter_context(nc.semaphore("ps_" + e)) for e in self.ENGS}
            rings = {}
            for e in self.ENGS:
                if self.ndma[e]:
                    rings[e] = [st.enter_context(nc.semaphore("dr_%s_%d" % (e, i))) for i in range(self.ring)]
            block = st.enter_context(nc.Block())
            R = self.ring

            def tok(d):
                if d.isdma:
                    return rings[d.eng][d.dma_i % R], 16 * (d.dma_i // R + 1)
                return psem[d.eng], d.sigval

            def run(ename, eng):
                waited = {}

                def wait(sem, val):
                    key = id(sem)
                    if waited.get(key, 0) >= val:
                        return
                    eng.wait_ge(sem, val)
                    waited[key] = val

                for op in self.ops[ename]:
                    for d in op.deps:
                        if (not d.isdma) and d.eng == ename and ename == "pe":
                            continue
                        s, v = tok(d)
                        wait(s, v)
                    if op.isdma:
                        if op.dma_i >= R:
                            wait(rings[ename][op.dma_i % R], 16 * (op.dma_i // R))
                        ins = op.fn(eng)
                        ins.then_inc(rings[ename][op.dma_i % R], 16)
                    else:
                        ins = op.fn(eng)
                        if op.sig:
                            ins.then_inc(psem[ename], 1)
                n = self.ndma[ename]
                if n:
                    for j in range(R):
                        cnt = (n - j + R - 1) // R if n > j else 0
                        if cnt:
                            wait(rings[ename][j], 16 * cnt)

            @block.sync
            def _(e):
                run("sp", e)

            @block.scalar
            def _(e):
                run("act", e)

            @block.vector
            def _(e):
                run("dve", e)

            @block.gpsimd
            def _(e):
                run("pool", e)

            @block.tensor
            def _(e):
                run("pe", e)


class Rot:
    def __init__(self, name, bufs):
        self.name, self.bufs, self.i = name, bufs, -1

    def next(self):
        self.i += 1
        j = self.i % len(self.bufs)
        return self.bufs[j], (self.name, j)


CP_EVNORM = 0
CP_DWW = 8
CP_DWB = CP_DWW + 248
CP_LNW = CP_DWB + 8
CP_LNB = CP_LNW + 8
CP_ODNORM = CP_LNB + 8
CP_CW = CP_ODNORM + 8
CP_CB = CP_CW + 96
CP_N = CP_CB + 24
RP_FNORM = 0
RP_GNORM = 1024
RP_DSK = 3072
RP_DTB = 3104
RP_ALOG = 3136
RP_N = 3168


ARENA_WORDS = 27616


def build(nseq=2, stop_after=None):
    nc = bass.Bass("TRN2", target_bir_lowering=False)
    S = Sched(nc)

    def din(name, shape, dt=F32):
        return nc.dram_tensor(name, list(shape), dt, kind="ExternalInput").ap()

    def dscr(name, shape, dt):
        return nc.dram_tensor(name, list(shape), dt, kind="Internal").ap()

    x_d = din("x", [nseq, T, D])
    ev_w_in = din("ev_w_in", [D, IN_EVEN])
    ev_w_out = din("ev_w_out", [2048, D])
    od_w_in = din("od_w_in", [D, IN_ODD])
    od_w_out = din("od_w_out", [2048, D])
    colpack = din("colpack", [128, CP_N])
    rowpack = din("rowpack", [128, RP_N])
    cmats = din("cmats", [128, 7, 128])
    mask0_d = din("mask0", [128, 512])
    out_d = nc.dram_tensor("out", [nseq, T, D], F32, kind="ExternalOutput").ap()

    wb_ev_in = dscr("wb_ev_in", [D, IN_EVEN], BF16)
    wb_ev_out = dscr("wb_ev_out", [2048, D], BF16)
    wb_od_in = dscr("wb_od_in", [D, IN_ODD], BF16)
    wb_od_out = dscr("wb_od_out", [2048, D], BF16)
    conv_s = dscr("conv_s", [8, 128, T], F32)
    y_s = dscr("y_s", [16, 128, T], BF16)
    h1_s = dscr("h1_s", [T, D], F32)

    A = nc.alloc_sbuf_tensor
    PS = nc.alloc_psum_tensor

    cp = A("cp", [128, CP_N], F32)
    rp = A("rp", [128, RP_N], F32)
    cm = A("cm", [128, 7, 128], F32)
    cmb = A("cmb", [128, 7, 128], BF16)
    mask0 = A("mask0s", [128, 512], F32)
    hT = A("hT", [128, 8, T], BF16)
    wch = [A("wch%d" % i, [128, 8, 128], BF16) for i in range(6)]
    stat = [A("st%d" % i, [128, 4], F32) for i in range(2)]
    arena = A("arena", [128, ARENA_WORDS], F32)
    psb = [PS("psb%d" % i, [128, 512], F32) for i in range(8)]

    class Carver:
        def __init__(self):
            self.off = 0

        def f32(self, n):
            v = arena[:, self.off:self.off + n]
            self.off += n
            assert self.off <= ARENA_WORDS, self.off
            return v

        def bf16(self, n):
            w = (n + 1) // 2
            v = arena[:, self.off:self.off + w].bitcast(BF16)
            self.off += w
            assert self.off <= ARENA_WORDS, self.off
            return v

    S.dma("sp", cp[:, :], colpack, writes=["cp"])
    S.dma("sp", rp[:, :], rowpack, writes=["rp"])
    S.dma("sp", cm[:, :, :], cmats, writes=["cm"])
    S.dma("sp", mask0[:, :], mask0_d, writes=["mask0"])
    S.op("dve", lambda e: e.tensor_copy(out=cmb[:, :, :], in_=cm[:, :, :]), reads=["cm"], writes=["cmb"])
    ident = cm[:, 0, :]
    ones_f = cm[:, 1, :]
    negU_b = cmb[:, 2, :]
    zeros_b = cmb[:, 3, :]
    negones_b = cmb[:, 4, :]

    WK = {}
    for nm, src, dst, rows in (("ev_in", ev_w_in, wb_ev_in, D), ("ev_out", ev_w_out, wb_ev_out, 2048),
                               ("od_in", od_w_in, wb_od_in, D), ("od_out", od_w_out, wb_od_out, 2048)):
        step = rows // 8
        for i in range(8):
            S.dma("pool", dst[i * step:(i + 1) * step, :], src[i * step:(i + 1) * step, :], pwrites=[("W", nm)])
        WK[nm] = ("W", nm)

    strot = Rot("st", stat)
    wrot = Rot("wch", wch)

    def load_wchunk(wb, wkey, col0):
        buf, key = wrot.next()
        S.dma("sp", buf[:, :, :], wb.rearrange("(k p) n -> p k n", p=128)[:, :, col0:col0 + 128],
              reads=[wkey], writes=[key])
        return buf, key

    def rstd_ops(st_, sk, src, srck, sqb):
        S.op("act", lambda e: e.activation(out=sqb, in_=src, func=AF.Square), reads=[srck], writes=["sqb"])
        S.op("dve", lambda e: e.tensor_reduce(out=st_[:, 0:1], in_=sqb, axis=AX.X, op=ALU.add),
             reads=["sqb"], writes=[(sk, 0)])
        S.op("dve", lambda e: e.tensor_scalar(out=st_[:, 1:2], in0=st_[:, 0:1], scalar1=1.0 / D, scalar2=EPS,
                                              op0=ALU.mult, op1=ALU.add), reads=[(sk, 0)], writes=[(sk, 1)])
        S.op("act", lambda e: e.activation(out=st_[:, 2:3], in_=st_[:, 1:2], func=AF.Sqrt),
             reads=[(sk, 1)], writes=[(sk, 2)])
        S.op("dve", lambda e: e.reciprocal(out=st_[:, 3:4], in_=st_[:, 2:3]), reads=[(sk, 2)], writes=[(sk, 3)])

    def rms_to_hT(src, srckey, wcol0):
        S.barrier()
        cv = Carver()
        xbufs = [cv.f32(D) for _ in range(3)]
        sqbs = [cv.f32(D) for _ in range(2)]
        ssq = cv.f32(32)
        rs = cv.f32(32)
        xrot = Rot("xb", xbufs)
        sqrot = Rot("sqb2", sqbs)
        for t in range(32):
            xb, xk = xrot.next()
            sq_, sqk = sqrot.next()
            S.dma("sp", xb, src[t * 128:(t + 1) * 128, :], reads=[srckey], writes=[xk])
            S.op("act", lambda e, xb=xb, sq_=sq_: e.activation(out=sq_, in_=xb, func=AF.Square), reads=[xk], writes=[sqk])
            S.op("dve", lambda e, sq_=sq_, t=t: e.tensor_reduce(out=ssq[:, t:t + 1], in_=sq_, axis=AX.X, op=ALU.add),
                 reads=[sqk], pwrites=["ssq"])
        S.op("dve", lambda e: e.tensor_scalar(out=rs, in0=ssq, scalar1=1.0 / D, scalar2=EPS, op0=ALU.mult, op1=ALU.add),
             reads=["ssq"], writes=["rs"])
        S.op("act", lambda e: e.activation(out=rs, in_=rs, func=AF.Sqrt), reads=["rs"], writes=["rs"])
        S.op("dve", lambda e: e.reciprocal(out=rs, in_=rs), reads=["rs"], writes=["rs"])
        prot = Rot("psT", [(psb[0], psb[1]), (psb[2], psb[3]), (psb[4], psb[5])])

        def p2_load(t):
            xb, xk = xrot.next()
            S.dma("sp", xb, src[t * 128:(t + 1) * 128, :], reads=[srckey], writes=[xk])
            if t % 2:
                S.op("act", lambda e, xb=xb, t=t: e.activation(out=xb, in_=xb, func=AF.Copy, scale=rs[:, t:t + 1]),
                     reads=["rs"], writes=[xk])
            else:
                S.op("dve", lambda e, xb=xb, t=t: e.tensor_scalar(
                    out=xb, in0=xb, scalar1=rs[:, t:t + 1], scalar2=None, op0=ALU.mult), reads=["rs"], writes=[xk])
            return xb, xk

        def p2_rest(t, xb, xk):
            (p0, p1), pk = prot.next()
            for k in range(8):
                pt = p0 if k < 4 else p1
                kk = k % 4
                S.op("pe", lambda e, pt=pt, kk=kk, k=k, xb=xb: e.transpose(
                    out=pt[:, kk * 128:(kk + 1) * 128], in_=xb[:, k * 128:(k + 1) * 128], identity=ident),
                    reads=[xk, "cm"], pwrites=[(pk, k // 4)])
            for half in range(2):
                pt = p0 if half == 0 else p1
                eng = "dve" if half == 0 else "act"
                if eng == "dve":
                    S.op("dve", lambda e, pt=pt, half=half, t=t: e.tensor_tensor(
                        out=hT[:, half * 4:(half + 1) * 4, t * 128:(t + 1) * 128],
                        in0=pt[:, :].rearrange("p (k t) -> p k t", k=4),
                        in1=cp[:, wcol0 + half * 4:wcol0 + (half + 1) * 4].unsqueeze(2).broadcast_to([128, 4, 128]),
                        op=ALU.mult), reads=[(pk, half), "cp"], pwrites=["hT"])
                else:
                    for kk in range(4):
                        k = half * 4 + kk
                        S.op("act", lambda e, pt=pt, kk=kk, k=k, t=t: e.activation(
                            out=hT[:, k, t * 128:(t + 1) * 128], in_=pt[:, kk * 128:(kk + 1) * 128],
                            func=AF.Copy, scale=cp[:, wcol0 + k:wcol0 + k + 1]),
                            reads=[(pk, half), "cp"], pwrites=["hT"])

        nxt = p2_load(0)
        for t in range(32):
            cur = nxt
            if t + 1 < 32:
                nxt = p2_load(t + 1)
            p2_rest(t, *cur)

    def proj_fm(ps, pkey, wbuf, wkey, sb):
        for k in range(8):
            S.op("pe", lambda e, k=k: e.matmul(ps[:, :], lhsT=wbuf[:, k, :], rhs=hT[:, k, sb * 512:(sb + 1) * 512],
                                               start=(k == 0), stop=(k == 7)),
                 reads=[wkey, "hT"], writes=[pkey] if k == 0 else (), pwrites=() if k == 0 else [pkey])

    PE_TAPS = list(range(14, 31))
    DVE_TAPS = list(range(0, 14))

    def l0_conv(seq):
        S.barrier()
        cv = Carver()
        upad = [cv.f32(4128) for _ in range(2)]
        acc = [cv.f32(T) for _ in range(2)]
        sig = [cv.f32(512) for _ in range(2)]
        ubf = [cv.bf16(4128) for _ in range(2)]
        dgb = [cv.bf16(len(PE_TAPS) * 128) for _ in range(2)]
        for i in range(2):
            S.op("pool", lambda e, i=i: e.memset(upad[i][:, 0:30], 0.0), writes=[("upadz", i)])
            S.op("pool", lambda e, i=i: e.memset(ubf[i][:, 0:30], 0.0), writes=[("ubfz", i)])
        sigrot = Rot("sig", sig)
        prot = Rot("psB1", [(psb[0], psb[1]), (psb[2], psb[3])])
        crot = Rot("psC", [psb[4], psb[5], psb[6], psb[7]])
        def setup(c):
            X = {"c": c}
            X["wa"], X["wak"] = load_wchunk(wb_ev_in, WK["ev_in"], c * 128)
            X["wg"], X["wgk"] = load_wchunk(wb_ev_in, WK["ev_in"], 1024 + c * 128)
            X["up"], X["upk"] = upad[c % 2], ("upad", c % 2)
            X["ub"], X["ubk"] = ubf[c % 2], ("ubf", c % 2)
            X["ac"], X["ack"] = acc[c % 2], ("acc", c % 2)
            dg = dgb[c % 2].rearrange("p (i m) -> p i m", i=len(PE_TAPS))
            dgk = ("dg", c % 2)
            X["dg"], X["dgk"] = dg, dgk
            for i, k in enumerate(PE_TAPS):
                col = CP_DWW + c * 31 + k
                S.op("act", lambda e, dg=dg, i=i, col=col: e.activation(
                    out=dg[:, i, :], in_=cmb[:, 0, :], func=AF.Copy, scale=cp[:, col:col + 1]),
                    reads=["cmb", "cp"], writes=[dgk] if i == 0 else (), pwrites=() if i == 0 else [dgk])
            return X

        def P_block(X, sb):
            c, up, ub, upk, ubk = X["c"], X["up"], X["ub"], X["upk"], X["ubk"]
            (pa, pb), pk = prot.next()
            proj_fm(pa, (pk, "a"), X["wa"], X["wak"], sb)
            proj_fm(pb, (pk, "b"), X["wg"], X["wgk"], sb)
            sg, sgk = sigrot.next()
            usl = slice(30 + sb * 512, 30 + (sb + 1) * 512)
            S.op("act", lambda e, sg=sg, pb=pb: e.activation(out=sg, in_=pb[:, :], func=AF.Sigmoid),
                 reads=[(pk, "b")], writes=[sgk])
            S.op("dve", lambda e, sg=sg, pa=pa, up=up, usl=usl: e.tensor_tensor(
                out=up[:, usl], in0=pa[:, :], in1=sg, op=ALU.mult),
                reads=[(pk, "a"), sgk, ("upadz", c % 2)], pwrites=[upk])
            S.op("act", lambda e, up=up, ub=ub, usl=usl: e.activation(out=ub[:, usl], in_=up[:, usl], func=AF.Copy),
                 reads=[upk, ("ubfz", c % 2)], pwrites=[ubk])

        def tap(X, n_, k):
            c, up, ac, upk, ack = X["c"], X["up"], X["ac"], X["upk"], X["ack"]
            col = CP_DWW + c * 31 + k
            if n_ == 0:
                S.op("dve", lambda e, up=up, ac=ac, col=col, c=c, k=k: e.tensor_scalar(
                    out=ac, in0=up[:, k:k + T], scalar1=cp[:, col:col + 1],
                    scalar2=cp[:, CP_DWB + c:CP_DWB + c + 1], op0=ALU.mult, op1=ALU.add),
                    reads=[upk, "cp"], writes=[ack])
            else:
                S.op("dve", lambda e, up=up, ac=ac, col=col, k=k: e.scalar_tensor_tensor(
                    out=ac, in0=up[:, k:k + T], scalar=cp[:, col:col + 1], in1=ac,
                    op0=ALU.mult, op1=ALU.add),
                    reads=[upk, "cp"], writes=[ack])

        def C_block(X, sb):
            dg, dgk, ub, ubk, ac, ack = X["dg"], X["dgk"], X["ub"], X["ubk"], X["ac"], X["ack"]
            pc, pck = crot.next()
            for i, k in enumerate(PE_TAPS):
                S.op("pe", lambda e, pc=pc, dg=dg, i=i, k=k, sb=sb, ub=ub: e.matmul(
                    pc[:, :], lhsT=dg[:, i, :], rhs=ub[:, k + sb * 512:k + (sb + 1) * 512],
                    start=(i == 0), stop=(i == len(PE_TAPS) - 1)),
                    reads=[dgk, ubk], writes=[pck] if i == 0 else (), pwrites=() if i == 0 else [pck])
            S.op("dve", lambda e, pc=pc, ac=ac, sb=sb: e.tensor_tensor(
                out=ac[:, sb * 512:(sb + 1) * 512], in0=pc[:, :], in1=ac[:, sb * 512:(sb + 1) * 512], op=ALU.add),
                reads=[pck], writes=[ack])

        Xn = setup(0)
        for sb in range(8):
            P_block(Xn, sb)
        for c in range(8):
            X = Xn
            if c + 1 < 8:
                Xn = setup(c + 1)
            for n_, k in enumerate(DVE_TAPS):
                tap(X, n_, k)
                if n_ < 8:
                    if c + 1 < 8:
                        P_block(Xn, n_)
                    C_block(X, n_)
            S.dma("pool", conv_s[c], X["ac"], reads=[X["ack"]], pwrites=[("conv_s", c)])

    def l0_ln(seq):
        S.barrier()
        cv = Carver()
        mean = cv.f32(T)
        rstd = cv.f32(T)
        cvts = [cv.bf16(4096) for _ in range(2)]
        sqs_ = [cv.bf16(4096) for _ in range(2)]
        tmp512 = cv.f32(512)
        cvk = [("conv_s", c) for c in range(8)]
        cvrot = Rot("cvb", cvts)
        sqrot = Rot("sqb3", sqs_)
        psrot = Rot("psLN", [(psb[4], psb[5]), (psb[6], psb[7])])
        ones_b = cmb[:, 1, :]
        for sb in range(8):
            cvt, cvtk = cvrot.next()
            sq, sqk = sqrot.next()
            S.dma("pool", cvt.rearrange("p (c t) -> p c t", c=8),
                  conv_s.rearrange("c p t -> p c t")[:, :, sb * 512:(sb + 1) * 512], reads=cvk, writes=[cvtk])
            S.op("act", lambda e, sq=sq, cvt=cvt: e.activation(out=sq, in_=cvt, func=AF.Square), reads=[cvtk], writes=[sqk])
            (ps1, ps2), pk = psrot.next()
            for c in range(8):
                S.op("pe", lambda e, c=c, ps1=ps1, cvt=cvt: e.matmul(ps1[:, :], lhsT=ones_b, rhs=cvt[:, c * 512:(c + 1) * 512],
                                                                    start=(c == 0), stop=(c == 7)),
                     reads=[cvtk, "cmb"], writes=[(pk, 1)] if c == 0 else (), pwrites=() if c == 0 else [(pk, 1)])
            for c in range(8):
                S.op("pe", lambda e, c=c, ps2=ps2, sq=sq: e.matmul(ps2[:, :], lhsT=ones_b, rhs=sq[:, c * 512:(c + 1) * 512],
                                                                   start=(c == 0), stop=(c == 7)),
                     reads=[sqk, "cmb"], writes=[(pk, 2)] if c == 0 else (), pwrites=() if c == 0 else [(pk, 2)])
            msl = mean[:, sb * 512:(sb + 1) * 512]
            rsl = rstd[:, sb * 512:(sb + 1) * 512]
            S.op("act", lambda e, msl=msl, ps1=ps1: e.activation(out=msl, in_=ps1[:, :], func=AF.Copy, scale=1.0 / 1024),
                 reads=[(pk, 1)], pwrites=["mean"])
            S.op("dve", lambda e, msl=msl: e.tensor_tensor(out=tmp512, in0=msl, in1=msl, op=ALU.mult),
                 reads=["mean"], writes=["tmp512"])
            S.op("dve", lambda e, rsl=rsl, ps2=ps2: e.scalar_tensor_tensor(out=rsl, in0=ps2[:, :], scalar=1.0 / 1024,
                                                                           in1=tmp512, op0=ALU.mult, op1=ALU.subtract),
                 reads=[(pk, 2), "tmp512"], pwrites=["rstd"])
            S.op("dve", lambda e, rsl=rsl: e.tensor_scalar(out=rsl, in0=rsl, scalar1=EPS, scalar2=None, op0=ALU.add),
                 reads=["rstd"], pwrites=["rstd"])
            S.op("act", lambda e, rsl=rsl: e.activation(out=rsl, in_=rsl, func=AF.Sqrt),
                 reads=["rstd"], pwrites=["rstd"])
            S.op("dve", lambda e, rsl=rsl: e.reciprocal(out=rsl, in_=rsl), reads=["rstd"], pwrites=["rstd"])
        S.barrier()
        cv = Carver()
        cv.f32(2 * T)
        cch = [cv.f32(T) for _ in range(2)]
        ych = [cv.bf16(T) for _ in range(2)]
        sgt = [cv.f32(512) for _ in range(2)]
        sgrot = Rot("sgt", sgt)
        prot = Rot("psB2", [psb[0], psb[1], psb[2], psb[3]])
        for c in range(8):
            wg, wgk = load_wchunk(wb_ev_in, WK["ev_in"], 2048 + c * 128)
            cc = cch[c % 2]
            cck = ("cch", c % 2)
            yc = ych[c % 2]
            yck = ("ych", c % 2)
            S.dma("sp", cc, conv_s[c], reads=[("conv_s", c)], writes=[cck])
            for sb in range(8):
                ps, pk = prot.next()
                proj_fm(ps, pk, wg, wgk, sb)
                sg, sgk = sgrot.next()
                sl = slice(sb * 512, (sb + 1) * 512)
                S.op("act", lambda e, sg=sg, ps=ps: e.activation(out=sg, in_=ps[:, :], func=AF.Silu),
                     reads=[pk], writes=[sgk])
                S.op("dve", lambda e, cc=cc, sl=sl: e.tensor_tensor(out=cc[:, sl], in0=cc[:, sl], in1=mean[:, sl],
                                                                    op=ALU.subtract),
                     reads=[cck], pwrites=[cck])
                S.op("pool", lambda e, cc=cc, sl=sl: e.tensor_tensor(out=cc[:, sl], in0=cc[:, sl], in1=rstd[:, sl],
                                                                     op=ALU.mult),
                     reads=[cck], pwrites=[cck])
                S.op("act", lambda e, cc=cc, sl=sl, c=c: e.activation(
                    out=cc[:, sl], in_=cc[:, sl], func=AF.Silu, scale=cp[:, CP_LNW + c:CP_LNW + c + 1],
                    bias=cp[:, CP_LNB + c:CP_LNB + c + 1]), reads=[cck, "cp"], pwrites=[cck])
                S.op("dve", lambda e, cc=cc, sl=sl, sg=sg, yc=yc: e.tensor_tensor(
                    out=yc[:, sl], in0=cc[:, sl], in1=sg, op=ALU.mult),
                    reads=[cck, sgk], pwrites=[yck])
            S.dma("pool", y_s[c], yc, reads=[yck], pwrites=[("y_s", c)])

    def l0_attn(seq):
        S.barrier()
        cv = Carver()
        gaT = cv.bf16(T)
        qTb = cv.bf16(T)
        kTb = cv.bf16(T)
        vtok = cv.bf16(T)
        yhb = [cv.bf16(T) for _ in range(2)]
        e_b = [cv.f32(512) for _ in range(3)]
        spf = [cv.f32(512) for _ in range(2)]
        spb = [cv.bf16(512) for _ in range(4)]
        wtb = [cv.bf16(512) for _ in range(3)]
        wtf = [cv.f32(512) for _ in range(2)]
        Sf = cv.f32(512)
        Sb = [cv.bf16(512) for _ in range(3)]
        scale = 128.0 ** -0.5
        erot, sfrot, sbrot, wbrot, wfrot, Sbrot = (Rot("eb", e_b), Rot("spf", spf), Rot("spb", spb),
                                                   Rot("wtb", wtb), Rot("wtf", wtf), Rot("Sb", Sb))
        zrot = Rot("zp", [psb[0], psb[1]])
        arot = Rot("ap", [psb[2], psb[3]])
        orot = Rot("oT", [psb[4], psb[5]])
        prot = Rot("pp", [psb[6], psb[7]])
        for h in range(8):
            wq, wqk = load_wchunk(wb_ev_in, WK["ev_in"], 3072 + h * 128)
            wk_, wkk = load_wchunk(wb_ev_in, WK["ev_in"], 4096 + h * 128)
            wv, wvk = load_wchunk(wb_ev_in, WK["ev_in"], 5120 + h * 128)
            wga, wgak = load_wchunk(wb_ev_in, WK["ev_in"], 6144 + h * 128)
            for sb in range(8):
                sl = slice(sb * 512, (sb + 1) * 512)
                ps, pk = prot.next()
                proj_fm(ps, pk, wq, wqk, sb)
                S.op("act", lambda e, ps=ps, sl=sl: e.activation(out=qTb[:, sl], in_=ps[:, :], func=AF.Copy, scale=scale),
                     reads=[pk], pwrites=["qT"])
                ps, pk = prot.next()
                proj_fm(ps, pk, wk_, wkk, sb)
                S.op("dve", lambda e, ps=ps, sl=sl: e.tensor_copy(out=kTb[:, sl], in_=ps[:, :]),
                     reads=[pk], pwrites=["kT"])
                ps, pk = prot.next()
                proj_fm(ps, pk, wga, wgak, sb)
                S.op("act", lambda e, ps=ps, sl=sl: e.activation(out=gaT[:, sl], in_=ps[:, :], func=AF.Silu),
                     reads=[pk], pwrites=["gaT"])
                ps, pk = prot.next()
                for j in range(4):
                    tt = sb * 4 + j
                    for k in range(8):
                        S.op("pe", lambda e, ps=ps, j=j, k=k, tt=tt, wv=wv: e.matmul(
                            ps[:, j * 128:(j + 1) * 128], lhsT=hT[:, k, tt * 128:(tt + 1) * 128], rhs=wv[:, k, :],
                            start=(k == 0), stop=(k == 7)),
                            reads=[wvk, "hT"], writes=[pk] if (k == 0 and j == 0) else (),
                            pwrites=() if (k == 0 and j == 0) else [pk])
                S.op("dve", lambda e, ps=ps, sl=sl: e.tensor_copy(out=vtok[:, sl], in_=ps[:, :]),
                     reads=[pk], pwrites=["vtok"])
            yh = yhb[h % 2]
            yhk = ("yh", h % 2)
            def stageA(Q, b, first, last):
                qs, oT, ok = Q
                if first:
                    S.op("pe", lambda e, oT=oT: e.matmul(oT[:, :], lhsT=zeros_b, rhs=qTb[:, 0:512], start=True, stop=False),
                         reads=["cmb", "qT"], writes=[ok])
                    S.op("pool", lambda e: e.memset(Sf, 0.0), writes=["Sf"])
                c0 = max(0, 128 * (b - 4 * qs))
                W = 512 - c0
                diag = b >= 4 * qs
                q0 = qs * 512 + c0
                zp, zk = zrot.next()
                S.op("pe", lambda e, zp=zp, b=b, q0=q0, W=W: e.matmul(
                    zp[:, 0:W], lhsT=kTb[:, b * 128:(b + 1) * 128], rhs=qTb[:, q0:q0 + W], start=True, stop=True),
                    reads=["kT", "qT"], writes=[zk])
                eb, ek = erot.next()
                S.op("act", lambda e, eb=eb, zp=zp, W=W: e.activation(out=eb[:, 0:W], in_=zp[:, 0:W], func=AF.Exp),
                     reads=[zk], writes=[ek])
                sb_, sbk = sbrot.next()
                if diag:
                    sf_, sfk = sfrot.next()
                    S.op("act", lambda e, eb=eb, sf_=sf_, W=W: e.activation(
                        out=sf_[:, 0:W], in_=eb[:, 0:W], func=AF.Ln, bias=1.0), reads=[ek], writes=[sfk])
                    S.op("dve", lambda e, sf_=sf_, sb_=sb_, W=W: e.tensor_tensor(
                        out=sb_[:, 0:W], in0=sf_[:, 0:W], in1=mask0[:, 0:W], op=ALU.mult),
                        reads=[sfk, "mask0"], writes=[sbk])
                else:
                    S.op("act", lambda e, eb=eb, sb_=sb_, W=W: e.activation(
                        out=sb_[:, 0:W], in_=eb[:, 0:W], func=AF.Ln, bias=1.0), reads=[ek], writes=[sbk])
                Snext = None
                if not last:
                    S.op("dve", lambda e, sb_=sb_, c0=c0, W=W: e.tensor_tensor(
                        out=Sf[:, c0:512], in0=Sf[:, c0:512], in1=sb_[:, 0:W], op=ALU.add),
                        reads=[sbk], writes=["Sf"])
                    sbn, sbnk = Sbrot.next()
                    S.op("dve", lambda e, sbn=sbn: e.tensor_copy(out=sbn, in_=Sf), reads=["Sf"], writes=[sbnk])
                    Snext = (sbn, sbnk)
                return (b, c0, W, diag, q0, sb_, sbk, Snext)

            def stageB1(Q, st, Scur, first):
                b, c0, W, diag, q0, sb_, sbk, _ = st
                ap_, ak = arot.next()
                S.op("pe", lambda e, ap_=ap_, b=b, q0=q0, W=W: e.matmul(
                    ap_[:, 0:W], lhsT=kTb[:, b * 128:(b + 1) * 128], rhs=qTb[:, q0:q0 + W], start=True, stop=False),
                    reads=["kT", "qT"], writes=[ak])
                S.op("pe", lambda e, ap_=ap_, sb_=sb_, W=W, first=first: e.matmul(
                    ap_[:, 0:W], lhsT=negU_b, rhs=sb_[:, 0:W], start=False, stop=first),
                    reads=[sbk, "cmb"], pwrites=[ak])
                if not first:
                    S.op("pe", lambda e, ap_=ap_, Scur=Scur, c0=c0, W=W: e.matmul(
                        ap_[:, 0:W], lhsT=negones_b, rhs=Scur[0][:, c0:512], start=False, stop=True),
                        reads=[Scur[1], "cmb"], pwrites=[ak])
                wb_, wbk = wbrot.next()
                if diag:
                    wf_, wfk = wfrot.next()
                    S.op("act", lambda e, wf_=wf_, ap_=ap_, W=W: e.activation(out=wf_[:, 0:W], in_=ap_[:, 0:W], func=AF.Exp),
                         reads=[ak], writes=[wfk])
                    S.op("dve", lambda e, wf_=wf_, wb_=wb_, W=W: e.tensor_tensor(
                        out=wb_[:, 0:W], in0=wf_[:, 0:W], in1=mask0[:, 0:W], op=ALU.mult),
                        reads=[wfk, "mask0"], writes=[wbk])
                else:
                    S.op("act", lambda e, wb_=wb_, ap_=ap_, W=W: e.activation(out=wb_[:, 0:W], in_=ap_[:, 0:W], func=AF.Exp),
                         reads=[ak], writes=[wbk])
                return (Q, b, c0, W, wb_, wbk)

            def stageB2(w):
                (qs, oT, ok), b, c0, W, wb_, wbk = w
                S.op("pe", lambda e, oT=oT, b=b, wb_=wb_, c0=c0, W=W: e.matmul(
                    oT[:, c0:512], lhsT=vtok[:, b * 128:(b + 1) * 128], rhs=wb_[:, 0:W], start=False, stop=(b == 0)),
                    reads=["vtok", wbk], pwrites=[ok])
                if b == 0:
                    sl = slice(qs * 512, (qs + 1) * 512)
                    S.op("dve", lambda e, oT=oT, sl=sl, yh=yh: e.tensor_tensor(
                        out=yh[:, sl], in0=oT[:, :], in1=gaT[:, sl], op=ALU.mult),
                        reads=[ok, "gaT"], pwrites=[yhk])

            steps = []
            for qs in range(8):
                oT, ok = orot.next()
                Q = (qs, oT, ok)
                nb = 4 * qs + 4
                for n, b in enumerate(range(nb - 1, -1, -1)):
                    steps.append((Q, b, n == 0, n == nb - 1))
            nxt = stageA(*steps[0])
            Scur = None
            wprev = None
            for i, (Q, b, first, last) in enumerate(steps):
                cur = nxt
                if i + 1 < len(steps):
                    nxt = stageA(*steps[i + 1])
                wcur = stageB1(Q, cur, None if first else Scur, first)
                if wprev is not None:
                    stageB2(wprev)
                wprev = wcur
                Scur = cur[7]
            stageB2(wprev)
            S.dma("pool", y_s[8 + h], yh, reads=[yhk], pwrites=[("y_s", 8 + h)])

    def outproj(wb, wkey, res_src, res_key, dst, dst_key, final):
        S.barrier()
        cv = Carver()
        wout = cv.bf16(16 * D)
        ytl = [cv.bf16(16 * 128) for _ in range(2)]
        hob = [cv.f32(D) for _ in range(2)]
        xbufs = [cv.f32(D) for _ in range(2)]
        sqb = cv.f32(D)
        wout3 = wout.rearrange("p (c n) -> p c n", c=16)
        S.dma("sp", wout3, wb.rearrange("(c p) n -> p c n", p=128), reads=[wkey], writes=["wout"])
        yrot = Rot("ytl", ytl)
        hrot = Rot("hob", hob)
        xrot = Rot("xb", xbufs)
        prot = Rot("psO", [(psb[0], psb[1]), (psb[2], psb[3])])
        ykeys = [("y_s", c) for c in range(16)]
        for tt in range(32):
            yt, ytk = yrot.next()
            yt3 = yt.rearrange("p (c t) -> p c t", c=16)
            S.dma("sp", yt3, y_s.rearrange("c p t -> p c t")[:, :, tt * 128:(tt + 1) * 128], reads=ykeys, writes=[ytk])
            xb, xk = xrot.next()
            S.dma("sp", xb, res_src[tt * 128:(tt + 1) * 128, :], reads=[res_key], writes=[xk])
            (p0, p1), pk = prot.next()
            for n, pp in enumerate((p0, p1)):
                for c in range(16):
                    S.op("pe", lambda e, pp=pp, c=c, n=n, yt3=yt3: e.matmul(
                        pp[:, :], lhsT=yt3[:, c, :], rhs=wout3[:, c, n * 512:(n + 1) * 512], start=(c == 0), stop=(c == 15)),
                        reads=[ytk, "wout"], writes=[(pk, n)] if c == 0 else (), pwrites=() if c == 0 else [(pk, n)])
            ho, hk = hrot.next()
            for n, pp in enumerate((p0, p1)):
                S.op("dve", lambda e, pp=pp, n=n, ho=ho, xb=xb: e.tensor_tensor(
                    out=ho[:, n * 512:(n + 1) * 512], in0=pp[:, :], in1=xb[:, n * 512:(n + 1) * 512], op=ALU.add),
                    reads=[(pk, n), xk], pwrites=[hk])
            if final:
                st_, sk = strot.next()
                rstd_ops(st_, sk, ho, hk, sqb)
                S.op("dve", lambda e, ho=ho, st_=st_: e.scalar_tensor_tensor(
                    out=ho, in0=ho, scalar=st_[:, 3:4], in1=rp[:, RP_FNORM:RP_FNORM + D],
                    op0=ALU.mult, op1=ALU.mult), reads=[(sk, 3), "rp", hk], writes=[hk])
            S.dma("pool", dst[tt * 128:(tt + 1) * 128, :], ho, reads=[hk], pwrites=[dst_key])


    xt_s = dscr("xt_s", [T, 2048], F32)
    bt_s = dscr("bt_s", [T, 512], BF16)
    bc_s = dscr("bc_s", [8, 128, T], BF16)
    z_s = dscr("z_s", [T, 2048], F32)
    dtl_s = dscr("dtl_s", [T, 64], F32)
    arow = A("arow", [128, 32], F32)
    S.op("act", lambda e: e.activation(out=arow[:, :], in_=rp[:, RP_ALOG:RP_ALOG + 32], func=AF.Exp),
         reads=["rp"], writes=["arow"])
    ustrict = cm[:, 6, :]
    tri_f = cm[:, 5, :]

    def l1_inproj(seq):
        S.barrier()
        cv = Carver()
        xpad = [cv.f32(4100) for _ in range(2)]
        acc = [cv.f32(T) for _ in range(2)]
        xtb = cv.f32(T)
        xtb_bf = xtb[:, 0:2048].bitcast(BF16)
        for i in range(2):
            S.op("pool", lambda e, i=i: e.memset(xpad[i][:, 0:3], 0.0), writes=[("xpadz", i)])
        prot = Rot("psL", [psb[0], psb[1], psb[6], psb[7]])
        trot = Rot("psX", [psb[2], psb[3], psb[4], psb[5]])
        def l1_proj(cc):
                w, wk = load_wchunk(wb_od_in, WK["od_in"], 2048 + cc * 128)
                xp = xpad[cc % 2]
                xpk = ("xpad", cc % 2)
                ac = acc[cc % 2]
                ack = ("acc1", cc % 2)
                for sb in range(8):
                    ps, pk = prot.next()
                    proj_fm(ps, pk, w, wk, sb)
                    S.op("act", lambda e, ps=ps, xp=xp, sb=sb: e.activation(
                        out=xp[:, 3 + sb * 512:3 + (sb + 1) * 512], in_=ps[:, :], func=AF.Copy),
                        reads=[pk, ("xpadz", cc % 2)], pwrites=[xpk])
                return xp, xpk, ac, ack

        def l1_post(cc, xp, xpk, ac, ack):
                for k in range(4):
                    col = CP_CW + cc * 4 + k
                    if k == 0:
                        S.op("dve", lambda e, xp=xp, ac=ac, col=col, cc=cc: e.tensor_scalar(
                            out=ac, in0=xp[:, 0:T], scalar1=cp[:, col:col + 1],
                            scalar2=cp[:, CP_CB + cc:CP_CB + cc + 1], op0=ALU.mult, op1=ALU.add),
                            reads=[xpk, "cp"], writes=[ack])
                    else:
                        S.op("dve", lambda e, xp=xp, ac=ac, col=col, k=k: e.scalar_tensor_tensor(
                            out=ac, in0=xp[:, k:k + T], scalar=cp[:, col:col + 1], in1=ac,
                            op0=ALU.mult, op1=ALU.add), reads=[xpk, "cp"], writes=[ack])
                S.op("act", lambda e, ac=ac: e.activation(out=ac, in_=ac, func=AF.Silu), reads=[ack], writes=[ack])
                if cc >= 16:
                    S.dma("pool", bc_s[cc - 16], ac, reads=[ack], pwrites=[("bc_s", cc - 16)])
                if cc < 20:
                    isb = cc >= 16
                    stg = xtb_bf if isb else xtb
                    for q in range(8):
                        pt, ptk = trot.next()
                        for j in range(4):
                            tt = q * 4 + j
                            S.op("pe", lambda e, pt=pt, j=j, tt=tt, ac=ac: e.transpose(
                                out=pt[:, j * 128:(j + 1) * 128], in_=ac[:, tt * 128:(tt + 1) * 128], identity=ident),
                                reads=[ack, "cm"], pwrites=[ptk])
                        if q % 2 == 0:
                            S.op("dve", lambda e, pt=pt, q=q, stg=stg: e.tensor_copy(
                                out=stg[:, q * 512:(q + 1) * 512], in_=pt[:, :]), reads=[ptk], pwrites=["xtb"])
                        else:
                            S.op("act", lambda e, pt=pt, q=q, stg=stg: e.activation(
                                out=stg[:, q * 512:(q + 1) * 512], in_=pt[:, :], func=AF.Copy), reads=[ptk], pwrites=["xtb"])
                    if isb:
                        S.dma("pool", bt_s.rearrange("(tt p) c -> p tt c", p=128)[:, :, (cc - 16) * 128:(cc - 15) * 128],
                              stg.rearrange("p (tt c) -> p tt c", c=128), reads=["xtb"], pwrites=["bt_s"])
                    else:
                        S.dma("pool", xt_s.rearrange("(tt p) c -> p tt c", p=128)[:, :, cc * 128:(cc + 1) * 128],
                              stg.rearrange("p (tt c) -> p tt c", c=128), reads=["xtb"], pwrites=["xt_s"])

        nxt_ = l1_proj(0)
        for cc in range(24):
            cur_ = nxt_
            if cc + 1 < 24:
                nxt_ = l1_proj(cc + 1)
            l1_post(cc, *cur_)
        S.barrier()
        cv = Carver()
        wz = cv.bf16(8 * 2048)
        wz3 = wz.rearrange("p (k n) -> p k n", k=8)
        wdt = cv.bf16(8 * 32)
        wdt3 = wdt.rearrange("p (k n) -> p k n", k=8)
        ztl = [cv.f32(2048) for _ in range(2)]
        dtl = [cv.f32(64) for _ in range(2)]
        tmpd = [cv.f32(32) for _ in range(2)]
        S.dma("sp", wz3, wb_od_in.rearrange("(k p) n -> p k n", p=128)[:, :, 0:2048], reads=[WK["od_in"]], writes=["wz"])
        S.dma("sp", wdt3, wb_od_in.rearrange("(k p) n -> p k n", p=128)[:, :, 5120:5152], reads=[WK["od_in"]], writes=["wdt"])
        zrot = Rot("ztl", ztl)
        drot = Rot("dtl", dtl)
        tdrot = Rot("tmpd", tmpd)
        prot = Rot("psZ", [psb[0], psb[1], psb[2], psb[3], psb[4], psb[5]])
        dprot = Rot("psD", [psb[6], psb[7]])
        for tt in range(32):
            zt, ztk = zrot.next()
            tsl = slice(tt * 128, (tt + 1) * 128)
            for g in range(4):
                ps, pk = prot.next()
                for k in range(8):
                    S.op("pe", lambda e, ps=ps, k=k, g=g, tsl=tsl: e.matmul(
                        ps[:, :], lhsT=hT[:, k, tsl], rhs=wz3[:, k, g * 512:(g + 1) * 512], start=(k == 0), stop=(k == 7)),
                        reads=["wz", "hT"], writes=[pk] if k == 0 else (), pwrites=() if k == 0 else [pk])
                S.op("act", lambda e, ps=ps, zt=zt, g=g: e.activation(out=zt[:, g * 512:(g + 1) * 512], in_=ps[:, :], func=AF.Silu),
                     reads=[pk], pwrites=[ztk])
            S.dma("pool", z_s[tsl, :], zt, reads=[ztk], pwrites=["z_s"])
            ps, pk = dprot.next()
            for k in range(8):
                S.op("pe", lambda e, ps=ps, k=k, tsl=tsl: e.matmul(
                    ps[:, 0:32], lhsT=hT[:, k, tsl], rhs=wdt3[:, k, :], start=(k == 0), stop=(k == 7)),
                    reads=["wdt", "hT"], writes=[pk] if k == 0 else (), pwrites=() if k == 0 else [pk])
            dl, dlk = drot.next()
            td, tdk = tdrot.next()
            S.op("dve", lambda e, ps=ps, td=td: e.tensor_tensor(out=td, in0=ps[:, 0:32], in1=rp[:, RP_DTB:RP_DTB + 32], op=ALU.add),
                 reads=[pk, "rp"], writes=[tdk])
            S.op("act", lambda e, td=td: e.activation(out=td, in_=td, func=AF.Exp), reads=[tdk], writes=[tdk])
            S.op("act", lambda e, td=td, dl=dl: e.activation(out=dl[:, 0:32], in_=td, func=AF.Ln, bias=1.0),
                 reads=[tdk], pwrites=[dlk])
            S.op("dve", lambda e, dl=dl: e.scalar_tensor_tensor(out=dl[:, 32:64], in0=dl[:, 0:32], scalar=-1.0, in1=arow[:, :],
                                                                op0=ALU.mult, op1=ALU.mult),
                 reads=[dlk, "arow"], pwrites=[dlk])
            S.dma("pool", dtl_s[tsl, :], dl, reads=[dlk], pwrites=["dtl_s"])

    def l1_ssd(seq):
        S.barrier()
        cv = Carver()
        state = cv.f32(2048)
        state_bf = cv.bf16(2048)
        xtl = [cv.f32(2048) for _ in range(1)]
        ztl = [cv.f32(2048) for _ in range(1)]
        ytoks = [cv.f32(2048) for _ in range(2)]
        tmpys = [cv.f32(512) for _ in range(2)]
        sqs = cv.bf16(2048)
        ybf = cv.bf16(2048)
        xss = [cv.bf16(2048) for _ in range(2)]
        xst = cv.bf16(2048)
        dxs = cv.f32(2048)
        btl = [cv.bf16(512) for _ in range(2)]
        bctl = [cv.bf16(1024) for _ in range(2)]
        dtll = [cv.f32(64) for _ in range(2)]
        ecss = [cv.f32(32) for _ in range(2)]
        etots = [cv.f32(32) for _ in range(2)]
        cbms = [cv.bf16(512) for _ in range(2)]
        ULb = [cv.f32(1024) for _ in range(2)]
        Eb = [cv.bf16(1024) for _ in range(2)]
        MTb = [cv.bf16(1024) for _ in range(2)]
        gsts = [cv.f32(8) for _ in range(2)]
        yTc = [cv.bf16(2048) for _ in range(1)]
        S.op("pool", lambda e: e.memset(state, 0.0), writes=[("state", g_) for g_ in range(4)])
        S.op("pool", lambda e: e.memset(state_bf, 0.0), writes=[("state_bf", g_) for g_ in range(4)])
        xrot, zrot, brot, bcrot, dlrot = Rot("xtl", xtl), Rot("ztl1", ztl), Rot("btl", btl), Rot("bctl", bctl), Rot("dtll", dtll)
        ulrot, erot, mrot, ytrot = Rot("UL", ULb), Rot("E", Eb), Rot("MT", MTb), Rot("yTc", yTc)
        B_D = [psb[0], psb[1]]
        drot = Rot("psDD", [(psb[0], psb[1]), (psb[2], psb[3])])
        def tail_piece(P, piece):
            c_, ytok, ytk, gst, gk = P["c"], P["ytok"], P["ytk"], P["gst"], P["gk"]
            if piece == 0:
                zt, ztk = P["zt"], P["ztk"]
                S.op("pool", lambda e, ytok=ytok: e.tensor_tensor(out=ytok, in0=ytok, in1=dxs, op=ALU.add), reads=["dxs"], writes=[ytk])
                S.op("dve", lambda e, ytok=ytok, zt=zt: e.tensor_tensor(out=ytok, in0=ytok, in1=zt, op=ALU.mult), reads=[ztk], writes=[ytk])
                S.op("act", lambda e, ytok=ytok: e.activation(out=sqs, in_=ytok, func=AF.Square), reads=[ytk], writes=["sqs"])
            elif piece == 1:
                S.op("dve", lambda e, gst=gst: e.tensor_reduce(out=gst[:, 0:4], in_=sqs.rearrange("p (g d) -> p g d", g=4), axis=AX.X, op=ALU.add),
                     reads=["sqs"], writes=[gk])
                S.op("dve", lambda e, gst=gst: e.tensor_scalar(out=gst[:, 0:4], in0=gst[:, 0:4], scalar1=1.0 / 512, scalar2=EPS, op0=ALU.mult, op1=ALU.add),
                     reads=[gk], writes=[gk])
                S.op("act", lambda e, gst=gst: e.activation(out=gst[:, 0:4], in_=gst[:, 0:4], func=AF.Sqrt), reads=[gk], writes=[gk])
            elif piece == 2:
                S.op("dve", lambda e, gst=gst: e.reciprocal(out=gst[:, 4:8], in_=gst[:, 0:4]), reads=[gk], writes=[gk])
                for g in range(4):
                    gs = slice(g * 512, (g + 1) * 512)
                    S.op("dve", lambda e, g=g, gs=gs, gst=gst, ytok=ytok: e.scalar_tensor_tensor(
                        out=ybf[:, gs], in0=ytok[:, gs], scalar=gst[:, 4 + g:5 + g], in1=rp[:, RP_GNORM + g * 512:RP_GNORM + (g + 1) * 512],
                        op0=ALU.mult, op1=ALU.mult), reads=[gk, "rp", ytk], writes=["ybf"] if g == 0 else (), pwrites=() if g == 0 else ["ybf"])
            else:
                yt_, ytk_ = ytrot.next()
                pbk = ("pb", 7)
                for q in range(4):
                    pbv = psb[7][:, 0:256].bitcast(BF16)
                    for j in range(4):
                        cch_ = q * 4 + j
                        S.op("pe", lambda e, pbv=pbv, j=j, cch_=cch_: e.transpose(
                            out=pbv[:, j * 128:(j + 1) * 128], in_=ybf[:, cch_ * 128:(cch_ + 1) * 128], identity=cmb[:, 0, :]),
                            reads=["ybf", "cmb"], writes=[pbk] if j == 0 else (), pwrites=() if j == 0 else [pbk])
                    S.op("act" if q % 2 else "dve",
                         (lambda e, pbv=pbv, q=q, yt_=yt_: e.activation(out=yt_[:, q * 512:(q + 1) * 512], in_=pbv, func=AF.Copy)) if q % 2 else
                         (lambda e, pbv=pbv, q=q, yt_=yt_: e.tensor_copy(out=yt_[:, q * 512:(q + 1) * 512], in_=pbv)),
                         reads=[pbk], pwrites=[ytk_])
                S.dma("pool", y_s.rearrange("c p t -> p c t")[:, :, c_ * 128:(c_ + 1) * 128], yt_.rearrange("p (c t) -> p c t", c=16),
                      reads=[ytk_], pwrites=[("y_s", cc_) for cc_ in range(16)])

        def chunk_pre(c):
            X = {"c": c, "p": c % 2}
            tsl = slice(c * 128, (c + 1) * 128)
            p = c % 2
            xt, xtk = xrot.next()
            bt, btk = brot.next()
            bct, bctk = bcrot.next()
            dl, dlk = dlrot.next()
            bct3 = bct.rearrange("p (g t) -> p g t", g=8)
            xs, xsk = xss[p], ("xs", p)
            ecs, ecsk = ecss[p], ("ecs", p)
            etot, etk = etots[p], ("etot", p)
            cbm, cbk = cbms[p], ("cbm", p)
            X.update(xt=xt, xtk=xtk, bt=bt, btk=btk, bct3=bct3, bctk=bctk, dl=dl, dlk=dlk, xs=xs, xsk=xsk,
                     ecs=ecs, ecsk=ecsk, etot=etot, etk=etk, cbm=cbm, cbk=cbk, ytok=ytoks[p], ytk=("ytok", p))
            S.dma("sp", xt, xt_s[tsl, :], reads=["xt_s"], writes=[xtk])
            S.dma("sp", bt, bt_s[tsl, :], reads=["bt_s"], writes=[btk])
            S.dma("sp", bct3, bc_s.rearrange("g p t -> p g t")[:, :, tsl], reads=[("bc_s", g) for g in range(8)], writes=[bctk])
            S.dma("sp", dl, dtl_s[tsl, :], reads=["dtl_s"], writes=[dlk])
            S.op("dve", lambda e: e.tensor_tensor(
                out=xs.rearrange("p (h d) -> p h d", h=32), in0=xt.rearrange("p (h d) -> p h d", h=32),
                in1=dl[:, 0:32].unsqueeze(2).broadcast_to([128, 32, 64]), op=ALU.mult),
                reads=[xtk, dlk], writes=[xsk])
            S.op("pe", lambda e: e.matmul(psb[5][:, 0:32], lhsT=ones_f, rhs=dl[:, 32:64], start=True, stop=True),
                 reads=[dlk, "cm"], writes=[("pb", 5)])
            S.op("pe", lambda e: e.matmul(psb[5][:, 32:64], lhsT=tri_f, rhs=dl[:, 32:64], start=True, stop=True),
                 reads=[dlk, "cm"], pwrites=[("pb", 5)])
            S.op("act", lambda e: e.activation(out=etot, in_=psb[5][:, 0:32], func=AF.Exp), reads=[("pb", 5)], writes=[etk])
            S.op("act", lambda e: e.activation(out=ecs, in_=psb[5][:, 32:64], func=AF.Exp), reads=[("pb", 5)], writes=[ecsk])
            for g in range(4):
                S.op("pe", lambda e, g=g: e.matmul(psb[7][:, g * 128:(g + 1) * 128], lhsT=bct3[:, g, :], rhs=bct3[:, 4 + g, :],
                                                   start=True, stop=True),
                     reads=[bctk], writes=[("pb", 7)] if g == 0 else (), pwrites=() if g == 0 else [("pb", 7)])
            S.op("dve", lambda e: e.tensor_tensor(
                out=cbm.rearrange("p (g l) -> p g l", g=4), in0=psb[7][:, :].rearrange("p (g l) -> p g l", g=4),
                in1=tri_f.unsqueeze(1).broadcast_to([128, 4, 128]), op=ALU.mult),
                reads=[("pb", 7), "cm"], writes=[cbk])
            X["A0"] = stageA(X, 0)
            return X

        def stageA(X, g):
            dl, dlk = X["dl"], X["dlk"]
            ul, ulk = ulrot.next()
            ul3 = ul.rearrange("p (h s) -> p h s", h=8)
            S.op("pool", lambda e: e.tensor_tensor(
                out=ul3, in0=ustrict.unsqueeze(1).broadcast_to([128, 8, 128]),
                in1=dl[:, 32 + g * 8:40 + g * 8].unsqueeze(2).broadcast_to([128, 8, 128]), op=ALU.mult),
                reads=[dlk, "cm"], writes=[ulk])
            (d0, d1), dkk = drot.next()
            for hh in range(8):
                dp = d0 if hh < 4 else d1
                S.op("pe", lambda e, dp=dp, hh=hh: e.matmul(
                    dp[:, (hh % 4) * 128:(hh % 4 + 1) * 128], lhsT=ul3[:, hh, :], rhs=tri_f, start=True, stop=True),
                    reads=[ulk, "cm"], pwrites=[(dkk, hh // 4)])
            E, Ek = erot.next()
            S.op("act", lambda e: e.activation(out=E[:, 0:512], in_=d0[:, :], func=AF.Exp), reads=[(dkk, 0)], pwrites=[Ek])
            S.op("act", lambda e: e.activation(out=E[:, 512:1024], in_=d1[:, :], func=AF.Exp), reads=[(dkk, 1)], pwrites=[Ek])
            return E, Ek

        def stageB(X, g, E, Ek):
            xs, xsk, cbm, cbk, ecs, ecsk, etot, etk = X["xs"], X["xsk"], X["cbm"], X["cbk"], X["ecs"], X["ecsk"], X["etot"], X["etk"]
            bct3, bctk, bt, btk, ytok, ytk = X["bct3"], X["bctk"], X["bt"], X["btk"], X["ytok"], X["ytk"]
            hsl = slice(g * 8, (g + 1) * 8)
            gs = slice(g * 512, (g + 1) * 512)
            E3 = E.rearrange("p (h l) -> p h l", h=8)
            MT, MTk = mrot.next()
            MT3 = MT.rearrange("p (h l) -> p h l", h=8)
            S.op("dve", lambda e: e.tensor_tensor(
                out=MT3, in0=E3, in1=cbm[:, g * 128:(g + 1) * 128].unsqueeze(1).broadcast_to([128, 8, 128]), op=ALU.mult),
                reads=[Ek, cbk], writes=[MTk])
            S.op("dve", lambda e: e.tensor_tensor(
                out=xst[:, gs].rearrange("p (h d) -> p h d", h=8), in0=xs[:, gs].rearrange("p (h d) -> p h d", h=8),
                in1=E3[:, :, 127:128].broadcast_to([128, 8, 64]), op=ALU.mult),
                reads=[Ek, xsk], writes=[("xst", g % 2)])
            ib, ibk = psb[4], ("pb", 4)
            nb_, nbk = psb[5], ("pb", 5)
            sn, snk = psb[6], ("pb", 6)
            for hh in range(8):
                hs = slice(g * 512 + hh * 64, g * 512 + (hh + 1) * 64)
                hhs = slice(hh * 64, (hh + 1) * 64)
                S.op("pe", lambda e, hh=hh, hs=hs, hhs=hhs: e.matmul(
                    ib[:, hhs], lhsT=MT3[:, hh, :], rhs=xs[:, hs], start=True, stop=True),
                    reads=[MTk, xsk], writes=[ibk] if hh == 0 else (), pwrites=() if hh == 0 else [ibk])
            S.op("pe", lambda e: e.matmul(nb_[:, :], lhsT=bct3[:, 4 + g, :], rhs=state_bf[:, gs], start=True, stop=True),
                 reads=[bctk, ("state_bf", g)], writes=[nbk])
            S.op("pe", lambda e: e.matmul(sn[:, :], lhsT=bt[:, g * 128:(g + 1) * 128], rhs=xst[:, gs], start=True, stop=True),
                 reads=[btk, ("xst", g % 2)], writes=[snk])
            tmpy, tmk = tmpys[g % 2], ("tmpy", g % 2)
            S.op("dve", lambda e: e.tensor_tensor(
                out=tmpy.rearrange("p (h d) -> p h d", h=8), in0=nb_[:, :].rearrange("p (h d) -> p h d", h=8),
                in1=ecs[:, hsl].unsqueeze(2).broadcast_to([128, 8, 64]), op=ALU.mult),
                reads=[nbk, ecsk], writes=[tmk])
            S.op("dve", lambda e: e.tensor_tensor(out=ytok[:, gs], in0=ib[:, :], in1=tmpy, op=ALU.add),
                 reads=[ibk, tmk], pwrites=[ytk])
            S.op("dve", lambda e: e.tensor_tensor(
                out=state[:, gs].rearrange("p (h d) -> p h d", h=8), in0=state[:, gs].rearrange("p (h d) -> p h d", h=8),
                in1=etot[:, hsl].unsqueeze(2).broadcast_to([128, 8, 64]), op=ALU.mult),
                reads=[etk], writes=[("state", g)])
            S.op("dve", lambda e: e.tensor_tensor(out=state[:, gs], in0=sn[:, :], in1=state[:, gs], op=ALU.add),
                 reads=[snk], writes=[("state", g)])
            S.op("act", lambda e: e.activation(out=state_bf[:, gs], in_=state[:, gs], func=AF.Copy),
                 reads=[("state", g)], writes=[("state_bf", g)])

        pending = None
        Xn = chunk_pre(0)
        for c in range(32):
            X = Xn
            if c >= 1:
                zt, ztk = zrot.next()
                S.dma("sp", zt, z_s[(c - 1) * 128:c * 128, :], reads=["z_s"], writes=[ztk])
                pending["zt"], pending["ztk"] = zt, ztk
            nxt = X["A0"]
            for g in range(4):
                cur = nxt
                if g < 3:
                    nxt = stageA(X, g + 1)
                if g == 1:
                    xt, xtk = X["xt"], X["xtk"]
                    S.op("pool", lambda e, xt=xt: e.tensor_tensor(
                        out=dxs.rearrange("p (h d) -> p h d", h=32), in0=xt.rearrange("p (h d) -> p h d", h=32),
                        in1=rp[:, RP_DSK:RP_DSK + 32].unsqueeze(2).broadcast_to([128, 32, 64]), op=ALU.mult),
                        reads=[xtk, "rp"], writes=["dxs"])
                if g == 3 and c + 1 < 32:
                    Xn = chunk_pre(c + 1)
                stageB(X, g, *cur)
                if pending is not None:
                    tail_piece(pending, g)
            pending = {"c": c, "ytok": X["ytok"], "ytk": X["ytk"], "gst": gsts[c % 2], "gk": ("gst", c % 2)}
        zt, ztk = zrot.next()
        S.dma("sp", zt, z_s[31 * 128:32 * 128, :], reads=["z_s"], writes=[ztk])
        pending["zt"], pending["ztk"] = zt, ztk
        for g in range(4):
            tail_piece(pending, g)

    def layer1(seq):
        rms_to_hT(h1_s, "h1_s", CP_ODNORM)
        l1_inproj(seq)
        l1_ssd(seq)
        outproj(wb_od_out, WK["od_out"], h1_s, "h1_s", out_d[seq], "out", final=True)

    dbg = None
    if stop_after in ("hT", "yconv", "attn"):
        dbg = nc.dram_tensor("dbg", [16, 128, T], BF16, kind="ExternalOutput").ap()
    if stop_after == "conv":
        dbg = nc.dram_tensor("dbg", [8, 128, T], F32, kind="ExternalOutput").ap()

    for seq in range(nseq):
        x_seq = x_d[seq]
        rms_to_hT(x_seq, "x_in", CP_EVNORM)
        if stop_after == "hT":
            for k in range(8):
                S.dma("sp", dbg[k], hT[:, k, :], reads=["hT"], pwrites=["dbg"])
            break
        l0_conv(seq)
        if stop_after == "conv":
            S.dma("sp", dbg, conv_s, reads=[("conv_s", c) for c in range(8)], pwrites=["dbg"])
            break
        l0_ln(seq)
        if stop_after == "yconv":
            S.dma("sp", dbg[0:8], y_s[0:8], reads=[("y_s", c) for c in range(8)], pwrites=["dbg"])
            break
        l0_attn(seq)
        if stop_after == "attn":
            S.dma("sp", dbg, y_s, reads=[("y_s", c) for c in range(16)], pwrites=["dbg"])
            break
        if stop_after == "l0":
            outproj(wb_ev_out, WK["ev_out"], x_seq, "x_in", out_d[seq], "out", final=False)
            continue
        outproj(wb_ev_out, WK["ev_out"], x_seq, "x_in", h1_s, "h1_s", final=False)
        layer1(seq)
    S.barrier()
    S.emit()
    return nc


def host_consts(inp):
    f = np.float32
    cp = np.zeros((128, CP_N), f)

    def colv(v):
        return np.ascontiguousarray(np.asarray(v, f).reshape(-1, 128).T)
    cp[:, CP_EVNORM:CP_EVNORM + 8] = colv(inp["ev_norm_w"][0])
    dw = np.asarray(inp["ev_dw_w"][0], f)
    cp[:, CP_DWW:CP_DWW + 248] = dw.reshape(31, 8, 128).transpose(2, 1, 0).reshape(128, 248)
    cp[:, CP_DWB:CP_DWB + 8] = colv(inp["ev_dw_b"][0])
    cp[:, CP_LNW:CP_LNW + 8] = colv(inp["ev_ln_w"][0])
    cp[:, CP_LNB:CP_LNB + 8] = colv(inp["ev_ln_b"][0])
    cp[:, CP_ODNORM:CP_ODNORM + 8] = colv(inp["od_norm_w"][0])
    cw = np.asarray(inp["od_conv_w"][0], f)
    cp[:, CP_CW:CP_CW + 96] = cw.reshape(4, 24, 128).transpose(2, 1, 0).reshape(128, 96)
    cp[:, CP_CB:CP_CB + 24] = colv(inp["od_conv_b"][0])
    rp = np.zeros((128, RP_N), f)
    rp[:, RP_FNORM:RP_FNORM + 1024] = np.asarray(inp["final_norm_w"], f)[None, :]
    rp[:, RP_GNORM:RP_GNORM + 2048] = np.asarray(inp["od_gnorm_w"][0], f)[None, :]
    rp[:, RP_DSK:RP_DSK + 32] = np.asarray(inp["od_d"][0], f)[None, :]
    rp[:, RP_DTB:RP_DTB + 32] = np.asarray(inp["od_dt_bias"][0], f)[None, :]
    rp[:, RP_ALOG:RP_ALOG + 32] = np.asarray(inp["od_a_log"][0], f)[None, :]
    j = np.arange(128)
    cm = np.zeros((128, 7, 128), f)
    cm[:, 0, :] = np.eye(128, dtype=f)
    cm[:, 1, :] = 1.0
    cm[:, 2, :] = -(j[:, None] >= j[None, :]).astype(f)
    cm[:, 3, :] = 0.0
    cm[:, 4, :] = -1.0
    cm[:, 5, :] = (j[:, None] <= j[None, :]).astype(f)
    cm[:, 6, :] = (j[:, None] > j[None, :]).astype(f)
    mask0 = (j[:, None] < np.arange(512)[None, :]).astype(f)
    return cp, rp, cm, mask0


def make_in_maps(inputs, nseq=2, ncores=NCORES):
    cp, rp, cm, mask0 = host_consts(inputs)
    x = np.asarray(inputs["x"], np.float32)
    common = {
        "ev_w_in": np.ascontiguousarray(np.asarray(inputs["ev_w_in"], np.float32)[0]),
        "ev_w_out": np.ascontiguousarray(np.asarray(inputs["ev_w_out"], np.float32)[0]),
        "od_w_in": np.ascontiguousarray(np.asarray(inputs["od_w_in"], np.float32)[0]),
        "od_w_out": np.ascontiguousarray(np.asarray(inputs["od_w_out"], np.float32)[0]),
        "colpack": cp, "rowpack": rp, "cmats": cm, "mask0": mask0,
    }
    maps = []
    for c in range(ncores):
        m = dict(common)
        m["x"] = np.ascontiguousarray(x[c * nseq:(c + 1) * nseq])
        maps.append(m)
    return maps


def kernel(**inputs):
    nc = build(nseq=2)
    maps = make_in_maps(inputs, 2, NCORES)
    res = run_bass_kernel_spmd(nc, maps, core_ids=list(range(NCORES)))
    outs = [np.asarray(r["out"], np.float32) for r in res.results]
    return np.concatenate(outs, axis=0)
```

```python
import numpy as np
from contextlib import ExitStack
import concourse.bass as bass
import concourse.mybir as mybir
from concourse.bass_utils import run_bass_kernel_spmd

F32 = mybir.dt.float32
BF16 = mybir.dt.bfloat16
AF = mybir.ActivationFunctionType
ALU = mybir.AluOpType
AX = mybir.AxisListType

T = 4096
D = 1024
NCORES = 8
EPS = 1e-6
IN_EVEN = 7168
IN_ODD = 5152


class _Op:
    __slots__ = ("eng", "fn", "deps", "sig", "sigval", "isdma", "dma_i", "idx")


class Sched:
    ENGS = ("sp", "act", "dve", "pool", "pe")

    def __init__(self, nc, ring=8):
        self.nc = nc
        self.ops = {e: [] for e in self.ENGS}
        self.res = {}
        self.ndma = {e: 0 for e in self.ENGS}
        self.ring = ring
        self.pend = {e: [] for e in self.ENGS}
        self.alldma = []

    def barrier(self):
        lst = []
        for e in self.ENGS:
            for op in reversed(self.ops[e]):
                if not op.isdma:
                    lst.append(op)
                    break
        lst.extend(self.alldma)
        self.alldma = []
        for e in self.ENGS:
            self.pend[e] = list(lst)

    def _add(self, eng, fn, reads, writes, pwrites, isdma):
        op = _Op()
        op.eng, op.fn, op.isdma, op.sig, op.sigval, op.dma_i = eng, fn, isdma, False, 0, -1
        op.idx = len(self.ops[eng])
        deps = {}
        for k in reads:
            r = self.res.get(k)
            if r:
                for w in r["w"]:
                    deps[id(w)] = w
        for k in list(writes) + list(pwrites):
            r = self.res.get(k)
            if r:
                for w in r["w"]:
                    deps[id(w)] = w
                for w in r["rc"].values():
                    deps[id(w)] = w
                for w in r["rd"]:
                    deps[id(w)] = w
        for w in self.pend[eng]:
            deps[id(w)] = w
        self.pend[eng] = []
        deps.pop(id(op), None)
        best = {}
        keep = []
        for d in deps.values():
            if d.isdma:
                keep.append(d)
            elif d.eng not in best or d.idx > best[d.eng].idx:
                best[d.eng] = d
        op.deps = keep + list(best.values())
        for d in op.deps:
            if (not d.isdma) and not (d.eng == eng == "pe"):
                d.sig = True
        for k in reads:
            r = self.res.setdefault(k, {"w": [], "rc": {}, "rd": []})
            if isdma:
                r["rd"].append(op)
            else:
                r["rc"][eng] = op
        for k in writes:
            self.res[k] = {"w": [op], "rc": {}, "rd": []}
        for k in pwrites:
            r = self.res.get(k)
            if r is None or r["rc"] or r["rd"]:
                self.res[k] = {"w": [op], "rc": {}, "rd": []}
            else:
                r["w"].append(op)
        if isdma:
            op.dma_i = self.ndma[eng]
            self.ndma[eng] += 1
            self.alldma.append(op)
        self.ops[eng].append(op)
        return op

    def op(self, eng, fn, reads=(), writes=(), pwrites=()):
        return self._add(eng, fn, reads, writes, pwrites, False)

    def dma(self, eng, out, in_, reads=(), writes=(), pwrites=()):
        return self._add(eng, lambda e: e.dma_start(out=out, in_=in_), reads, writes, pwrites, True)

    def emit(self):
        nc = self.nc
        for e in self.ENGS:
            c = 0
            for op in self.ops[e]:
                if (not op.isdma) and op.sig:
                    c += 1
                    op.sigval = c
        with ExitStack() as st:
            psem = {e: st.enter_context(nc.semaphore("ps_" + e)) for e in self.ENGS}
            rings = {}
            for e in self.ENGS:
                if self.ndma[e]:
                    rings[e] = [st.enter_context(nc.semaphore("dr_%s_%d" % (e, i))) for i in range(self.ring)]
            block = st.enter_context(nc.Block())
            R = self.ring

            def tok(d):
                if d.isdma:
                    return rings[d.eng][d.dma_i % R], 16 * (d.dma_i // R + 1)
                return psem[d.eng], d.sigval

            def run(ename, eng):
                waited = {}

                def wait(sem, val):
                    key = id(sem)
                    if waited.get(key, 0) >= val:
                        return
                    eng.wait_ge(sem, val)
                    waited[key] = val

                for op in self.ops[ename]:
                    for d in op.deps:
                        if (not d.isdma) and d.eng == ename and ename == "pe":
                            continue
                        s, v = tok(d)
                        wait(s, v)
                    if op.isdma:
                        if op.dma_i >= R:
                            wait(rings[ename][op.dma_i % R], 16 * (op.dma_i // R))
                        ins = op.fn(eng)
                        ins.then_inc(rings[ename][op.dma_i % R], 16)
                    else:
                        ins = op.fn(eng)
                        if op.sig:
                            ins.then_inc(psem[ename], 1)
                n = self.ndma[ename]
                if n:
                    for j in range(R):
                        cnt = (n - j + R - 1) // R if n > j else 0
                        if cnt:
                            wait(rings[ename][j], 16 * cnt)

            @block.sync
            def _(e):
                run("sp", e)

            @block.scalar
            def _(e):
                run("act", e)

            @block.vector
            def _(e):
                run("dve", e)

            @block.gpsimd
            def _(e):
                run("pool", e)

            @block.tensor
            def _(e):
                run("pe", e)


class Rot:
    def __init__(self, name, bufs):
        self.name, self.bufs, self.i = name, bufs, -1

    def next(self):
        self.i += 1
        j = self.i % len(self.bufs)
        return self.bufs[j], (self.name, j)


CP_EVNORM = 0
CP_DWW = 8
CP_DWB = CP_DWW + 248
CP_LNW = CP_DWB + 8
CP_LNB = CP_LNW + 8
CP_ODNORM = CP_LNB + 8
CP_CW = CP_ODNORM + 8
CP_CB = CP_CW + 96
CP_N = CP_CB + 24
RP_FNORM = 0
RP_GNORM = 1024
RP_DSK = 3072
RP_DTB = 3104
RP_ALOG = 3136
RP_N = 3168


ARENA_WORDS = 27616


def build(nseq=2, stop_after=None):
    nc = bass.Bass("TRN2", target_bir_lowering=False)
    S = Sched(nc)

    def din(name, shape, dt=F32):
        return nc.dram_tensor(name, list(shape), dt, kind="ExternalInput").ap()

    def dscr(name, shape, dt):
        return nc.dram_tensor(name, list(shape), dt, kind="Internal").ap()

    x_d = din("x", [nseq, T, D])
    ev_w_in = din("ev_w_in", [D, IN_EVEN])
    ev_w_out = din("ev_w_out", [2048, D])
    od_w_in = din("od_w_in", [D, IN_ODD])
    od_w_out = din("od_w_out", [2048, D])
    colpack = din("colpack", [128, CP_N])
    rowpack = din("rowpack", [128, RP_N])
    cmats = din("cmats", [128, 7, 128])
    mask0_d = din("mask0", [128, 512])
    out_d = nc.dram_tensor("out", [nseq, T, D], F32, kind="ExternalOutput").ap()

    wb_ev_in = dscr("wb_ev_in", [D, IN_EVEN], BF16)
    wb_ev_out = dscr("wb_ev_out", [2048, D], BF16)
    wb_od_in = dscr("wb_od_in", [D, IN_ODD], BF16)
    wb_od_out = dscr("wb_od_out", [2048, D], BF16)
    conv_s = dscr("conv_s", [8, 128, T], F32)
    y_s = dscr("y_s", [16, 128, T], BF16)
    h1_s = dscr("h1_s", [T, D], F32)

    A = nc.alloc_sbuf_tensor
    PS = nc.alloc_psum_tensor

    cp = A("cp", [128, CP_N], F32)
    rp = A("rp", [128, RP_N], F32)
    cm = A("cm", [128, 7, 128], F32)
    cmb = A("cmb", [128, 7, 128], BF16)
    mask0 = A("mask0s", [128, 512], F32)
    hT = A("hT", [128, 8, T], BF16)
    wch = [A("wch%d" % i, [128, 8, 128], BF16) for i in range(6)]
    stat = [A("st%d" % i, [128, 4], F32) for i in range(2)]
    arena = A("arena", [128, ARENA_WORDS], F32)
    psb = [PS("psb%d" % i, [128, 512], F32) for i in range(8)]

    class Carver:
        def __init__(self):
            self.off = 0

        def f32(self, n):
            v = arena[:, self.off:self.off + n]
            self.off += n
            assert self.off <= ARENA_WORDS, self.off
            return v

        def bf16(self, n):
            w = (n + 1) // 2
            v = arena[:, self.off:self.off + w].bitcast(BF16)
            self.off += w
            assert self.off <= ARENA_WORDS, self.off
            return v

    S.dma("sp", cp[:, :], colpack, writes=["cp"])
    S.dma("sp", rp[:, :], rowpack, writes=["rp"])
    S.dma("sp", cm[:, :, :], cmats, writes=["cm"])
    S.dma("sp", mask0[:, :], mask0_d, writes=["mask0"])
    S.op("dve", lambda e: e.tensor_copy(out=cmb[:, :, :], in_=cm[:, :, :]), reads=["cm"], writes=["cmb"])
    ident = cm[:, 0, :]
    ones_f = cm[:, 1, :]
    negU_b = cmb[:, 2, :]
    zeros_b = cmb[:, 3, :]
    negones_b = cmb[:, 4, :]

    WK = {}
    for nm, src, dst, rows in (("ev_in", ev_w_in, wb_ev_in, D), ("ev_out", ev_w_out, wb_ev_out, 2048),
                               ("od_in", od_w_in, wb_od_in, D), ("od_out", od_w_out, wb_od_out, 2048)):
        step = rows // 8
        for i in range(8):
            S.dma("pool", dst[i * step:(i + 1) * step, :], src[i * step:(i + 1) * step, :], pwrites=[("W", nm)])
        WK[nm] = ("W", nm)

    strot = Rot("st", stat)
    wrot = Rot("wch", wch)

    def load_wchunk(wb, wkey, col0):
        buf, key = wrot.next()
        S.dma("sp", buf[:, :, :], wb.rearrange("(k p) n -> p k n", p=128)[:, :, col0:col0 + 128],
              reads=[wkey], writes=[key])
        return buf, key

    def rstd_ops(st_, sk, src, srck, sqb):
        S.op("act", lambda e: e.activation(out=sqb, in_=src, func=AF.Square), reads=[srck], writes=["sqb"])
        S.op("dve", lambda e: e.tensor_reduce(out=st_[:, 0:1], in_=sqb, axis=AX.X, op=ALU.add),
             reads=["sqb"], writes=[(sk, 0)])
        S.op("dve", lambda e: e.tensor_scalar(out=st_[:, 1:2], in0=st_[:, 0:1], scalar1=1.0 / D, scalar2=EPS,
                                              op0=ALU.mult, op1=ALU.add), reads=[(sk, 0)], writes=[(sk, 1)])
        S.op("act", lambda e: e.activation(out=st_[:, 2:3], in_=st_[:, 1:2], func=AF.Sqrt),
             reads=[(sk, 1)], writes=[(sk, 2)])
        S.op("dve", lambda e: e.reciprocal(out=st_[:, 3:4], in_=st_[:, 2:3]), reads=[(sk, 2)], writes=[(sk, 3)])

    def rms_to_hT(src, srckey, wcol0):
        S.barrier()
        cv = Carver()
        xbufs = [cv.f32(D) for _ in range(3)]
        sqbs = [cv.f32(D) for _ in range(2)]
        ssq = cv.f32(32)
        rs = cv.f32(32)
        xrot = Rot("xb", xbufs)
        sqrot = Rot("sqb2", sqbs)
        for t in range(32):
            xb, xk = xrot.next()
            sq_, sqk = sqrot.next()
            S.dma("sp", xb, src[t * 128:(t + 1) * 128, :], reads=[srckey], writes=[xk])
            S.op("act", lambda e, xb=xb, sq_=sq_: e.activation(out=sq_, in_=xb, func=AF.Square), reads=[xk], writes=[sqk])
            S.op("dve", lambda e, sq_=sq_, t=t: e.tensor_reduce(out=ssq[:, t:t + 1], in_=sq_, axis=AX.X, op=ALU.add),
                 reads=[sqk], pwrites=["ssq"])
        S.op("dve", lambda e: e.tensor_scalar(out=rs, in0=ssq, scalar1=1.0 / D, scalar2=EPS, op0=ALU.mult, op1=ALU.add),
             reads=["ssq"], writes=["rs"])
        S.op("act", lambda e: e.activation(out=rs, in_=rs, func=AF.Sqrt), reads=["rs"], writes=["rs"])
        S.op("dve", lambda e: e.reciprocal(out=rs, in_=rs), reads=["rs"], writes=["rs"])
        prot = Rot("psT", [(psb[0], psb[1]), (psb[2], psb[3]), (psb[4], psb[5])])

        def p2_load(t):
            xb, xk = xrot.next()
            S.dma("sp", xb, src[t * 128:(t + 1) * 128, :], reads=[srckey], writes=[xk])
            if t % 2:
                S.op("act", lambda e, xb=xb, t=t: e.activation(out=xb, in_=xb, func=AF.Copy, scale=rs[:, t:t + 1]),
                     reads=["rs"], writes=[xk])
            else:
                S.op("dve", lambda e, xb=xb, t=t: e.tensor_scalar(
                    out=xb, in0=xb, scalar1=rs[:, t:t + 1], scalar2=None, op0=ALU.mult), reads=["rs"], writes=[xk])
            return xb, xk

        def p2_rest(t, xb, xk):
            (p0, p1), pk = prot.next()
            for k in range(8):
                pt = p0 if k < 4 else p1
                kk = k % 4
                S.op("pe", lambda e, pt=pt, kk=kk, k=k, xb=xb: e.transpose(
                    out=pt[:, kk * 128:(kk + 1) * 128], in_=xb[:, k * 128:(k + 1) * 128], identity=ident),
                    reads=[xk, "cm"], pwrites=[(pk, k // 4)])
            for half in range(2):
                pt = p0 if half == 0 else p1
                eng = "dve" if half == 0 else "act"
                if eng == "dve":
                    S.op("dve", lambda e, pt=pt, half=half, t=t: e.tensor_tensor(
                        out=hT[:, half * 4:(half + 1) * 4, t * 128:(t + 1) * 128],
                        in0=pt[:, :].rearrange("p (k t) -> p k t", k=4),
                        in1=cp[:, wcol0 + half * 4:wcol0 + (half + 1) * 4].unsqueeze(2).broadcast_to([128, 4, 128]),
                        op=ALU.mult), reads=[(pk, half), "cp"], pwrites=["hT"])
                else:
                    for kk in range(4):
                        k = half * 4 + kk
                        S.op("act", lambda e, pt=pt, kk=kk, k=k, t=t: e.activation(
                            out=hT[:, k, t * 128:(t + 1) * 128], in_=pt[:, kk * 128:(kk + 1) * 128],
                            func=AF.Copy, scale=cp[:, wcol0 + k:wcol0 + k + 1]),
                            reads=[(pk, half), "cp"], pwrites=["hT"])

        nxt = p2_load(0)
        for t in range(32):
            cur = nxt
            if t + 1 < 32:
                nxt = p2_load(t + 1)
            p2_rest(t, *cur)

    def proj_fm(ps, pkey, wbuf, wkey, sb):
        for k in range(8):
            S.op("pe", lambda e, k=k: e.matmul(ps[:, :], lhsT=wbuf[:, k, :], rhs=hT[:, k, sb * 512:(sb + 1) * 512],
                                               start=(k == 0), stop=(k == 7)),
                 reads=[wkey, "hT"], writes=[pkey] if k == 0 else (), pwrites=() if k == 0 else [pkey])

    PE_TAPS = list(range(14, 31))
    DVE_TAPS = list(range(0, 14))

    def l0_conv(seq):
        S.barrier()
        cv = Carver()
        upad = [cv.f32(4128) for _ in range(2)]
        acc = [cv.f32(T) for _ in range(2)]
        sig = [cv.f32(512) for _ in range(2)]
        ubf = [cv.bf16(4128) for _ in range(2)]
        dgb = [cv.bf16(len(PE_TAPS) * 128) for _ in range(2)]
        for i in range(2):
            S.op("pool", lambda e, i=i: e.memset(upad[i][:, 0:30], 0.0), writes=[("upadz", i)])
            S.op("pool", lambda e, i=i: e.memset(ubf[i][:, 0:30], 0.0), writes=[("ubfz", i)])
        sigrot = Rot("sig", sig)
        prot = Rot("psB1", [(psb[0], psb[1]), (psb[2], psb[3])])
        crot = Rot("psC", [psb[4], psb[5], psb[6], psb[7]])
        def setup(c):
            X = {"c": c}
            X["wa"], X["wak"] = load_wchunk(wb_ev_in, WK["ev_in"], c * 128)
            X["wg"], X["wgk"] = load_wchunk(wb_ev_in, WK["ev_in"], 1024 + c * 128)
            X["up"], X["upk"] = upad[c % 2], ("upad", c % 2)
            X["ub"], X["ubk"] = ubf[c % 2], ("ubf", c % 2)
            X["ac"], X["ack"] = acc[c % 2], ("acc", c % 2)
            dg = dgb[c % 2].rearrange("p (i m) -> p i m", i=len(PE_TAPS))
            dgk = ("dg", c % 2)
            X["dg"], X["dgk"] = dg, dgk
            for i, k in enumerate(PE_TAPS):
                col = CP_DWW + c * 31 + k
                S.op("act", lambda e, dg=dg, i=i, col=col: e.activation(
                    out=dg[:, i, :], in_=cmb[:, 0, :], func=AF.Copy, scale=cp[:, col:col + 1]),
                    reads=["cmb", "cp"], writes=[dgk] if i == 0 else (), pwrites=() if i == 0 else [dgk])
            return X

        def P_block(X, sb):
            c, up, ub, upk, ubk = X["c"], X["up"], X["ub"], X["upk"], X["ubk"]
            (pa, pb), pk = prot.next()
            proj_fm(pa, (pk, "a"), X["wa"], X["wak"], sb)
            proj_fm(pb, (pk, "b"), X["wg"], X["wgk"], sb)
            sg, sgk = sigrot.next()
            usl = slice(30 + sb * 512, 30 + (sb + 1) * 512)
            S.op("act", lambda e, sg=sg, pb=pb: e.activation(out=sg, in_=pb[:, :], func=AF.Sigmoid),
                 reads=[(pk, "b")], writes=[sgk])
            S.op("dve", lambda e, sg=sg, pa=pa, up=up, usl=usl: e.tensor_tensor(
                out=up[:, usl], in0=pa[:, :], in1=sg, op=ALU.mult),
                reads=[(pk, "a"), sgk, ("upadz", c % 2)], pwrites=[upk])
            S.op("act", lambda e, up=up, ub=ub, usl=usl: e.activation(out=ub[:, usl], in_=up[:, usl], func=AF.Copy),
                 reads=[upk, ("ubfz", c % 2)], pwrites=[ubk])

        def tap(X, n_, k):
            c, up, ac, upk, ack = X["c"], X["up"], X["ac"], X["upk"], X["ack"]
            col = CP_DWW + c * 31 + k
            if n_ == 0:
                S.op("dve", lambda e, up=up, ac=ac, col=col, c=c, k=k: e.tensor_scalar(
                    out=ac, in0=up[:, k:k + T], scalar1=cp[:, col:col + 1],
                    scalar2=cp[:, CP_DWB + c:CP_DWB + c + 1], op0=ALU.mult, op1=ALU.add),
                    reads=[upk, "cp"], writes=[ack])
            else:
                S.op("dve", lambda e, up=up, ac=ac, col=col, k=k: e.scalar_tensor_tensor(
                    out=ac, in0=up[:, k:k + T], scalar=cp[:, col:col + 1], in1=ac,
                    op0=ALU.mult, op1=ALU.add),
                    reads=[upk, "cp"], writes=[ack])

        def C_block(X, sb):
            dg, dgk, ub, ubk, ac, ack = X["dg"], X["dgk"], X["ub"], X["ubk"], X["ac"], X["ack"]
            pc, pck = crot.next()
            for i, k in enumerate(PE_TAPS):
                S.op("pe", lambda e, pc=pc, dg=dg, i=i, k=k, sb=sb, ub=ub: e.matmul(
                    pc[:, :], lhsT=dg[:, i, :], rhs=ub[:, k + sb * 512:k + (sb + 1) * 512],
                    start=(i == 0), stop=(i == len(PE_TAPS) - 1)),
                    reads=[dgk, ubk], writes=[pck] if i == 0 else (), pwrites=() if i == 0 else [pck])
            S.op("dve", lambda e, pc=pc, ac=ac, sb=sb: e.tensor_tensor(
                out=ac[:, sb * 512:(sb + 1) * 512], in0=pc[:, :], in1=ac[:, sb * 512:(sb + 1) * 512], op=ALU.add),
                reads=[pck], writes=[ack])

        Xn = setup(0)
        for sb in range(8):
            P_block(Xn, sb)
        for c in range(8):
            X = Xn
            if c + 1 < 8:
                Xn = setup(c + 1)
            for n_, k in enumerate(DVE_TAPS):
                tap(X, n_, k)
                if n_ < 8:
                    if c + 1 < 8:
                        P_block(Xn, n_)
                    C_block(X, n_)
            S.dma("pool", conv_s[c], X["ac"], reads=[X["ack"]], pwrites=[("conv_s", c)])

    def l0_ln(seq):
        S.barrier()
        cv = Carver()
        mean = cv.f32(T)
        rstd = cv.f32(T)
        cvts = [cv.bf16(4096) for _ in range(2)]
        sqs_ = [cv.bf16(4096) for _ in range(2)]
        tmp512 = cv.f32(512)
        cvk = [("conv_s", c) for c in range(8)]
        cvrot = Rot("cvb", cvts)
        sqrot = Rot("sqb3", sqs_)
        psrot = Rot("psLN", [(psb[4], psb[5]), (psb[6], psb[7])])
        ones_b = cmb[:, 1, :]
        for sb in range(8):
            cvt, cvtk = cvrot.next()
            sq, sqk = sqrot.next()
            S.dma("pool", cvt.rearrange("p (c t) -> p c t", c=8),
                  conv_s.rearrange("c p t -> p c t")[:, :, sb * 512:(sb + 1) * 512], reads=cvk, writes=[cvtk])
            S.op("act", lambda e, sq=sq, cvt=cvt: e.activation(out=sq, in_=cvt, func=AF.Square), reads=[cvtk], writes=[sqk])
            (ps1, ps2), pk = psrot.next()
            for c in range(8):
                S.op("pe", lambda e, c=c, ps1=ps1, cvt=cvt: e.matmul(ps1[:, :], lhsT=ones_b, rhs=cvt[:, c * 512:(c + 1) * 512],
                                                                    start=(c == 0), stop=(c == 7)),
                     reads=[cvtk, "cmb"], writes=[(pk, 1)] if c == 0 else (), pwrites=() if c == 0 else [(pk, 1)])
            for c in range(8):
                S.op("pe", lambda e, c=c, ps2=ps2, sq=sq: e.matmul(ps2[:, :], lhsT=ones_b, rhs=sq[:, c * 512:(c + 1) * 512],
                                                                   start=(c == 0), stop=(c == 7)),
                     reads=[sqk, "cmb"], writes=[(pk, 2)] if c == 0 else (), pwrites=() if c == 0 else [(pk, 2)])
            msl = mean[:, sb * 512:(sb + 1) * 512]
            rsl = rstd[:, sb * 512:(sb + 1) * 512]
            S.op("act", lambda e, msl=msl, ps1=ps1: e.activation(out=msl, in_=ps1[:, :], func=AF.Copy, scale=1.0 / 1024),
                 reads=[(pk, 1)], pwrites=["mean"])
            S.op("dve", lambda e, msl=msl: e.tensor_tensor(out=tmp512, in0=msl, in1=msl, op=ALU.mult),
                 reads=["mean"], writes=["tmp512"])
            S.op("dve", lambda e, rsl=rsl, ps2=ps2: e.scalar_tensor_tensor(out=rsl, in0=ps2[:, :], scalar=1.0 / 1024,
                                                                           in1=tmp512, op0=ALU.mult, op1=ALU.subtract),
                 reads=[(pk, 2), "tmp512"], pwrites=["rstd"])
            S.op("dve", lambda e, rsl=rsl: e.tensor_scalar(out=rsl, in0=rsl, scalar1=EPS, scalar2=None, op0=ALU.add),
                 reads=["rstd"], pwrites=["rstd"])
            S.op("act", lambda e, rsl=rsl: e.activation(out=rsl, in_=rsl, func=AF.Sqrt),
                 reads=["rstd"], pwrites=["rstd"])
            S.op("dve", lambda e, rsl=rsl: e.reciprocal(out=rsl, in_=rsl), reads=["rstd"], pwrites=["rstd"])
        S.barrier()
        cv = Carver()
        cv.f32(2 * T)
        cch = [cv.f32(T) for _ in range(2)]
        ych = [cv.bf16(T) for _ in range(2)]
        sgt = [cv.f32(512) for _ in range(2)]
        sgrot = Rot("sgt", sgt)
        prot = Rot("psB2", [psb[0], psb[1], psb[2], psb[3]])
        for c in range(8):
            wg, wgk = load_wchunk(wb_ev_in, WK["ev_in"], 2048 + c * 128)
            cc = cch[c % 2]
            cck = ("cch", c % 2)
            yc = ych[c % 2]
            yck = ("ych", c % 2)
            S.dma("sp", cc, conv_s[c], reads=[("conv_s", c)], writes=[cck])
            for sb in range(8):
                ps, pk = prot.next()
                proj_fm(ps, pk, wg, wgk, sb)
                sg, sgk = sgrot.next()
                sl = slice(sb * 512, (sb + 1) * 512)
                S.op("act", lambda e, sg=sg, ps=ps: e.activation(out=sg, in_=ps[:, :], func=AF.Silu),
                     reads=[pk], writes=[sgk])
                S.op("dve", lambda e, cc=cc, sl=sl: e.tensor_tensor(out=cc[:, sl], in0=cc[:, sl], in1=mean[:, sl],
                                                                    op=ALU.subtract),
                     reads=[cck], pwrites=[cck])
                S.op("pool", lambda e, cc=cc, sl=sl: e.tensor_tensor(out=cc[:, sl], in0=cc[:, sl], in1=rstd[:, sl],
                                                                     op=ALU.mult),
                     reads=[cck], pwrites=[cck])
                S.op("act", lambda e, cc=cc, sl=sl, c=c: e.activation(
                    out=cc[:, sl], in_=cc[:, sl], func=AF.Silu, scale=cp[:, CP_LNW + c:CP_LNW + c + 1],
                    bias=cp[:, CP_LNB + c:CP_LNB + c + 1]), reads=[cck, "cp"], pwrites=[cck])
                S.op("dve", lambda e, cc=cc, sl=sl, sg=sg, yc=yc: e.tensor_tensor(
                    out=yc[:, sl], in0=cc[:, sl], in1=sg, op=ALU.mult),
                    reads=[cck, sgk], pwrites=[yck])
            S.dma("pool", y_s[c], yc, reads=[yck], pwrites=[("y_s", c)])

    def l0_attn(seq):
        S.barrier()
        cv = Carver()
        gaT = cv.bf16(T)
        qTb = cv.bf16(T)
        kTb = cv.bf16(T)
        vtok = cv.bf16(T)
        yhb = [cv.bf16(T) for _ in range(2)]
        e_b = [cv.f32(512) for _ in range(3)]
        spf = [cv.f32(512) for _ in range(2)]
        spb = [cv.bf16(512) for _ in range(4)]
        wtb = [cv.bf16(512) for _ in range(3)]
        wtf = [cv.f32(512) for _ in range(2)]
        Sf = cv.f32(512)
        Sb = [cv.bf16(512) for _ in range(3)]
        scale = 128.0 ** -0.5
        erot, sfrot, sbrot, wbrot, wfrot, Sbrot = (Rot("eb", e_b), Rot("spf", spf), Rot("spb", spb),
                                                   Rot("wtb", wtb), Rot("wtf", wtf), Rot("Sb", Sb))
        zrot = Rot("zp", [psb[0], psb[1]])
        arot = Rot("ap", [psb[2], psb[3]])
        orot = Rot("oT", [psb[4], psb[5]])
        prot = Rot("pp", [psb[6], psb[7]])
        for h in range(8):
            wq, wqk = load_wchunk(wb_ev_in, WK["ev_in"], 3072 + h * 128)
            wk_, wkk = load_wchunk(wb_ev_in, WK["ev_in"], 4096 + h * 128)
            wv, wvk = load_wchunk(wb_ev_in, WK["ev_in"], 5120 + h * 128)
            wga, wgak = load_wchunk(wb_ev_in, WK["ev_in"], 6144 + h * 128)
            for sb in range(8):
                sl = slice(sb * 512, (sb + 1) * 512)
                ps, pk = prot.next()
                proj_fm(ps, pk, wq, wqk, sb)
                S.op("act", lambda e, ps=ps, sl=sl: e.activation(out=qTb[:, sl], in_=ps[:, :], func=AF.Copy, scale=scale),
                     reads=[pk], pwrites=["qT"])
                ps, pk = prot.next()
                proj_fm(ps, pk, wk_, wkk, sb)
                S.op("dve", lambda e, ps=ps, sl=sl: e.tensor_copy(out=kTb[:, sl], in_=ps[:, :]),
                     reads=[pk], pwrites=["kT"])
                ps, pk = prot.next()
                proj_fm(ps, pk, wga, wgak, sb)
                S.op("act", lambda e, ps=ps, sl=sl: e.activation(out=gaT[:, sl], in_=ps[:, :], func=AF.Silu),
                     reads=[pk], pwrites=["gaT"])
                ps, pk = prot.next()
                for j in range(4):
                    tt = sb * 4 + j
                    for k in range(8):
                        S.op("pe", lambda e, ps=ps, j=j, k=k, tt=tt, wv=wv: e.matmul(
                            ps[:, j * 128:(j + 1) * 128], lhsT=hT[:, k, tt * 128:(tt + 1) * 128], rhs=wv[:, k, :],
                            start=(k == 0), stop=(k == 7)),
                            reads=[wvk, "hT"], writes=[pk] if (k == 0 and j == 0) else (),
                            pwrites=() if (k == 0 and j == 0) else [pk])
                S.op("dve", lambda e, ps=ps, sl=sl: e.tensor_copy(out=vtok[:, sl], in_=ps[:, :]),
                     reads=[pk], pwrites=["vtok"])
            yh = yhb[h % 2]
            yhk = ("yh", h % 2)
            def stageA(Q, b, first, last):
                qs, oT, ok = Q
                if first:
                    S.op("pe", lambda e, oT=oT: e.matmul(oT[:, :], lhsT=zeros_b, rhs=qTb[:, 0:512], start=True, stop=False),
                         reads=["cmb", "qT"], writes=[ok])
                    S.op("pool", lambda e: e.memset(Sf, 0.0), writes=["Sf"])
                c0 = max(0, 128 * (b - 4 * qs))
                W = 512 - c0
                diag = b >= 4 * qs
                q0 = qs * 512 + c0
                zp, zk = zrot.next()
                S.op("pe", lambda e, zp=zp, b=b, q0=q0, W=W: e.matmul(
                    zp[:, 0:W], lhsT=kTb[:, b * 128:(b + 1) * 128], rhs=qTb[:, q0:q0 + W], start=True, stop=True),
                    reads=["kT", "qT"], writes=[zk])
                eb, ek = erot.next()
                S.op("act", lambda e, eb=eb, zp=zp, W=W: e.activation(out=eb[:, 0:W], in_=zp[:, 0:W], func=AF.Exp),
                     reads=[zk], writes=[ek])
                sb_, sbk = sbrot.next()
                if diag:
                    sf_, sfk = sfrot.next()
                    S.op("act", lambda e, eb=eb, sf_=sf_, W=W: e.activation(
                        out=sf_[:, 0:W], in_=eb[:, 0:W], func=AF.Ln, bias=1.0), reads=[ek], writes=[sfk])
                    S.op("dve", lambda e, sf_=sf_, sb_=sb_, W=W: e.tensor_tensor(
                        out=sb_[:, 0:W], in0=sf_[:, 0:W], in1=mask0[:, 0:W], op=ALU.mult),
                        reads=[sfk, "mask0"], writes=[sbk])
                else:
                    S.op("act", lambda e, eb=eb, sb_=sb_, W=W: e.activation(
                        out=sb_[:, 0:W], in_=eb[:, 0:W], func=AF.Ln, bias=1.0), reads=[ek], writes=[sbk])
                Snext = None
                if not last:
                    S.op("dve", lambda e, sb_=sb_, c0=c0, W=W: e.tensor_tensor(
                        out=Sf[:, c0:512], in0=Sf[:, c0:512], in1=sb_[:, 0:W], op=ALU.add),
                        reads=[sbk], writes=["Sf"])
                    sbn, sbnk = Sbrot.next()
                    S.op("dve", lambda e, sbn=sbn: e.tensor_copy(out=sbn, in_=Sf), reads=["Sf"], writes=[sbnk])
                    Snext = (sbn, sbnk)
                return (b, c0, W, diag, q0, sb_, sbk, Snext)

            def stageB1(Q, st, Scur, first):
                b, c0, W, diag, q0, sb_, sbk, _ = st
                ap_, ak = arot.next()
                S.op("pe", lambda e, ap_=ap_, b=b, q0=q0, W=W: e.matmul(
                    ap_[:, 0:W], lhsT=kTb[:, b * 128:(b + 1) * 128], rhs=qTb[:, q0:q0 + W], start=True, stop=False),
                    reads=["kT", "qT"], writes=[ak])
                S.op("pe", lambda e, ap_=ap_, sb_=sb_, W=W, first=first: e.matmul(
                    ap_[:, 0:W], lhsT=negU_b, rhs=sb_[:, 0:W], start=False, stop=first),
                    reads=[sbk, "cmb"], pwrites=[ak])
                if not first:
                    S.op("pe", lambda e, ap_=ap_, Scur=Scur, c0=c0, W=W: e.matmul(
                        ap_[:, 0:W], lhsT=negones_b, rhs=Scur[0][:, c0:512], start=False, stop=True),
                        reads=[Scur[1], "cmb"], pwrites=[ak])
                wb_, wbk = wbrot.next()
                if diag:
                    wf_, wfk = wfrot.next()
                    S.op("act", lambda e, wf_=wf_, ap_=ap_, W=W: e.activation(out=wf_[:, 0:W], in_=ap_[:, 0:W], func=AF.Exp),
                         reads=[ak], writes=[wfk])
                    S.op("dve", lambda e, wf_=wf_, wb_=wb_, W=W: e.tensor_tensor(
                        out=wb_[:, 0:W], in0=wf_[:, 0:W], in1=mask0[:, 0:W], op=ALU.mult),
                        reads=[wfk, "mask0"], writes=[wbk])
                else:
                    S.op("act", lambda e, wb_=wb_, ap_=ap_, W=W: e.activation(out=wb_[:, 0:W], in_=ap_[:, 0:W], func=AF.Exp),
                         reads=[ak], writes=[wbk])
                return (Q, b, c0, W, wb_, wbk)

            def stageB2(w):
                (qs, oT, ok), b, c0, W, wb_, wbk = w
                S.op("pe", lambda e, oT=oT, b=b, wb_=wb_, c0=c0, W=W: e.matmul(
                    oT[:, c0:512], lhsT=vtok[:, b * 128:(b + 1) * 128], rhs=wb_[:, 0:W], start=False, stop=(b == 0)),
                    reads=["vtok", wbk], pwrites=[ok])
                if b == 0:
                    sl = slice(qs * 512, (qs + 1) * 512)
                    S.op("dve", lambda e, oT=oT, sl=sl, yh=yh: e.tensor_tensor(
                        out=yh[:, sl], in0=oT[:, :], in1=gaT[:, sl], op=ALU.mult),
                        reads=[ok, "gaT"], pwrites=[yhk])

            steps = []
            for qs in range(8):
                oT, ok = orot.next()
                Q = (qs, oT, ok)
                nb = 4 * qs + 4
                for n, b in enumerate(range(nb - 1, -1, -1)):
                    steps.append((Q, b, n == 0, n == nb - 1))
            nxt = stageA(*steps[0])
            Scur = None
            wprev = None
            for i, (Q, b, first, last) in enumerate(steps):
                cur = nxt
                if i + 1 < len(steps):
                    nxt = stageA(*steps[i + 1])
                wcur = stageB1(Q, cur, None if first else Scur, first)
                if wprev is not None:
                    stageB2(wprev)
                wprev = wcur
                Scur = cur[7]
            stageB2(wprev)
            S.dma("pool", y_s[8 + h], yh, reads=[yhk], pwrites=[("y_s", 8 + h)])

    def outproj(wb, wkey, res_src, res_key, dst, dst_key, final):
        S.barrier()
        cv = Carver()
        wout = cv.bf16(16 * D)
        ytl = [cv.bf16(16 * 128) for _ in range(2)]
        hob = [cv.f32(D) for _ in range(2)]
        xbufs = [cv.f32(D) for _ in range(2)]
        sqb = cv.f32(D)
        wout3 = wout.rearrange("p (c n) -> p c n", c=16)
        S.dma("sp", wout3, wb.rearrange("(c p) n -> p c n", p=128), reads=[wkey], writes=["wout"])
        yrot = Rot("ytl", ytl)
        hrot = Rot("hob", hob)
        xrot = Rot("xb", xbufs)
        prot = Rot("psO", [(psb[0], psb[1]), (psb[2], psb[3])])
        ykeys = [("y_s", c) for c in range(16)]
        for tt in range(32):
            yt, ytk = yrot.next()
            yt3 = yt.rearrange("p (c t) -> p c t", c=16)
            S.dma("sp", yt3, y_s.rearrange("c p t -> p c t")[:, :, tt * 128:(tt + 1) * 128], reads=ykeys, writes=[ytk])
            xb, xk = xrot.next()
            S.dma("sp", xb, res_src[tt * 128:(tt + 1) * 128, :], reads=[res_key], writes=[xk])
            (p0, p1), pk = prot.next()
            for n, pp in enumerate((p0, p1)):
                for c in range(16):
                    S.op("pe", lambda e, pp=pp, c=c, n=n, yt3=yt3: e.matmul(
                        pp[:, :], lhsT=yt3[:, c, :], rhs=wout3[:, c, n * 512:(n + 1) * 512], start=(c == 0), stop=(c == 15)),
                        reads=[ytk, "wout"], writes=[(pk, n)] if c == 0 else (), pwrites=() if c == 0 else [(pk, n)])
            ho, hk = hrot.next()
            for n, pp in enumerate((p0, p1)):
                S.op("dve", lambda e, pp=pp, n=n, ho=ho, xb=xb: e.tensor_tensor(
                    out=ho[:, n * 512:(n + 1) * 512], in0=pp[:, :], in1=xb[:, n * 512:(n + 1) * 512], op=ALU.add),
                    reads=[(pk, n), xk], pwrites=[hk])
            if final:
                st_, sk = strot.next()
                rstd_ops(st_, sk, ho, hk, sqb)
                S.op("dve", lambda e, ho=ho, st_=st_: e.scalar_tensor_tensor(
                    out=ho, in0=ho, scalar=st_[:, 3:4], in1=rp[:, RP_FNORM:RP_FNORM + D],
                    op0=ALU.mult, op1=ALU.mult), reads=[(sk, 3), "rp", hk], writes=[hk])
            S.dma("pool", dst[tt * 128:(tt + 1) * 128, :], ho, reads=[hk], pwrites=[dst_key])


    xt_s = dscr("xt_s", [T, 2048], F32)
    bt_s = dscr("bt_s", [T, 512], BF16)
    bc_s = dscr("bc_s", [8, 128, T], BF16)
    z_s = dscr("z_s", [T, 2048], F32)
    dtl_s = dscr("dtl_s", [T, 64], F32)
    arow = A("arow", [128, 32], F32)
    S.op("act", lambda e: e.activation(out=arow[:, :], in_=rp[:, RP_ALOG:RP_ALOG + 32], func=AF.Exp),
         reads=["rp"], writes=["arow"])
    ustrict = cm[:, 6, :]
    tri_f = cm[:, 5, :]

    def l1_inproj(seq):
        S.barrier()
        cv = Carver()
        xpad = [cv.f32(4100) for _ in range(2)]
        acc = [cv.f32(T) for _ in range(2)]
        xtb = cv.f32(T)
        xtb_bf = xtb[:, 0:2048].bitcast(BF16)
        for i in range(2):
            S.op("pool", lambda e, i=i: e.memset(xpad[i][:, 0:3], 0.0), writes=[("xpadz", i)])
        prot = Rot("psL", [psb[0], psb[1], psb[6], psb[7]])
        trot = Rot("psX", [psb[2], psb[3], psb[4], psb[5]])
        def l1_proj(cc):
                w, wk = load_wchunk(wb_od_in, WK["od_in"], 2048 + cc * 128)
                xp = xpad[cc % 2]
                xpk = ("xpad", cc % 2)
                ac = acc[cc % 2]
                ack = ("acc1", cc % 2)
                for sb in range(8):
                    ps, pk = prot.next()
                    proj_fm(ps, pk, w, wk, sb)
                    S.op("act", lambda e, ps=ps, xp=xp, sb=sb: e.activation(
                        out=xp[:, 3 + sb * 512:3 + (sb + 1) * 512], in_=ps[:, :], func=AF.Copy),
                        reads=[pk, ("xpadz", cc % 2)], pwrites=[xpk])
                return xp, xpk, ac, ack

        def l1_post(cc, xp, xpk, ac, ack):
                for k in range(4):
                    col = CP_CW + cc * 4 + k
                    if k == 0:
                        S.op("dve", lambda e, xp=xp, ac=ac, col=col, cc=cc: e.tensor_scalar(
                            out=ac, in0=xp[:, 0:T], scalar1=cp[:, col:col + 1],
                            scalar2=cp[:, CP_CB + cc:CP_CB + cc + 1], op0=ALU.mult, op1=ALU.add),
                            reads=[xpk, "cp"], writes=[ack])
                    else:
                        S.op("dve", lambda e, xp=xp, ac=ac, col=col, k=k: e.scalar_tensor_tensor(
                            out=ac, in0=xp[:, k:k + T], scalar=cp[:, col:col + 1], in1=ac,
                            op0=ALU.mult, op1=ALU.add), reads=[xpk, "cp"], writes=[ack])
                S.op("act", lambda e, ac=ac: e.activation(out=ac, in_=ac, func=AF.Silu), reads=[ack], writes=[ack])
                if cc >= 16:
                    S.dma("pool", bc_s[cc - 16], ac, reads=[ack], pwrites=[("bc_s", cc - 16)])
                if cc < 20:
                    isb = cc >= 16
                    stg = xtb_bf if isb else xtb
                    for q in range(8):
                        pt, ptk = trot.next()
                        for j in range(4):
                            tt = q * 4 + j
                            S.op("pe", lambda e, pt=pt, j=j, tt=tt, ac=ac: e.transpose(
                                out=pt[:, j * 128:(j + 1) * 128], in_=ac[:, tt * 128:(tt + 1) * 128], identity=ident),
                                reads=[ack, "cm"], pwrites=[ptk])
                        if q % 2 == 0:
                            S.op("dve", lambda e, pt=pt, q=q, stg=stg: e.tensor_copy(
                                out=stg[:, q * 512:(q + 1) * 512], in_=pt[:, :]), reads=[ptk], pwrites=["xtb"])
                        else:
                            S.op("act", lambda e, pt=pt, q=q, stg=stg: e.activation(
                                out=stg[:, q * 512:(q + 1) * 512], in_=pt[:, :], func=AF.Copy), reads=[ptk], pwrites=["xtb"])
                    if isb:
                        S.dma("pool", bt_s.rearrange("(tt p) c -> p tt c", p=128)[:, :, (cc - 16) * 128:(cc - 15) * 128],
                              stg.rearrange("p (tt c) -> p tt c", c=128), reads=["xtb"], pwrites=["bt_s"])
                    else:
                        S.dma("pool", xt_s.rearrange("(tt p) c -> p tt c", p=128)[:, :, cc * 128:(cc + 1) * 128],
                              stg.rearrange("p (tt c) -> p tt c", c=128), reads=["xtb"], pwrites=["xt_s"])

        nxt_ = l1_proj(0)
        for cc in range(24):
            cur_ = nxt_
            if cc + 1 < 24:
                nxt_ = l1_proj(cc + 1)
            l1_post(cc, *cur_)
        S.barrier()
        cv = Carver()
        wz = cv.bf16(8 * 2048)
        wz3 = wz.rearrange("p (k n) -> p k n", k=8)
        wdt = cv.bf16(8 * 32)
        wdt3 = wdt.rearrange("p (k n) -> p k n", k=8)
        ztl = [cv.f32(2048) for _ in range(2)]
        dtl = [cv.f32(64) for _ in range(2)]
        tmpd = [cv.f32(32) for _ in range(2)]
        S.dma("sp", wz3, wb_od_in.rearrange("(k p) n -> p k n", p=128)[:, :, 0:2048], reads=[WK["od_in"]], writes=["wz"])
        S.dma("sp", wdt3, wb_od_in.rearrange("(k p) n -> p k n", p=128)[:, :, 5120:5152], reads=[WK["od_in"]], writes=["wdt"])
        zrot = Rot("ztl", ztl)
        drot = Rot("dtl", dtl)
        tdrot = Rot("tmpd", tmpd)
        prot = Rot("psZ", [psb[0], psb[1], psb[2], psb[3], psb[4], psb[5]])
        dprot = Rot("psD", [psb[6], psb[7]])
        for tt in range(32):
            zt, ztk = zrot.next()
            tsl = slice(tt * 128, (tt + 1) * 128)
            for g in range(4):
                ps, pk = prot.next()
                for k in range(8):
                    S.op("pe", lambda e, ps=ps, k=k, g=g, tsl=tsl: e.matmul(
                        ps[:, :], lhsT=hT[:, k, tsl], rhs=wz3[:, k, g * 512:(g + 1) * 512], start=(k == 0), stop=(k == 7)),
                        reads=["wz", "hT"], writes=[pk] if k == 0 else (), pwrites=() if k == 0 else [pk])
                S.op("act", lambda e, ps=ps, zt=zt, g=g: e.activation(out=zt[:, g * 512:(g + 1) * 512], in_=ps[:, :], func=AF.Silu),
                     reads=[pk], pwrites=[ztk])
            S.dma("pool", z_s[tsl, :], zt, reads=[ztk], pwrites=["z_s"])
            ps, pk = dprot.next()
            for k in range(8):
                S.op("pe", lambda e, ps=ps, k=k, tsl=tsl: e.matmul(
                    ps[:, 0:32], lhsT=hT[:, k, tsl], rhs=wdt3[:, k, :], start=(k == 0), stop=(k == 7)),
                    reads=["wdt", "hT"], writes=[pk] if k == 0 else (), pwrites=() if k == 0 else [pk])
            dl, dlk = drot.next()
            td, tdk = tdrot.next()
            S.op("dve", lambda e, ps=ps, td=td: e.tensor_tensor(out=td, in0=ps[:, 0:32], in1=rp[:, RP_DTB:RP_DTB + 32], op=ALU.add),
                 reads=[pk, "rp"], writes=[tdk])
            S.op("act", lambda e, td=td: e.activation(out=td, in_=td, func=AF.Exp), reads=[tdk], writes=[tdk])
            S.op("act", lambda e, td=td, dl=dl: e.activation(out=dl[:, 0:32], in_=td, func=AF.Ln, bias=1.0),
                 reads=[tdk], pwrites=[dlk])
            S.op("dve", lambda e, dl=dl: e.scalar_tensor_tensor(out=dl[:, 32:64], in0=dl[:, 0:32], scalar=-1.0, in1=arow[:, :],
                                                                op0=ALU.mult, op1=ALU.mult),
                 reads=[dlk, "arow"], pwrites=[dlk])
            S.dma("pool", dtl_s[tsl, :], dl, reads=[dlk], pwrites=["dtl_s"])

    def l1_ssd(seq):
        S.barrier()
        cv = Carver()
        state = cv.f32(2048)
        state_bf = cv.bf16(2048)
        xtl = [cv.f32(2048) for _ in range(1)]
        ztl = [cv.f32(2048) for _ in range(1)]
        ytoks = [cv.f32(2048) for _ in range(2)]
        tmpys = [cv.f32(512) for _ in range(2)]
        sqs = cv.bf16(2048)
        ybf = cv.bf16(2048)
        xs = cv.bf16(2048)
        xst = cv.bf16(2048)
        dxs = cv.f32(2048)
        btl = [cv.bf16(512) for _ in range(2)]
        bctl = [cv.bf16(1024) for _ in range(2)]
        dtll = [cv.f32(64) for _ in range(2)]
        ecs = cv.f32(32)
        etot = cv.f32(32)
        cbm = cv.bf16(512)
        ULb = [cv.f32(1024) for _ in range(2)]
        Eb = [cv.bf16(1024) for _ in range(2)]
        MTb = [cv.bf16(1024) for _ in range(2)]
        gsts = [cv.f32(8) for _ in range(2)]
        yTc = [cv.bf16(2048) for _ in range(1)]
        S.op("pool", lambda e: e.memset(state, 0.0), writes=[("state", g_) for g_ in range(4)])
        S.op("pool", lambda e: e.memset(state_bf, 0.0), writes=[("state_bf", g_) for g_ in range(4)])
        xrot, zrot, brot, bcrot, dlrot = Rot("xtl", xtl), Rot("ztl1", ztl), Rot("btl", btl), Rot("bctl", bctl), Rot("dtll", dtll)
        ulrot, erot, mrot, ytrot = Rot("UL", ULb), Rot("E", Eb), Rot("MT", MTb), Rot("yTc", yTc)
        B_D = [psb[0], psb[1]]
        drot = Rot("psDD", [(psb[0], psb[1]), (psb[2], psb[3])])
        def tail_piece(P, piece):
            c_, ytok, ytk, gst, gk = P["c"], P["ytok"], P["ytk"], P["gst"], P["gk"]
            if piece == 0:
                zt, ztk = P["zt"], P["ztk"]
                S.op("pool", lambda e, ytok=ytok: e.tensor_tensor(out=ytok, in0=ytok, in1=dxs, op=ALU.add), reads=["dxs"], writes=[ytk])
                S.op("pool", lambda e, ytok=ytok, zt=zt: e.tensor_tensor(out=ytok, in0=ytok, in1=zt, op=ALU.mult), reads=[ztk], writes=[ytk])
                S.op("act", lambda e, ytok=ytok: e.activation(out=sqs, in_=ytok, func=AF.Square), reads=[ytk], writes=["sqs"])
            elif piece == 1:
                S.op("dve", lambda e, gst=gst: e.tensor_reduce(out=gst[:, 0:4], in_=sqs.rearrange("p (g d) -> p g d", g=4), axis=AX.X, op=ALU.add),
                     reads=["sqs"], writes=[gk])
                S.op("dve", lambda e, gst=gst: e.tensor_scalar(out=gst[:, 0:4], in0=gst[:, 0:4], scalar1=1.0 / 512, scalar2=EPS, op0=ALU.mult, op1=ALU.add),
                     reads=[gk], writes=[gk])
                S.op("act", lambda e, gst=gst: e.activation(out=gst[:, 0:4], in_=gst[:, 0:4], func=AF.Sqrt), reads=[gk], writes=[gk])
            elif piece == 2:
                S.op("dve", lambda e, gst=gst: e.reciprocal(out=gst[:, 4:8], in_=gst[:, 0:4]), reads=[gk], writes=[gk])
                for g in range(4):
                    gs = slice(g * 512, (g + 1) * 512)
                    S.op("dve", lambda e, g=g, gs=gs, gst=gst, ytok=ytok: e.scalar_tensor_tensor(
                        out=ybf[:, gs], in0=ytok[:, gs], scalar=gst[:, 4 + g:5 + g], in1=rp[:, RP_GNORM + g * 512:RP_GNORM + (g + 1) * 512],
                        op0=ALU.mult, op1=ALU.mult), reads=[gk, "rp", ytk], writes=["ybf"] if g == 0 else (), pwrites=() if g == 0 else ["ybf"])
            else:
                yt_, ytk_ = ytrot.next()
                pbk = ("pb", 7)
                for q in range(4):
                    pbv = psb[7][:, 0:256].bitcast(BF16)
                    for j in range(4):
                        cch_ = q * 4 + j
                        S.op("pe", lambda e, pbv=pbv, j=j, cch_=cch_: e.transpose(
                            out=pbv[:, j * 128:(j + 1) * 128], in_=ybf[:, cch_ * 128:(cch_ + 1) * 128], identity=cmb[:, 0, :]),
                            reads=["ybf", "cmb"], writes=[pbk] if j == 0 else (), pwrites=() if j == 0 else [pbk])
                    S.op("act" if q % 2 else "dve",
                         (lambda e, pbv=pbv, q=q, yt_=yt_: e.activation(out=yt_[:, q * 512:(q + 1) * 512], in_=pbv, func=AF.Copy)) if q % 2 else
                         (lambda e, pbv=pbv, q=q, yt_=yt_: e.tensor_copy(out=yt_[:, q * 512:(q + 1) * 512], in_=pbv)),
                         reads=[pbk], pwrites=[ytk_])
                S.dma("pool", y_s.rearrange("c p t -> p c t")[:, :, c_ * 128:(c_ + 1) * 128], yt_.rearrange("p (c t) -> p c t", c=16),
                      reads=[ytk_], pwrites=[("y_s", cc_) for cc_ in range(16)])

        pending = None
        for c in range(32):
            tsl = slice(c * 128, (c + 1) * 128)
            ytok = ytoks[c % 2]
            ytk = ("ytok", c % 2)
            xt, xtk = xrot.next()
            bt, btk = brot.next()
            bct, bctk = bcrot.next()
            dl, dlk = dlrot.next()
            bct3 = bct.rearrange("p (g t) -> p g t", g=8)
            S.dma("sp", xt, xt_s[tsl, :], reads=["xt_s"], writes=[xtk])
            if c >= 1:
                zt, ztk = zrot.next()
                S.dma("sp", zt, z_s[(c - 1) * 128:c * 128, :], reads=["z_s"], writes=[ztk])
                pending["zt"], pending["ztk"] = zt, ztk
            S.dma("sp", bt, bt_s[tsl, :], reads=["bt_s"], writes=[btk])
            S.dma("sp", bct3, bc_s.rearrange("g p t -> p g t")[:, :, tsl], reads=[("bc_s", g) for g in range(8)], writes=[bctk])
            S.dma("sp", dl, dtl_s[tsl, :], reads=["dtl_s"], writes=[dlk])
            S.op("dve", lambda e, xt=xt, dl=dl: e.tensor_tensor(
                out=xs.rearrange("p (h d) -> p h d", h=32), in0=xt.rearrange("p (h d) -> p h d", h=32),
                in1=dl[:, 0:32].unsqueeze(2).broadcast_to([128, 32, 64]), op=ALU.mult),
                reads=[xtk, dlk], writes=["xs"])
            S.op("pe", lambda e, dl=dl: e.matmul(psb[5][:, 0:32], lhsT=ones_f, rhs=dl[:, 32:64], start=True, stop=True),
                 reads=[dlk, "cm"], writes=[("pb", 5)])
            S.op("pe", lambda e, dl=dl: e.matmul(psb[5][:, 32:64], lhsT=tri_f, rhs=dl[:, 32:64], start=True, stop=True),
                 reads=[dlk, "cm"], pwrites=[("pb", 5)])
            S.op("act", lambda e: e.activation(out=etot, in_=psb[5][:, 0:32], func=AF.Exp), reads=[("pb", 5)], writes=["etot"])
            S.op("act", lambda e: e.activation(out=ecs, in_=psb[5][:, 32:64], func=AF.Exp), reads=[("pb", 5)], writes=["ecs"])
            for g in range(4):
                S.op("pe", lambda e, g=g, bct3=bct3: e.matmul(psb[7][:, g * 128:(g + 1) * 128], lhsT=bct3[:, g, :], rhs=bct3[:, 4 + g, :],
                                                              start=True, stop=True),
                     reads=[bctk], writes=[("pb", 7)] if g == 0 else (), pwrites=() if g == 0 else [("pb", 7)])
            S.op("dve", lambda e: e.tensor_tensor(
                out=cbm.rearrange("p (g l) -> p g l", g=4), in0=psb[7][:, :].rearrange("p (g l) -> p g l", g=4),
                in1=tri_f.unsqueeze(1).broadcast_to([128, 4, 128]), op=ALU.mult),
                reads=[("pb", 7), "cm"], writes=["cbm"])
            def stageA(g):
                hsl = slice(g * 8, (g + 1) * 8)
                ul, ulk = ulrot.next()
                ul3 = ul.rearrange("p (h s) -> p h s", h=8)
                S.op("pool", lambda e, ul3=ul3, dl=dl, g=g: e.tensor_tensor(
                    out=ul3, in0=ustrict.unsqueeze(1).broadcast_to([128, 8, 128]),
                    in1=dl[:, 32 + g * 8:40 + g * 8].unsqueeze(2).broadcast_to([128, 8, 128]), op=ALU.mult),
                    reads=[dlk, "cm"], writes=[ulk])
                (d0, d1), dkk = drot.next()
                for hh in range(8):
                    dp = d0 if hh < 4 else d1
                    S.op("pe", lambda e, dp=dp, ul3=ul3, hh=hh: e.matmul(
                        dp[:, (hh % 4) * 128:(hh % 4 + 1) * 128], lhsT=ul3[:, hh, :], rhs=tri_f, start=True, stop=True),
                        reads=[ulk, "cm"], pwrites=[(dkk, hh // 4)])
                E, Ek = erot.next()
                S.op("act", lambda e, E=E, d0=d0: e.activation(out=E[:, 0:512], in_=d0[:, :], func=AF.Exp),
                     reads=[(dkk, 0)], pwrites=[Ek])
                S.op("act", lambda e, E=E, d1=d1: e.activation(out=E[:, 512:1024], in_=d1[:, :], func=AF.Exp),
                     reads=[(dkk, 1)], pwrites=[Ek])
                return E, Ek

            def stageB(g, E, Ek):
                hsl = slice(g * 8, (g + 1) * 8)
                gs = slice(g * 512, (g + 1) * 512)
                E3 = E.rearrange("p (h l) -> p h l", h=8)
                MT, MTk = mrot.next()
                MT3 = MT.rearrange("p (h l) -> p h l", h=8)
                S.op("dve", lambda e, E3=E3, MT3=MT3, g=g: e.tensor_tensor(
                    out=MT3, in0=E3, in1=cbm[:, g * 128:(g + 1) * 128].unsqueeze(1).broadcast_to([128, 8, 128]), op=ALU.mult),
                    reads=[Ek, "cbm"], writes=[MTk])
                S.op("dve", lambda e, E3=E3, gs=gs: e.tensor_tensor(
                    out=xst[:, gs].rearrange("p (h d) -> p h d", h=8), in0=xs[:, gs].rearrange("p (h d) -> p h d", h=8),
                    in1=E3[:, :, 127:128].broadcast_to([128, 8, 64]), op=ALU.mult),
                    reads=[Ek, "xs"], writes=[("xst", g % 2)])
                ib, ibk = psb[4], ("pb", 4)
                nb_, nbk = psb[5], ("pb", 5)
                sn, snk = psb[6], ("pb", 6)
                for hh in range(8):
                    hs = slice(g * 512 + hh * 64, g * 512 + (hh + 1) * 64)
                    hhs = slice(hh * 64, (hh + 1) * 64)
                    S.op("pe", lambda e, MT3=MT3, hh=hh, hs=hs, hhs=hhs: e.matmul(
                        ib[:, hhs], lhsT=MT3[:, hh, :], rhs=xs[:, hs], start=True, stop=True),
                        reads=[MTk, "xs"], writes=[ibk] if hh == 0 else (), pwrites=() if hh == 0 else [ibk])
                S.op("pe", lambda e, g=g, gs=gs, bct3=bct3: e.matmul(nb_[:, :], lhsT=bct3[:, 4 + g, :], rhs=state_bf[:, gs], start=True, stop=True),
                     reads=[bctk, ("state_bf", g)], writes=[nbk])
                S.op("pe", lambda e, g=g, gs=gs, bt=bt: e.matmul(sn[:, :], lhsT=bt[:, g * 128:(g + 1) * 128], rhs=xst[:, gs], start=True, stop=True),
                     reads=[btk, ("xst", g % 2)], writes=[snk])
                tmpy, tmk = tmpys[g % 2], ("tmpy", g % 2)
                S.op("dve", lambda e, tmpy=tmpy, hsl=hsl: e.tensor_tensor(
                    out=tmpy.rearrange("p (h d) -> p h d", h=8), in0=nb_[:, :].rearrange("p (h d) -> p h d", h=8),
                    in1=ecs[:, hsl].unsqueeze(2).broadcast_to([128, 8, 64]), op=ALU.mult),
                    reads=[nbk, "ecs"], writes=[tmk])
                S.op("dve", lambda e, gs=gs, tmpy=tmpy, ytok=ytok: e.tensor_tensor(out=ytok[:, gs], in0=ib[:, :], in1=tmpy, op=ALU.add),
                     reads=[ibk, tmk], pwrites=[ytk])
                S.op("pool", lambda e, gs=gs, hsl=hsl: e.tensor_tensor(
                    out=state[:, gs].rearrange("p (h d) -> p h d", h=8), in0=state[:, gs].rearrange("p (h d) -> p h d", h=8),
                    in1=etot[:, hsl].unsqueeze(2).broadcast_to([128, 8, 64]), op=ALU.mult),
                    reads=["etot"], writes=[("state", g)])
                S.op("dve", lambda e, gs=gs: e.tensor_tensor(out=state[:, gs], in0=sn[:, :], in1=state[:, gs], op=ALU.add),
                     reads=[snk], writes=[("state", g)])
                S.op("act", lambda e, gs=gs: e.activation(out=state_bf[:, gs], in_=state[:, gs], func=AF.Copy),
                     reads=[("state", g)], writes=[("state_bf", g)])

            nxt = stageA(0)
            for g in range(4):
                cur = nxt
                if g < 3:
                    nxt = stageA(g + 1)
                if g == 1:
                    S.op("pool", lambda e, xt=xt: e.tensor_tensor(
                        out=dxs.rearrange("p (h d) -> p h d", h=32), in0=xt.rearrange("p (h d) -> p h d", h=32),
                        in1=rp[:, RP_DSK:RP_DSK + 32].unsqueeze(2).broadcast_to([128, 32, 64]), op=ALU.mult),
                        reads=[xtk, "rp"], writes=["dxs"])
                stageB(g, *cur)
                if pending is not None:
                    tail_piece(pending, g)
            pending = {"c": c, "ytok": ytok, "ytk": ytk, "gst": gsts[c % 2], "gk": ("gst", c % 2)}
        zt, ztk = zrot.next()
        S.dma("sp", zt, z_s[31 * 128:32 * 128, :], reads=["z_s"], writes=[ztk])
        pending["zt"], pending["ztk"] = zt, ztk
        for g in range(4):
            tail_piece(pending, g)

    def layer1(seq):
        rms_to_hT(h1_s, "h1_s", CP_ODNORM)
        l1_inproj(seq)
        l1_ssd(seq)
        outproj(wb_od_out, WK["od_out"], h1_s, "h1_s", out_d[seq], "out", final=True)

    dbg = None
    if stop_after in ("hT", "yconv", "attn"):
        dbg = nc.dram_tensor("dbg", [16, 128, T], BF16, kind="ExternalOutput").ap()
    if stop_after == "conv":
        dbg = nc.dram_tensor("dbg", [8, 128, T], F32, kind="ExternalOutput").ap()

    for seq in range(nseq):
        x_seq = x_d[seq]
        rms_to_hT(x_seq, "x_in", CP_EVNORM)
        if stop_after == "hT":
            for k in range(8):
                S.dma("sp", dbg[k], hT[:, k, :], reads=["hT"], pwrites=["dbg"])
            break
        l0_conv(seq)
        if stop_after == "conv":
            S.dma("sp", dbg, conv_s, reads=[("conv_s", c) for c in range(8)], pwrites=["dbg"])
            break
        l0_ln(seq)
        if stop_after == "yconv":
            S.dma("sp", dbg[0:8], y_s[0:8], reads=[("y_s", c) for c in range(8)], pwrites=["dbg"])
            break
        l0_attn(seq)
        if stop_after == "attn":
            S.dma("sp", dbg, y_s, reads=[("y_s", c) for c in range(16)], pwrites=["dbg"])
            break
        if stop_after == "l0":
            outproj(wb_ev_out, WK["ev_out"], x_seq, "x_in", out_d[seq], "out", final=False)
            continue
        outproj(wb_ev_out, WK["ev_out"], x_seq, "x_in", h1_s, "h1_s", final=False)
        layer1(seq)
    S.barrier()
    S.emit()
    return nc


def host_consts(inp):
    f = np.float32
    cp = np.zeros((128, CP_N), f)

    def colv(v):
        return np.ascontiguousarray(np.asarray(v, f).reshape(-1, 128).T)
    cp[:, CP_EVNORM:CP_EVNORM + 8] = colv(inp["ev_norm_w"][0])
    dw = np.asarray(inp["ev_dw_w"][0], f)
    cp[:, CP_DWW:CP_DWW + 248] = dw.reshape(31, 8, 128).transpose(2, 1, 0).reshape(128, 248)
    cp[:, CP_DWB:CP_DWB + 8] = colv(inp["ev_dw_b"][0])
    cp[:, CP_LNW:CP_LNW + 8] = colv(inp["ev_ln_w"][0])
    cp[:, CP_LNB:CP_LNB + 8] = colv(inp["ev_ln_b"][0])
    cp[:, CP_ODNORM:CP_ODNORM + 8] = colv(inp["od_norm_w"][0])
    cw = np.asarray(inp["od_conv_w"][0], f)
    cp[:, CP_CW:CP_CW + 96] = cw.reshape(4, 24, 128).transpose(2, 1, 0).reshape(128, 96)
    cp[:, CP_CB:CP_CB + 24] = colv(inp["od_conv_b"][0])
    rp = np.zeros((128, RP_N), f)
    rp[:, RP_FNORM:RP_FNORM + 1024] = np.asarray(inp["final_norm_w"], f)[None, :]
    rp[:, RP_GNORM:RP_GNORM + 2048] = np.asarray(inp["od_gnorm_w"][0], f)[None, :]
    rp[:, RP_DSK:RP_DSK + 32] = np.asarray(inp["od_d"][0], f)[None, :]
    rp[:, RP_DTB:RP_DTB + 32] = np.asarray(inp["od_dt_bias"][0], f)[None, :]
    rp[:, RP_ALOG:RP_ALOG + 32] = np.asarray(inp["od_a_log"][0], f)[None, :]
    j = np.arange(128)
    cm = np.zeros((128, 7, 128), f)
    cm[:, 0, :] = np.eye(128, dtype=f)
    cm[:, 1, :] = 1.0
    cm[:, 2, :] = -(j[:, None] >= j[None, :]).astype(f)
    cm[:, 3, :] = 0.0
    cm[:, 4, :] = -1.0
    cm[:, 5, :] = (j[:, None] <= j[None, :]).astype(f)
    cm[:, 6, :] = (j[:, None] > j[None, :]).astype(f)
    mask0 = (j[:, None] < np.arange(512)[None, :]).astype(f)
    return cp, rp, cm, mask0


def make_in_maps(inputs, nseq=2, ncores=NCORES):
    cp, rp, cm, mask0 = host_consts(inputs)
    x = np.asarray(inputs["x"], np.float32)
    common = {
        "ev_w_in": np.ascontiguousarray(np.asarray(inputs["ev_w_in"], np.float32)[0]),
        "ev_w_out": np.ascontiguousarray(np.asarray(inputs["ev_w_out"], np.float32)[0]),
        "od_w_in": np.ascontiguousarray(np.asarray(inputs["od_w_in"], np.float32)[0]),
        "od_w_out": np.ascontiguousarray(np.asarray(inputs["od_w_out"], np.float32)[0]),
        "colpack": cp, "rowpack": rp, "cmats": cm, "mask0": mask0,
    }
    maps = []
    for c in range(ncores):
        m = dict(common)
        m["x"] = np.ascontiguousarray(x[c * nseq:(c + 1) * nseq])
        maps.append(m)
    return maps


def kernel(**inputs):
    nc = build(nseq=2)
    maps = make_in_maps(inputs, 2, NCORES)
    res = run_bass_kernel_spmd(nc, maps, core_ids=list(range(NCORES)))
    outs = [np.asarray(r["out"], np.float32) for r in res.results]
    return np.concatenate(outs, axis=0)
```

```python
import numpy as np
from contextlib import ExitStack
import concourse.bass as bass
import concourse.mybir as mybir
from concourse.bass_utils import run_bass_kernel_spmd

F32 = mybir.dt.float32
BF16 = mybir.dt.bfloat16
AF = mybir.ActivationFunctionType
ALU = mybir.AluOpType
AX = mybir.AxisListType

T = 4096
D = 1024
NCORES = 8
EPS = 1e-6
IN_EVEN = 7168
IN_ODD = 5152


class _Op:
    __slots__ = ("eng", "fn", "deps", "sig", "sigval", "isdma", "dma_i", "idx")


class Sched:
    ENGS = ("sp", "act", "dve", "pool", "pe")

    def __init__(self, nc, ring=8):
        self.nc = nc
        self.ops = {e: [] for e in self.ENGS}
        self.res = {}
        self.ndma = {e: 0 for e in self.ENGS}
        self.ring = ring
        self.pend = {e: [] for e in self.ENGS}
        self.alldma = []

    def barrier(self):
        lst = []
        for e in self.ENGS:
            for op in reversed(self.ops[e]):
                if not op.isdma:
                    lst.append(op)
                    break
        lst.extend(self.alldma)
        self.alldma = []
        for e in self.ENGS:
            self.pend[e] = list(lst)

    def _add(self, eng, fn, reads, writes, pwrites, isdma):
        op = _Op()
        op.eng, op.fn, op.isdma, op.sig, op.sigval, op.dma_i = eng, fn, isdma, False, 0, -1
        op.idx = len(self.ops[eng])
        deps = {}
        for k in reads:
            r = self.res.get(k)
            if r:
                for w in r["w"]:
                    deps[id(w)] = w
        for k in list(writes) + list(pwrites):
            r = self.res.get(k)
            if r:
                for w in r["w"]:
                    deps[id(w)] = w
                for w in r["rc"].values():
                    deps[id(w)] = w
                for w in r["rd"]:
                    deps[id(w)] = w
        for w in self.pend[eng]:
            deps[id(w)] = w
        self.pend[eng] = []
        deps.pop(id(op), None)
        best = {}
        keep = []
        for d in deps.values():
            if d.isdma:
                keep.append(d)
            elif d.eng not in best or d.idx > best[d.eng].idx:
                best[d.eng] = d
        op.deps = keep + list(best.values())
        for d in op.deps:
            if (not d.isdma) and not (d.eng == eng == "pe"):
                d.sig = True
        for k in reads:
            r = self.res.setdefault(k, {"w": [], "rc": {}, "rd": []})
            if isdma:
                r["rd"].append(op)
            else:
                r["rc"][eng] = op
        for k in writes:
            self.res[k] = {"w": [op], "rc": {}, "rd": []}
        for k in pwrites:
            r = self.res.get(k)
            if r is None or r["rc"] or r["rd"]:
                self.res[k] = {"w": [op], "rc": {}, "rd": []}
            else:
                r["w"].append(op)
        if isdma:
            op.dma_i = self.ndma[eng]
            self.ndma[eng] += 1
            self.alldma.append(op)
        self.ops[eng].append(op)
        return op

    def op(self, eng, fn, reads=(), writes=(), pwrites=()):
        return self._add(eng, fn, reads, writes, pwrites, False)

    def dma(self, eng, out, in_, reads=(), writes=(), pwrites=()):
        return self._add(eng, lambda e: e.dma_start(out=out, in_=in_), reads, writes, pwrites, True)

    def emit(self):
        nc = self.nc
        for e in self.ENGS:
            c = 0
            for op in self.ops[e]:
                if (not op.isdma) and op.sig:
                    c += 1
                    op.sigval = c
        with ExitStack() as st:
            psem = {e: st.enter_context(nc.semaphore("ps_" + e)) for e in self.ENGS}
            rings = {}
            for e in self.ENGS:
                if self.ndma[e]:
                    rings[e] = [st.enter_context(nc.semaphore("dr_%s_%d" % (e, i))) for i in range(self.ring)]
            block = st.enter_context(nc.Block())
            R = self.ring

            def tok(d):
                if d.isdma:
                    return rings[d.eng][d.dma_i % R], 16 * (d.dma_i // R + 1)
                return psem[d.eng], d.sigval

            def run(ename, eng):
                waited = {}

                def wait(sem, val):
                    key = id(sem)
                    if waited.get(key, 0) >= val:
                        return
                    eng.wait_ge(sem, val)
                    waited[key] = val

                for op in self.ops[ename]:
                    for d in op.deps:
                        if (not d.isdma) and d.eng == ename and ename == "pe":
                            continue
                        s, v = tok(d)
                        wait(s, v)
                    if op.isdma:
                        if op.dma_i >= R:
                            wait(rings[ename][op.dma_i % R], 16 * (op.dma_i // R))
                        ins = op.fn(eng)
                        ins.then_inc(rings[ename][op.dma_i % R], 16)
                    else:
                        ins = op.fn(eng)
                        if op.sig:
                            ins.then_inc(psem[ename], 1)
                n = self.ndma[ename]
                if n:
                    for j in range(R):
                        cnt = (n - j + R - 1) // R if n > j else 0
                        if cnt:
                            wait(rings[ename][j], 16 * cnt)

            @block.sync
            def _(e):
                run("sp", e)

            @block.scalar
            def _(e):
                run("act", e)

            @block.vector
            def _(e):
                run("dve", e)

            @block.gpsimd
            def _(e):
                run("pool", e)

            @block.tensor
            def _(e):
                run("pe", e)


class Rot:
    def __init__(self, name, bufs):
        self.name, self.bufs, self.i = name, bufs, -1

    def next(self):
        self.i += 1
        j = self.i % len(self.bufs)
        return self.bufs[j], (self.name, j)


CP_EVNORM = 0
CP_DWW = 8
CP_DWB = CP_DWW + 248
CP_LNW = CP_DWB + 8
CP_LNB = CP_LNW + 8
CP_ODNORM = CP_LNB + 8
CP_CW = CP_ODNORM + 8
CP_CB = CP_CW + 96
CP_N = CP_CB + 24
RP_FNORM = 0
RP_GNORM = 1024
RP_DSK = 3072
RP_DTB = 3104
RP_ALOG = 3136
RP_N = 3168


ARENA_WORDS = 27616


def build(nseq=2, stop_after=None):
    nc = bass.Bass("TRN2", target_bir_lowering=False)
    S = Sched(nc)

    def din(name, shape, dt=F32):
        return nc.dram_tensor(name, list(shape), dt, kind="ExternalInput").ap()

    def dscr(name, shape, dt):
        return nc.dram_tensor(name, list(shape), dt, kind="Internal").ap()

    x_d = din("x", [nseq, T, D])
    ev_w_in = din("ev_w_in", [D, IN_EVEN])
    ev_w_out = din("ev_w_out", [2048, D])
    od_w_in = din("od_w_in", [D, IN_ODD])
    od_w_out = din("od_w_out", [2048, D])
    colpack = din("colpack", [128, CP_N])
    rowpack = din("rowpack", [128, RP_N])
    cmats = din("cmats", [128, 7, 128])
    mask0_d = din("mask0", [128, 512])
    out_d = nc.dram_tensor("out", [nseq, T, D], F32, kind="ExternalOutput").ap()

    wb_ev_in = dscr("wb_ev_in", [D, IN_EVEN], BF16)
    wb_ev_out = dscr("wb_ev_out", [2048, D], BF16)
    wb_od_in = dscr("wb_od_in", [D, IN_ODD], BF16)
    wb_od_out = dscr("wb_od_out", [2048, D], BF16)
    conv_s = dscr("conv_s", [8, 128, T], F32)
    y_s = dscr("y_s", [16, 128, T], BF16)
    h1_s = dscr("h1_s", [T, D], F32)

    A = nc.alloc_sbuf_tensor
    PS = nc.alloc_psum_tensor

    cp = A("cp", [128, CP_N], F32)
    rp = A("rp", [128, RP_N], F32)
    cm = A("cm", [128, 7, 128], F32)
    cmb = A("cmb", [128, 7, 128], BF16)
    mask0 = A("mask0s", [128, 512], F32)
    hT = A("hT", [128, 8, T], BF16)
    wch = [A("wch%d" % i, [128, 8, 128], BF16) for i in range(6)]
    stat = [A("st%d" % i, [128, 4], F32) for i in range(2)]
    arena = A("arena", [128, ARENA_WORDS], F32)
    psb = [PS("psb%d" % i, [128, 512], F32) for i in range(8)]

    class Carver:
        def __init__(self):
            self.off = 0

        def f32(self, n):
            v = arena[:, self.off:self.off + n]
            self.off += n
            assert self.off <= ARENA_WORDS, self.off
            return v

        def bf16(self, n):
            w = (n + 1) // 2
            v = arena[:, self.off:self.off + w].bitcast(BF16)
            self.off += w
            assert self.off <= ARENA_WORDS, self.off
            return v

    S.dma("sp", cp[:, :], colpack, writes=["cp"])
    S.dma("sp", rp[:, :], rowpack, writes=["rp"])
    S.dma("sp", cm[:, :, :], cmats, writes=["cm"])
    S.dma("sp", mask0[:, :], mask0_d, writes=["mask0"])
    S.op("dve", lambda e: e.tensor_copy(out=cmb[:, :, :], in_=cm[:, :, :]), reads=["cm"], writes=["cmb"])
    ident = cm[:, 0, :]
    ones_f = cm[:, 1, :]
    negU_b = cmb[:, 2, :]
    zeros_b = cmb[:, 3, :]
    negones_b = cmb[:, 4, :]

    WK = {}
    for nm, src, dst, rows in (("ev_in", ev_w_in, wb_ev_in, D), ("ev_out", ev_w_out, wb_ev_out, 2048),
                               ("od_in", od_w_in, wb_od_in, D), ("od_out", od_w_out, wb_od_out, 2048)):
        step = rows // 8
        for i in range(8):
            S.dma("pool", dst[i * step:(i + 1) * step, :], src[i * step:(i + 1) * step, :], pwrites=[("W", nm)])
        WK[nm] = ("W", nm)

    strot = Rot("st", stat)
    wrot = Rot("wch", wch)

    def load_wchunk(wb, wkey, col0):
        buf, key = wrot.next()
        S.dma("sp", buf[:, :, :], wb.rearrange("(k p) n -> p k n", p=128)[:, :, col0:col0 + 128],
              reads=[wkey], writes=[key])
        return buf, key

    def rstd_ops(st_, sk, src, srck, sqb):
        S.op("act", lambda e: e.activation(out=sqb, in_=src, func=AF.Square), reads=[srck], writes=["sqb"])
        S.op("dve", lambda e: e.tensor_reduce(out=st_[:, 0:1], in_=sqb, axis=AX.X, op=ALU.add),
             reads=["sqb"], writes=[(sk, 0)])
        S.op("dve", lambda e: e.tensor_scalar(out=st_[:, 1:2], in0=st_[:, 0:1], scalar1=1.0 / D, scalar2=EPS,
                                              op0=ALU.mult, op1=ALU.add), reads=[(sk, 0)], writes=[(sk, 1)])
        S.op("act", lambda e: e.activation(out=st_[:, 2:3], in_=st_[:, 1:2], func=AF.Sqrt),
             reads=[(sk, 1)], writes=[(sk, 2)])
        S.op("dve", lambda e: e.reciprocal(out=st_[:, 3:4], in_=st_[:, 2:3]), reads=[(sk, 2)], writes=[(sk, 3)])

    def rms_to_hT(src, srckey, wcol0):
        S.barrier()
        cv = Carver()
        xbufs = [cv.f32(D) for _ in range(3)]
        sqbs = [cv.f32(D) for _ in range(2)]
        ssq = cv.f32(32)
        rs = cv.f32(32)
        xrot = Rot("xb", xbufs)
        sqrot = Rot("sqb2", sqbs)
        for t in range(32):
            xb, xk = xrot.next()
            sq_, sqk = sqrot.next()
            S.dma("sp", xb, src[t * 128:(t + 1) * 128, :], reads=[srckey], writes=[xk])
            S.op("act", lambda e, xb=xb, sq_=sq_: e.activation(out=sq_, in_=xb, func=AF.Square), reads=[xk], writes=[sqk])
            S.op("dve", lambda e, sq_=sq_, t=t: e.tensor_reduce(out=ssq[:, t:t + 1], in_=sq_, axis=AX.X, op=ALU.add),
                 reads=[sqk], pwrites=["ssq"])
        S.op("dve", lambda e: e.tensor_scalar(out=rs, in0=ssq, scalar1=1.0 / D, scalar2=EPS, op0=ALU.mult, op1=ALU.add),
             reads=["ssq"], writes=["rs"])
        S.op("act", lambda e: e.activation(out=rs, in_=rs, func=AF.Sqrt), reads=["rs"], writes=["rs"])
        S.op("dve", lambda e: e.reciprocal(out=rs, in_=rs), reads=["rs"], writes=["rs"])
        prot = Rot("psT", [(psb[0], psb[1]), (psb[2], psb[3]), (psb[4], psb[5])])

        def p2_load(t):
            xb, xk = xrot.next()
            S.dma("sp", xb, src[t * 128:(t + 1) * 128, :], reads=[srckey], writes=[xk])
            if t % 2:
                S.op("act", lambda e, xb=xb, t=t: e.activation(out=xb, in_=xb, func=AF.Copy, scale=rs[:, t:t + 1]),
                     reads=["rs"], writes=[xk])
            else:
                S.op("dve", lambda e, xb=xb, t=t: e.tensor_scalar(
                    out=xb, in0=xb, scalar1=rs[:, t:t + 1], scalar2=None, op0=ALU.mult), reads=["rs"], writes=[xk])
            return xb, xk

        def p2_rest(t, xb, xk):
            (p0, p1), pk = prot.next()
            for k in range(8):
                pt = p0 if k < 4 else p1
                kk = k % 4
                S.op("pe", lambda e, pt=pt, kk=kk, k=k, xb=xb: e.transpose(
                    out=pt[:, kk * 128:(kk + 1) * 128], in_=xb[:, k * 128:(k + 1) * 128], identity=ident),
                    reads=[xk, "cm"], pwrites=[(pk, k // 4)])
            for half in range(2):
                pt = p0 if half == 0 else p1
                eng = "dve" if half == 0 else "act"
                if eng == "dve":
                    S.op("dve", lambda e, pt=pt, half=half, t=t: e.tensor_tensor(
                        out=hT[:, half * 4:(half + 1) * 4, t * 128:(t + 1) * 128],
                        in0=pt[:, :].rearrange("p (k t) -> p k t", k=4),
                        in1=cp[:, wcol0 + half * 4:wcol0 + (half + 1) * 4].unsqueeze(2).broadcast_to([128, 4, 128]),
                        op=ALU.mult), reads=[(pk, half), "cp"], pwrites=["hT"])
                else:
                    for kk in range(4):
                        k = half * 4 + kk
                        S.op("act", lambda e, pt=pt, kk=kk, k=k, t=t: e.activation(
                            out=hT[:, k, t * 128:(t + 1) * 128], in_=pt[:, kk * 128:(kk + 1) * 128],
                            func=AF.Copy, scale=cp[:, wcol0 + k:wcol0 + k + 1]),
                            reads=[(pk, half), "cp"], pwrites=["hT"])

        nxt = p2_load(0)
        for t in range(32):
            cur = nxt
            if t + 1 < 32:
                nxt = p2_load(t + 1)
            p2_rest(t, *cur)

    def proj_fm(ps, pkey, wbuf, wkey, sb):
        for k in range(8):
            S.op("pe", lambda e, k=k: e.matmul(ps[:, :], lhsT=wbuf[:, k, :], rhs=hT[:, k, sb * 512:(sb + 1) * 512],
                                               start=(k == 0), stop=(k == 7)),
                 reads=[wkey, "hT"], writes=[pkey] if k == 0 else (), pwrites=() if k == 0 else [pkey])

    PE_TAPS = list(range(13, 31))
    DVE_TAPS = list(range(0, 13))

    def l0_conv(seq):
        S.barrier()
        cv = Carver()
        upad = [cv.f32(4128) for _ in range(2)]
        acc = [cv.f32(T) for _ in range(2)]
        sig = [cv.f32(512) for _ in range(2)]
        ubf = [cv.bf16(4128) for _ in range(2)]
        dgb = [cv.bf16(len(PE_TAPS) * 128) for _ in range(2)]
        for i in range(2):
            S.op("pool", lambda e, i=i: e.memset(upad[i][:, 0:30], 0.0), writes=[("upadz", i)])
            S.op("pool", lambda e, i=i: e.memset(ubf[i][:, 0:30], 0.0), writes=[("ubfz", i)])
        sigrot = Rot("sig", sig)
        prot = Rot("psB1", [(psb[0], psb[1]), (psb[2], psb[3])])
        crot = Rot("psC", [psb[4], psb[5], psb[6], psb[7]])
        def setup(c):
            X = {"c": c}
            X["wa"], X["wak"] = load_wchunk(wb_ev_in, WK["ev_in"], c * 128)
            X["wg"], X["wgk"] = load_wchunk(wb_ev_in, WK["ev_in"], 1024 + c * 128)
            X["up"], X["upk"] = upad[c % 2], ("upad", c % 2)
            X["ub"], X["ubk"] = ubf[c % 2], ("ubf", c % 2)
            X["ac"], X["ack"] = acc[c % 2], ("acc", c % 2)
            dg = dgb[c % 2].rearrange("p (i m) -> p i m", i=len(PE_TAPS))
            dgk = ("dg", c % 2)
            X["dg"], X["dgk"] = dg, dgk
            for i, k in enumerate(PE_TAPS):
                col = CP_DWW + c * 31 + k
                S.op("act", lambda e, dg=dg, i=i, col=col: e.activation(
                    out=dg[:, i, :], in_=cmb[:, 0, :], func=AF.Copy, scale=cp[:, col:col + 1]),
                    reads=["cmb", "cp"], writes=[dgk] if i == 0 else (), pwrites=() if i == 0 else [dgk])
            return X

        def P_block(X, sb):
            c, up, ub, upk, ubk = X["c"], X["up"], X["ub"], X["upk"], X["ubk"]
            (pa, pb), pk = prot.next()
            proj_fm(pa, (pk, "a"), X["wa"], X["wak"], sb)
            proj_fm(pb, (pk, "b"), X["wg"], X["wgk"], sb)
            sg, sgk = sigrot.next()
            usl = slice(30 + sb * 512, 30 + (sb + 1) * 512)
            S.op("act", lambda e, sg=sg, pb=pb: e.activation(out=sg, in_=pb[:, :], func=AF.Sigmoid),
                 reads=[(pk, "b")], writes=[sgk])
            S.op("dve", lambda e, sg=sg, pa=pa, up=up, usl=usl: e.tensor_tensor(
                out=up[:, usl], in0=pa[:, :], in1=sg, op=ALU.mult),
                reads=[(pk, "a"), sgk, ("upadz", c % 2)], pwrites=[upk])
            S.op("act", lambda e, up=up, ub=ub, usl=usl: e.activation(out=ub[:, usl], in_=up[:, usl], func=AF.Copy),
                 reads=[upk, ("ubfz", c % 2)], pwrites=[ubk])

        def tap(X, n_, k):
            c, up, ac, upk, ack = X["c"], X["up"], X["ac"], X["upk"], X["ack"]
            col = CP_DWW + c * 31 + k
            if n_ == 0:
                S.op("dve", lambda e, up=up, ac=ac, col=col, c=c, k=k: e.tensor_scalar(
                    out=ac, in0=up[:, k:k + T], scalar1=cp[:, col:col + 1],
                    scalar2=cp[:, CP_DWB + c:CP_DWB + c + 1], op0=ALU.mult, op1=ALU.add),
                    reads=[upk, "cp"], writes=[ack])
            else:
                S.op("dve", lambda e, up=up, ac=ac, col=col, k=k: e.scalar_tensor_tensor(
                    out=ac, in0=up[:, k:k + T], scalar=cp[:, col:col + 1], in1=ac,
                    op0=ALU.mult, op1=ALU.add),
                    reads=[upk, "cp"], writes=[ack])

        def C_block(X, sb):
            dg, dgk, ub, ubk, ac, ack = X["dg"], X["dgk"], X["ub"], X["ubk"], X["ac"], X["ack"]
            pc, pck = crot.next()
            for i, k in enumerate(PE_TAPS):
                S.op("pe", lambda e, pc=pc, dg=dg, i=i, k=k, sb=sb, ub=ub: e.matmul(
                    pc[:, :], lhsT=dg[:, i, :], rhs=ub[:, k + sb * 512:k + (sb + 1) * 512],
                    start=(i == 0), stop=(i == len(PE_TAPS) - 1)),
                    reads=[dgk, ubk], writes=[pck] if i == 0 else (), pwrites=() if i == 0 else [pck])
            S.op("dve", lambda e, pc=pc, ac=ac, sb=sb: e.tensor_tensor(
                out=ac[:, sb * 512:(sb + 1) * 512], in0=pc[:, :], in1=ac[:, sb * 512:(sb + 1) * 512], op=ALU.add),
                reads=[pck], writes=[ack])

        Xn = setup(0)
        for sb in range(8):
            P_block(Xn, sb)
        for c in range(8):
            X = Xn
            if c + 1 < 8:
                Xn = setup(c + 1)
            for n_, k in enumerate(DVE_TAPS):
                tap(X, n_, k)
                if n_ < 8:
                    if c + 1 < 8:
                        P_block(Xn, n_)
                    C_block(X, n_)
            S.dma("pool", conv_s[c], X["ac"], reads=[X["ack"]], pwrites=[("conv_s", c)])

    def l0_ln(seq):
        S.barrier()
        cv = Carver()
        mean = cv.f32(T)
        rstd = cv.f32(T)
        cvts = [cv.bf16(4096) for _ in range(2)]
        sqs_ = [cv.bf16(4096) for _ in range(2)]
        tmp512 = cv.f32(512)
        cvk = [("conv_s", c) for c in range(8)]
        cvrot = Rot("cvb", cvts)
        sqrot = Rot("sqb3", sqs_)
        psrot = Rot("psLN", [(psb[4], psb[5]), (psb[6], psb[7])])
        ones_b = cmb[:, 1, :]
        for sb in range(8):
            cvt, cvtk = cvrot.next()
            sq, sqk = sqrot.next()
            S.dma("pool", cvt.rearrange("p (c t) -> p c t", c=8),
                  conv_s.rearrange("c p t -> p c t")[:, :, sb * 512:(sb + 1) * 512], reads=cvk, writes=[cvtk])
            S.op("act", lambda e, sq=sq, cvt=cvt: e.activation(out=sq, in_=cvt, func=AF.Square), reads=[cvtk], writes=[sqk])
            (ps1, ps2), pk = psrot.next()
            for c in range(8):
                S.op("pe", lambda e, c=c, ps1=ps1, cvt=cvt: e.matmul(ps1[:, :], lhsT=ones_b, rhs=cvt[:, c * 512:(c + 1) * 512],
                                                                    start=(c == 0), stop=(c == 7)),
                     reads=[cvtk, "cmb"], writes=[(pk, 1)] if c == 0 else (), pwrites=() if c == 0 else [(pk, 1)])
            for c in range(8):
                S.op("pe", lambda e, c=c, ps2=ps2, sq=sq: e.matmul(ps2[:, :], lhsT=ones_b, rhs=sq[:, c * 512:(c + 1) * 512],
                                                                   start=(c == 0), stop=(c == 7)),
                     reads=[sqk, "cmb"], writes=[(pk, 2)] if c == 0 else (), pwrites=() if c == 0 else [(pk, 2)])
            msl = mean[:, sb * 512:(sb + 1) * 512]
            rsl = rstd[:, sb * 512:(sb + 1) * 512]
            S.op("act", lambda e, msl=msl, ps1=ps1: e.activation(out=msl, in_=ps1[:, :], func=AF.Copy, scale=1.0 / 1024),
                 reads=[(pk, 1)], pwrites=["mean"])
            S.op("dve", lambda e, msl=msl: e.tensor_tensor(out=tmp512, in0=msl, in1=msl, op=ALU.mult),
                 reads=["mean"], writes=["tmp512"])
            S.op("dve", lambda e, rsl=rsl, ps2=ps2: e.scalar_tensor_tensor(out=rsl, in0=ps2[:, :], scalar=1.0 / 1024,
                                                                           in1=tmp512, op0=ALU.mult, op1=ALU.subtract),
                 reads=[(pk, 2), "tmp512"], pwrites=["rstd"])
            S.op("dve", lambda e, rsl=rsl: e.tensor_scalar(out=rsl, in0=rsl, scalar1=EPS, scalar2=None, op0=ALU.add),
                 reads=["rstd"], pwrites=["rstd"])
            S.op("act", lambda e, rsl=rsl: e.activation(out=rsl, in_=rsl, func=AF.Sqrt),
                 reads=["rstd"], pwrites=["rstd"])
            S.op("dve", lambda e, rsl=rsl: e.reciprocal(out=rsl, in_=rsl), reads=["rstd"], pwrites=["rstd"])
        S.barrier()
        cv = Carver()
        cv.f32(2 * T)
        cch = [cv.f32(T) for _ in range(2)]
        ych = [cv.bf16(T) for _ in range(2)]
        sgt = [cv.f32(512) for _ in range(2)]
        sgrot = Rot("sgt", sgt)
        prot = Rot("psB2", [psb[0], psb[1], psb[2], psb[3]])
        for c in range(8):
            wg, wgk = load_wchunk(wb_ev_in, WK["ev_in"], 2048 + c * 128)
            cc = cch[c % 2]
            cck = ("cch", c % 2)
            yc = ych[c % 2]
            yck = ("ych", c % 2)
            S.dma("sp", cc, conv_s[c], reads=[("conv_s", c)], writes=[cck])
            for sb in range(8):
                ps, pk = prot.next()
                proj_fm(ps, pk, wg, wgk, sb)
                sg, sgk = sgrot.next()
                sl = slice(sb * 512, (sb + 1) * 512)
                S.op("act", lambda e, sg=sg, ps=ps: e.activation(out=sg, in_=ps[:, :], func=AF.Silu),
                     reads=[pk], writes=[sgk])
                S.op("dve", lambda e, cc=cc, sl=sl: e.tensor_tensor(out=cc[:, sl], in0=cc[:, sl], in1=mean[:, sl],
                                                                    op=ALU.subtract),
                     reads=[cck], pwrites=[cck])
                S.op("pool", lambda e, cc=cc, sl=sl: e.tensor_tensor(out=cc[:, sl], in0=cc[:, sl], in1=rstd[:, sl],
                                                                     op=ALU.mult),
                     reads=[cck], pwrites=[cck])
                S.op("act", lambda e, cc=cc, sl=sl, c=c: e.activation(
                    out=cc[:, sl], in_=cc[:, sl], func=AF.Silu, scale=cp[:, CP_LNW + c:CP_LNW + c + 1],
                    bias=cp[:, CP_LNB + c:CP_LNB + c + 1]), reads=[cck, "cp"], pwrites=[cck])
                S.op("dve", lambda e, cc=cc, sl=sl, sg=sg, yc=yc: e.tensor_tensor(
                    out=yc[:, sl], in0=cc[:, sl], in1=sg, op=ALU.mult),
                    reads=[cck, sgk], pwrites=[yck])
            S.dma("pool", y_s[c], yc, reads=[yck], pwrites=[("y_s", c)])

    def l0_attn(seq):
        S.barrier()
        cv = Carver()
        gaT = cv.bf16(T)
        qTb = cv.bf16(T)
        kTb = cv.bf16(T)
        vtok = cv.bf16(T)
        yhb = [cv.bf16(T) for _ in range(2)]
        e_b = [cv.f32(512) for _ in range(3)]
        spf = [cv.f32(512) for _ in range(2)]
        spb = [cv.bf16(512) for _ in range(4)]
        wtb = [cv.bf16(512) for _ in range(3)]
        wtf = [cv.f32(512) for _ in range(2)]
        Sf = cv.f32(512)
        Sb = [cv.bf16(512) for _ in range(3)]
        scale = 128.0 ** -0.5
        erot, sfrot, sbrot, wbrot, wfrot, Sbrot = (Rot("eb", e_b), Rot("spf", spf), Rot("spb", spb),
                                                   Rot("wtb", wtb), Rot("wtf", wtf), Rot("Sb", Sb))
        zrot = Rot("zp", [psb[0], psb[1]])
        arot = Rot("ap", [psb[2], psb[3]])
        orot = Rot("oT", [psb[4], psb[5]])
        prot = Rot("pp", [psb[6], psb[7]])
        for h in range(8):
            wq, wqk = load_wchunk(wb_ev_in, WK["ev_in"], 3072 + h * 128)
            wk_, wkk = load_wchunk(wb_ev_in, WK["ev_in"], 4096 + h * 128)
            wv, wvk = load_wchunk(wb_ev_in, WK["ev_in"], 5120 + h * 128)
            wga, wgak = load_wchunk(wb_ev_in, WK["ev_in"], 6144 + h * 128)
            for sb in range(8):
                sl = slice(sb * 512, (sb + 1) * 512)
                ps, pk = prot.next()
                proj_fm(ps, pk, wq, wqk, sb)
                S.op("act", lambda e, ps=ps, sl=sl: e.activation(out=qTb[:, sl], in_=ps[:, :], func=AF.Copy, scale=scale),
                     reads=[pk], pwrites=["qT"])
                ps, pk = prot.next()
                proj_fm(ps, pk, wk_, wkk, sb)
                S.op("dve", lambda e, ps=ps, sl=sl: e.tensor_copy(out=kTb[:, sl], in_=ps[:, :]),
                     reads=[pk], pwrites=["kT"])
                ps, pk = prot.next()
                proj_fm(ps, pk, wga, wgak, sb)
                S.op("act", lambda e, ps=ps, sl=sl: e.activation(out=gaT[:, sl], in_=ps[:, :], func=AF.Silu),
                     reads=[pk], pwrites=["gaT"])
                ps, pk = prot.next()
                for j in range(4):
                    tt = sb * 4 + j
                    for k in range(8):
                        S.op("pe", lambda e, ps=ps, j=j, k=k, tt=tt, wv=wv: e.matmul(
                            ps[:, j * 128:(j + 1) * 128], lhsT=hT[:, k, tt * 128:(tt + 1) * 128], rhs=wv[:, k, :],
                            start=(k == 0), stop=(k == 7)),
                            reads=[wvk, "hT"], writes=[pk] if (k == 0 and j == 0) else (),
                            pwrites=() if (k == 0 and j == 0) else [pk])
                S.op("dve", lambda e, ps=ps, sl=sl: e.tensor_copy(out=vtok[:, sl], in_=ps[:, :]),
                     reads=[pk], pwrites=["vtok"])
            yh = yhb[h % 2]
            yhk = ("yh", h % 2)
            def stageA(Q, b, first, last):
                qs, oT, ok = Q
                if first:
                    S.op("pe", lambda e, oT=oT: e.matmul(oT[:, :], lhsT=zeros_b, rhs=qTb[:, 0:512], start=True, stop=False),
                         reads=["cmb", "qT"], writes=[ok])
                    S.op("pool", lambda e: e.memset(Sf, 0.0), writes=["Sf"])
                c0 = max(0, 128 * (b - 4 * qs))
                W = 512 - c0
                diag = b >= 4 * qs
                q0 = qs * 512 + c0
                zp, zk = zrot.next()
                S.op("pe", lambda e, zp=zp, b=b, q0=q0, W=W: e.matmul(
                    zp[:, 0:W], lhsT=kTb[:, b * 128:(b + 1) * 128], rhs=qTb[:, q0:q0 + W], start=True, stop=True),
                    reads=["kT", "qT"], writes=[zk])
                eb, ek = erot.next()
                S.op("act", lambda e, eb=eb, zp=zp, W=W: e.activation(out=eb[:, 0:W], in_=zp[:, 0:W], func=AF.Exp),
                     reads=[zk], writes=[ek])
                sb_, sbk = sbrot.next()
                if diag:
                    sf_, sfk = sfrot.next()
                    S.op("act", lambda e, eb=eb, sf_=sf_, W=W: e.activation(
                        out=sf_[:, 0:W], in_=eb[:, 0:W], func=AF.Ln, bias=1.0), reads=[ek], writes=[sfk])
                    S.op("dve", lambda e, sf_=sf_, sb_=sb_, W=W: e.tensor_tensor(
                        out=sb_[:, 0:W], in0=sf_[:, 0:W], in1=mask0[:, 0:W], op=ALU.mult),
                        reads=[sfk, "mask0"], writes=[sbk])
                else:
                    S.op("act", lambda e, eb=eb, sb_=sb_, W=W: e.activation(
                        out=sb_[:, 0:W], in_=eb[:, 0:W], func=AF.Ln, bias=1.0), reads=[ek], writes=[sbk])
                Snext = None
                if not last:
                    S.op("dve", lambda e, sb_=sb_, c0=c0, W=W: e.tensor_tensor(
                        out=Sf[:, c0:512], in0=Sf[:, c0:512], in1=sb_[:, 0:W], op=ALU.add),
                        reads=[sbk], writes=["Sf"])
                    sbn, sbnk = Sbrot.next()
                    S.op("dve", lambda e, sbn=sbn: e.tensor_copy(out=sbn, in_=Sf), reads=["Sf"], writes=[sbnk])
                    Snext = (sbn, sbnk)
                return (b, c0, W, diag, q0, sb_, sbk, Snext)

            def stageB1(Q, st, Scur, first):
                b, c0, W, diag, q0, sb_, sbk, _ = st
                ap_, ak = arot.next()
                S.op("pe", lambda e, ap_=ap_, b=b, q0=q0, W=W: e.matmul(
                    ap_[:, 0:W], lhsT=kTb[:, b * 128:(b + 1) * 128], rhs=qTb[:, q0:q0 + W], start=True, stop=False),
                    reads=["kT", "qT"], writes=[ak])
                S.op("pe", lambda e, ap_=ap_, sb_=sb_, W=W, first=first: e.matmul(
                    ap_[:, 0:W], lhsT=negU_b, rhs=sb_[:, 0:W], start=False, stop=first),
                    reads=[sbk, "cmb"], pwrites=[ak])
                if not first:
                    S.op("pe", lambda e, ap_=ap_, Scur=Scur, c0=c0, W=W: e.matmul(
                        ap_[:, 0:W], lhsT=negones_b, rhs=Scur[0][:, c0:512], start=False, stop=True),
                        reads=[Scur[1], "cmb"], pwrites=[ak])
                wb_, wbk = wbrot.next()
                if diag:
                    wf_, wfk = wfrot.next()
                    S.op("act", lambda e, wf_=wf_, ap_=ap_, W=W: e.activation(out=wf_[:, 0:W], in_=ap_[:, 0:W], func=AF.Exp),
                         reads=[ak], writes=[wfk])
                    S.op("dve", lambda e, wf_=wf_, wb_=wb_, W=W: e.tensor_tensor(
                        out=wb_[:, 0:W], in0=wf_[:, 0:W], in1=mask0[:, 0:W], op=ALU.mult),
                        reads=[wfk, "mask0"], writes=[wbk])
                else:
                    S.op("act", lambda e, wb_=wb_, ap_=ap_, W=W: e.activation(out=wb_[:, 0:W], in_=ap_[:, 0:W], func=AF.Exp),
                         reads=[ak], writes=[wbk])
                return (Q, b, c0, W, wb_, wbk)

            def stageB2(w):
                (qs, oT, ok), b, c0, W, wb_, wbk = w
                S.op("pe", lambda e, oT=oT, b=b, wb_=wb_, c0=c0, W=W: e.matmul(
                    oT[:, c0:512], lhsT=vtok[:, b * 128:(b + 1) * 128], rhs=wb_[:, 0:W], start=False, stop=(b == 0)),
                    reads=["vtok", wbk], pwrites=[ok])
                if b == 0:
                    sl = slice(qs * 512, (qs + 1) * 512)
                    S.op("dve", lambda e, oT=oT, sl=sl, yh=yh: e.tensor_tensor(
                        out=yh[:, sl], in0=oT[:, :], in1=gaT[:, sl], op=ALU.mult),
                        reads=[ok, "gaT"], pwrites=[yhk])

            steps = []
            for qs in range(8):
                oT, ok = orot.next()
                Q = (qs, oT, ok)
                nb = 4 * qs + 4
                for n, b in enumerate(range(nb - 1, -1, -1)):
                    steps.append((Q, b, n == 0, n == nb - 1))
            nxt = stageA(*steps[0])
            Scur = None
            wprev = None
            for i, (Q, b, first, last) in enumerate(steps):
                cur = nxt
                if i + 1 < len(steps):
                    nxt = stageA(*steps[i + 1])
                wcur = stageB1(Q, cur, None if first else Scur, first)
                if wprev is not None:
                    stageB2(wprev)
                wprev = wcur
                Scur = cur[7]
            stageB2(wprev)
            S.dma("pool", y_s[8 + h], yh, reads=[yhk], pwrites=[("y_s", 8 + h)])

    def outproj(wb, wkey, res_src, res_key, dst, dst_key, final):
        S.barrier()
        cv = Carver()
        wout = cv.bf16(16 * D)
        ytl = [cv.bf16(16 * 128) for _ in range(2)]
        hob = [cv.f32(D) for _ in range(2)]
        xbufs = [cv.f32(D) for _ in range(2)]
        sqb = cv.f32(D)
        wout3 = wout.rearrange("p (c n) -> p c n", c=16)
        S.dma("sp", wout3, wb.rearrange("(c p) n -> p c n", p=128), reads=[wkey], writes=["wout"])
        yrot = Rot("ytl", ytl)
        hrot = Rot("hob", hob)
        xrot = Rot("xb", xbufs)
        prot = Rot("psO", [(psb[0], psb[1]), (psb[2], psb[3])])
        ykeys = [("y_s", c) for c in range(16)]
        for tt in range(32):
            yt, ytk = yrot.next()
            yt3 = yt.rearrange("p (c t) -> p c t", c=16)
            S.dma("sp", yt3, y_s.rearrange("c p t -> p c t")[:, :, tt * 128:(tt + 1) * 128], reads=ykeys, writes=[ytk])
            xb, xk = xrot.next()
            S.dma("sp", xb, res_src[tt * 128:(tt + 1) * 128, :], reads=[res_key], writes=[xk])
            (p0, p1), pk = prot.next()
            for n, pp in enumerate((p0, p1)):
                for c in range(16):
                    S.op("pe", lambda e, pp=pp, c=c, n=n, yt3=yt3: e.matmul(
                        pp[:, :], lhsT=yt3[:, c, :], rhs=wout3[:, c, n * 512:(n + 1) * 512], start=(c == 0), stop=(c == 15)),
                        reads=[ytk, "wout"], writes=[(pk, n)] if c == 0 else (), pwrites=() if c == 0 else [(pk, n)])
            ho, hk = hrot.next()
            for n, pp in enumerate((p0, p1)):
                S.op("dve", lambda e, pp=pp, n=n, ho=ho, xb=xb: e.tensor_tensor(
                    out=ho[:, n * 512:(n + 1) * 512], in0=pp[:, :], in1=xb[:, n * 512:(n + 1) * 512], op=ALU.add),
                    reads=[(pk, n), xk], pwrites=[hk])
            if final:
                st_, sk = strot.next()
                rstd_ops(st_, sk, ho, hk, sqb)
                S.op("dve", lambda e, ho=ho, st_=st_: e.scalar_tensor_tensor(
                    out=ho, in0=ho, scalar=st_[:, 3:4], in1=rp[:, RP_FNORM:RP_FNORM + D],
                    op0=ALU.mult, op1=ALU.mult), reads=[(sk, 3), "rp", hk], writes=[hk])
            S.dma("pool", dst[tt * 128:(tt + 1) * 128, :], ho, reads=[hk], pwrites=[dst_key])


    xt_s = dscr("xt_s", [T, 2048], F32)
    bt_s = dscr("bt_s", [T, 512], BF16)
    bc_s = dscr("bc_s", [8, 128, T], BF16)
    z_s = dscr("z_s", [T, 2048], F32)
    dtl_s = dscr("dtl_s", [T, 64], F32)
    arow = A("arow", [128, 32], F32)
    S.op("act", lambda e: e.activation(out=arow[:, :], in_=rp[:, RP_ALOG:RP_ALOG + 32], func=AF.Exp),
         reads=["rp"], writes=["arow"])
    ustrict = cm[:, 6, :]
    tri_f = cm[:, 5, :]

    def l1_inproj(seq):
        S.barrier()
        cv = Carver()
        xpad = [cv.f32(4100) for _ in range(2)]
        acc = [cv.f32(T) for _ in range(2)]
        xtb = cv.f32(T)
        xtb_bf = xtb[:, 0:2048].bitcast(BF16)
        for i in range(2):
            S.op("pool", lambda e, i=i: e.memset(xpad[i][:, 0:3], 0.0), writes=[("xpadz", i)])
        prot = Rot("psL", [psb[0], psb[1], psb[6], psb[7]])
        trot = Rot("psX", [psb[2], psb[3], psb[4], psb[5]])
        def l1_proj(cc):
                w, wk = load_wchunk(wb_od_in, WK["od_in"], 2048 + cc * 128)
                xp = xpad[cc % 2]
                xpk = ("xpad", cc % 2)
                ac = acc[cc % 2]
                ack = ("acc1", cc % 2)
                for sb in range(8):
                    ps, pk = prot.next()
                    proj_fm(ps, pk, w, wk, sb)
                    S.op("act", lambda e, ps=ps, xp=xp, sb=sb: e.activation(
                        out=xp[:, 3 + sb * 512:3 + (sb + 1) * 512], in_=ps[:, :], func=AF.Copy),
                        reads=[pk, ("xpadz", cc % 2)], pwrites=[xpk])
                return xp, xpk, ac, ack

        def l1_post(cc, xp, xpk, ac, ack):
                for k in range(4):
                    col = CP_CW + cc * 4 + k
                    if k == 0:
                        S.op("dve", lambda e, xp=xp, ac=ac, col=col, cc=cc: e.tensor_scalar(
                            out=ac, in0=xp[:, 0:T], scalar1=cp[:, col:col + 1],
                            scalar2=cp[:, CP_CB + cc:CP_CB + cc + 1], op0=ALU.mult, op1=ALU.add),
                            reads=[xpk, "cp"], writes=[ack])
                    else:
                        S.op("dve", lambda e, xp=xp, ac=ac, col=col, k=k: e.scalar_tensor_tensor(
                            out=ac, in0=xp[:, k:k + T], scalar=cp[:, col:col + 1], in1=ac,
                            op0=ALU.mult, op1=ALU.add), reads=[xpk, "cp"], writes=[ack])
                S.op("act", lambda e, ac=ac: e.activation(out=ac, in_=ac, func=AF.Silu), reads=[ack], writes=[ack])
                if cc >= 16:
                    S.dma("pool", bc_s[cc - 16], ac, reads=[ack], pwrites=[("bc_s", cc - 16)])
                if cc < 20:
                    isb = cc >= 16
                    stg = xtb_bf if isb else xtb
                    for q in range(8):
                        pt, ptk = trot.next()
                        for j in range(4):
                            tt = q * 4 + j
                            S.op("pe", lambda e, pt=pt, j=j, tt=tt, ac=ac: e.transpose(
                                out=pt[:, j * 128:(j + 1) * 128], in_=ac[:, tt * 128:(tt + 1) * 128], identity=ident),
                                reads=[ack, "cm"], pwrites=[ptk])
                        if q % 2 == 0:
                            S.op("dve", lambda e, pt=pt, q=q, stg=stg: e.tensor_copy(
                                out=stg[:, q * 512:(q + 1) * 512], in_=pt[:, :]), reads=[ptk], pwrites=["xtb"])
                        else:
                            S.op("act", lambda e, pt=pt, q=q, stg=stg: e.activation(
                                out=stg[:, q * 512:(q + 1) * 512], in_=pt[:, :], func=AF.Copy), reads=[ptk], pwrites=["xtb"])
                    if isb:
                        S.dma("pool", bt_s.rearrange("(tt p) c -> p tt c", p=128)[:, :, (cc - 16) * 128:(cc - 15) * 128],
                              stg.rearrange("p (tt c) -> p tt c", c=128), reads=["xtb"], pwrites=["bt_s"])
                    else:
                        S.dma("pool", xt_s.rearrange("(tt p) c -> p tt c", p=128)[:, :, cc * 128:(cc + 1) * 128],
                              stg.rearrange("p (tt c) -> p tt c", c=128), reads=["xtb"], pwrites=["xt_s"])

        nxt_ = l1_proj(0)
        for cc in range(24):
            cur_ = nxt_
            if cc + 1 < 24:
                nxt_ = l1_proj(cc + 1)
            l1_post(cc, *cur_)
        S.barrier()
        cv = Carver()
        wz = cv.bf16(8 * 2048)
        wz3 = wz.rearrange("p (k n) -> p k n", k=8)
        wdt = cv.bf16(8 * 32)
        wdt3 = wdt.rearrange("p (k n) -> p k n", k=8)
        ztl = [cv.f32(2048) for _ in range(2)]
        dtl = [cv.f32(64) for _ in range(2)]
        tmpd = [cv.f32(32) for _ in range(2)]
        S.dma("sp", wz3, wb_od_in.rearrange("(k p) n -> p k n", p=128)[:, :, 0:2048], reads=[WK["od_in"]], writes=["wz"])
        S.dma("sp", wdt3, wb_od_in.rearrange("(k p) n -> p k n", p=128)[:, :, 5120:5152], reads=[WK["od_in"]], writes=["wdt"])
        zrot = Rot("ztl", ztl)
        drot = Rot("dtl", dtl)
        tdrot = Rot("tmpd", tmpd)
        prot = Rot("psZ", [psb[0], psb[1], psb[2], psb[3], psb[4], psb[5]])
        dprot = Rot("psD", [psb[6], psb[7]])
        for tt in range(32):
            zt, ztk = zrot.next()
            tsl = slice(tt * 128, (tt + 1) * 128)
            for g in range(4):
                ps, pk = prot.next()
                for k in range(8):
                    S.op("pe", lambda e, ps=ps, k=k, g=g, tsl=tsl: e.matmul(
                        ps[:, :], lhsT=hT[:, k, tsl], rhs=wz3[:, k, g * 512:(g + 1) * 512], start=(k == 0), stop=(k == 7)),
                        reads=["wz", "hT"], writes=[pk] if k == 0 else (), pwrites=() if k == 0 else [pk])
                S.op("act", lambda e, ps=ps, zt=zt, g=g: e.activation(out=zt[:, g * 512:(g + 1) * 512], in_=ps[:, :], func=AF.Silu),
                     reads=[pk], pwrites=[ztk])
            S.dma("pool", z_s[tsl, :], zt, reads=[ztk], pwrites=["z_s"])
            ps, pk = dprot.next()
            for k in range(8):
                S.op("pe", lambda e, ps=ps, k=k, tsl=tsl: e.matmul(
                    ps[:, 0:32], lhsT=hT[:, k, tsl], rhs=wdt3[:, k, :], start=(k == 0), stop=(k == 7)),
                    reads=["wdt", "hT"], writes=[pk] if k == 0 else (), pwrites=() if k == 0 else [pk])
            dl, dlk = drot.next()
            td, tdk = tdrot.next()
            S.op("dve", lambda e, ps=ps, td=td: e.tensor_tensor(out=td, in0=ps[:, 0:32], in1=rp[:, RP_DTB:RP_DTB + 32], op=ALU.add),
                 reads=[pk, "rp"], writes=[tdk])
            S.op("act", lambda e, td=td: e.activation(out=td, in_=td, func=AF.Exp), reads=[tdk], writes=[tdk])
            S.op("act", lambda e, td=td, dl=dl: e.activation(out=dl[:, 0:32], in_=td, func=AF.Ln, bias=1.0),
                 reads=[tdk], pwrites=[dlk])
            S.op("dve", lambda e, dl=dl: e.scalar_tensor_tensor(out=dl[:, 32:64], in0=dl[:, 0:32], scalar=-1.0, in1=arow[:, :],
                                                                op0=ALU.mult, op1=ALU.mult),
                 reads=[dlk, "arow"], pwrites=[dlk])
            S.dma("pool", dtl_s[tsl, :], dl, reads=[dlk], pwrites=["dtl_s"])

    def l1_ssd(seq):
        S.barrier()
        cv = Carver()
        state = cv.f32(2048)
        state_bf = cv.bf16(2048)
        xtl = [cv.f32(2048) for _ in range(1)]
        ztl = [cv.f32(2048) for _ in range(1)]
        ytoks = [cv.f32(2048) for _ in range(2)]
        tmpys = [cv.f32(512) for _ in range(2)]
        sqs = cv.bf16(2048)
        ybf = cv.bf16(2048)
        xs = cv.bf16(2048)
        xst = cv.bf16(2048)
        dxs = cv.f32(2048)
        btl = [cv.bf16(512) for _ in range(2)]
        bctl = [cv.bf16(1024) for _ in range(2)]
        dtll = [cv.f32(64) for _ in range(2)]
        ecs = cv.f32(32)
        etot = cv.f32(32)
        cbm = cv.bf16(512)
        ULb = [cv.f32(1024) for _ in range(2)]
        Eb = [cv.bf16(1024) for _ in range(2)]
        MTb = [cv.bf16(1024) for _ in range(2)]
        gsts = [cv.f32(8) for _ in range(2)]
        yTc = [cv.bf16(2048) for _ in range(1)]
        S.op("pool", lambda e: e.memset(state, 0.0), writes=[("state", g_) for g_ in range(4)])
        S.op("pool", lambda e: e.memset(state_bf, 0.0), writes=[("state_bf", g_) for g_ in range(4)])
        xrot, zrot, brot, bcrot, dlrot = Rot("xtl", xtl), Rot("ztl1", ztl), Rot("btl", btl), Rot("bctl", bctl), Rot("dtll", dtll)
        ulrot, erot, mrot, ytrot = Rot("UL", ULb), Rot("E", Eb), Rot("MT", MTb), Rot("yTc", yTc)
        B_D = [psb[0], psb[1]]
        drot = Rot("psDD", [(psb[0], psb[1]), (psb[2], psb[3])])
        def tail_piece(P, piece):
            c_, ytok, ytk, gst, gk = P["c"], P["ytok"], P["ytk"], P["gst"], P["gk"]
            if piece == 0:
                zt, ztk = P["zt"], P["ztk"]
                S.op("pool", lambda e, ytok=ytok: e.tensor_tensor(out=ytok, in0=ytok, in1=dxs, op=ALU.add), reads=["dxs"], writes=[ytk])
                S.op("pool", lambda e, ytok=ytok, zt=zt: e.tensor_tensor(out=ytok, in0=ytok, in1=zt, op=ALU.mult), reads=[ztk], writes=[ytk])
                S.op("act", lambda e, ytok=ytok: e.activation(out=sqs, in_=ytok, func=AF.Square), reads=[ytk], writes=["sqs"])
            elif piece == 1:
                S.op("dve", lambda e, gst=gst: e.tensor_reduce(out=gst[:, 0:4], in_=sqs.rearrange("p (g d) -> p g d", g=4), axis=AX.X, op=ALU.add),
                     reads=["sqs"], writes=[gk])
                S.op("dve", lambda e, gst=gst: e.tensor_scalar(out=gst[:, 0:4], in0=gst[:, 0:4], scalar1=1.0 / 512, scalar2=EPS, op0=ALU.mult, op1=ALU.add),
                     reads=[gk], writes=[gk])
                S.op("act", lambda e, gst=gst: e.activation(out=gst[:, 0:4], in_=gst[:, 0:4], func=AF.Sqrt), reads=[gk], writes=[gk])
            elif piece == 2:
                S.op("dve", lambda e, gst=gst: e.reciprocal(out=gst[:, 4:8], in_=gst[:, 0:4]), reads=[gk], writes=[gk])
                for g in range(4):
                    gs = slice(g * 512, (g + 1) * 512)
                    S.op("dve", lambda e, g=g, gs=gs, gst=gst, ytok=ytok: e.scalar_tensor_tensor(
                        out=ybf[:, gs], in0=ytok[:, gs], scalar=gst[:, 4 + g:5 + g], in1=rp[:, RP_GNORM + g * 512:RP_GNORM + (g + 1) * 512],
                        op0=ALU.mult, op1=ALU.mult), reads=[gk, "rp", ytk], writes=["ybf"] if g == 0 else (), pwrites=() if g == 0 else ["ybf"])
            else:
                yt_, ytk_ = ytrot.next()
                pbk = ("pb", 7)
                for q in range(4):
                    pbv = psb[7][:, 0:256].bitcast(BF16)
                    for j in range(4):
                        cch_ = q * 4 + j
                        S.op("pe", lambda e, pbv=pbv, j=j, cch_=cch_: e.transpose(
                            out=pbv[:, j * 128:(j + 1) * 128], in_=ybf[:, cch_ * 128:(cch_ + 1) * 128], identity=cmb[:, 0, :]),
                            reads=["ybf", "cmb"], writes=[pbk] if j == 0 else (), pwrites=() if j == 0 else [pbk])
                    S.op("act" if q % 2 else "dve",
                         (lambda e, pbv=pbv, q=q, yt_=yt_: e.activation(out=yt_[:, q * 512:(q + 1) * 512], in_=pbv, func=AF.Copy)) if q % 2 else
                         (lambda e, pbv=pbv, q=q, yt_=yt_: e.tensor_copy(out=yt_[:, q * 512:(q + 1) * 512], in_=pbv)),
                         reads=[pbk], pwrites=[ytk_])
                S.dma("act", y_s.rearrange("c p t -> p c t")[:, :, c_ * 128:(c_ + 1) * 128], yt_.rearrange("p (c t) -> p c t", c=16),
                      reads=[ytk_], pwrites=[("y_s", cc_) for cc_ in range(16)])

        pending = None
        for c in range(32):
            tsl = slice(c * 128, (c + 1) * 128)
            ytok = ytoks[c % 2]
            ytk = ("ytok", c % 2)
            xt, xtk = xrot.next()
            bt, btk = brot.next()
            bct, bctk = bcrot.next()
            dl, dlk = dlrot.next()
            bct3 = bct.rearrange("p (g t) -> p g t", g=8)
            S.dma("sp", xt, xt_s[tsl, :], reads=["xt_s"], writes=[xtk])
            if c >= 1:
                zt, ztk = zrot.next()
                S.dma("sp", zt, z_s[(c - 1) * 128:c * 128, :], reads=["z_s"], writes=[ztk])
                pending["zt"], pending["ztk"] = zt, ztk
            S.dma("sp", bt, bt_s[tsl, :], reads=["bt_s"], writes=[btk])
            S.dma("sp", bct3, bc_s.rearrange("g p t -> p g t")[:, :, tsl], reads=[("bc_s", g) for g in range(8)], writes=[bctk])
            S.dma("sp", dl, dtl_s[tsl, :], reads=["dtl_s"], writes=[dlk])
            S.op("dve", lambda e, xt=xt, dl=dl: e.tensor_tensor(
                out=xs.rearrange("p (h d) -> p h d", h=32), in0=xt.rearrange("p (h d) -> p h d", h=32),
                in1=dl[:, 0:32].unsqueeze(2).broadcast_to([128, 32, 64]), op=ALU.mult),
                reads=[xtk, dlk], writes=["xs"])
            S.op("pe", lambda e, dl=dl: e.matmul(psb[5][:, 0:32], lhsT=ones_f, rhs=dl[:, 32:64], start=True, stop=True),
                 reads=[dlk, "cm"], writes=[("pb", 5)])
            S.op("pe", lambda e, dl=dl: e.matmul(psb[5][:, 32:64], lhsT=tri_f, rhs=dl[:, 32:64], start=True, stop=True),
                 reads=[dlk, "cm"], pwrites=[("pb", 5)])
            S.op("act", lambda e: e.activation(out=etot, in_=psb[5][:, 0:32], func=AF.Exp), reads=[("pb", 5)], writes=["etot"])
            S.op("act", lambda e: e.activation(out=ecs, in_=psb[5][:, 32:64], func=AF.Exp), reads=[("pb", 5)], writes=["ecs"])
            for g in range(4):
                S.op("pe", lambda e, g=g, bct3=bct3: e.matmul(psb[7][:, g * 128:(g + 1) * 128], lhsT=bct3[:, g, :], rhs=bct3[:, 4 + g, :],
                                                              start=True, stop=True),
                     reads=[bctk], writes=[("pb", 7)] if g == 0 else (), pwrites=() if g == 0 else [("pb", 7)])
            S.op("dve", lambda e: e.tensor_tensor(
                out=cbm.rearrange("p (g l) -> p g l", g=4), in0=psb[7][:, :].rearrange("p (g l) -> p g l", g=4),
                in1=tri_f.unsqueeze(1).broadcast_to([128, 4, 128]), op=ALU.mult),
                reads=[("pb", 7), "cm"], writes=["cbm"])
            def stageA(g):
                hsl = slice(g * 8, (g + 1) * 8)
                ul, ulk = ulrot.next()
                ul3 = ul.rearrange("p (h s) -> p h s", h=8)
                S.op("pool", lambda e, ul3=ul3, dl=dl, g=g: e.tensor_tensor(
                    out=ul3, in0=ustrict.unsqueeze(1).broadcast_to([128, 8, 128]),
                    in1=dl[:, 32 + g * 8:40 + g * 8].unsqueeze(2).broadcast_to([128, 8, 128]), op=ALU.mult),
                    reads=[dlk, "cm"], writes=[ulk])
                (d0, d1), dkk = drot.next()
                for hh in range(8):
                    dp = d0 if hh < 4 else d1
                    S.op("pe", lambda e, dp=dp, ul3=ul3, hh=hh: e.matmul(
                        dp[:, (hh % 4) * 128:(hh % 4 + 1) * 128], lhsT=ul3[:, hh, :], rhs=tri_f, start=True, stop=True),
                        reads=[ulk, "cm"], pwrites=[(dkk, hh // 4)])
                E, Ek = erot.next()
                S.op("act", lambda e, E=E, d0=d0: e.activation(out=E[:, 0:512], in_=d0[:, :], func=AF.Exp),
                     reads=[(dkk, 0)], pwrites=[Ek])
                S.op("act", lambda e, E=E, d1=d1: e.activation(out=E[:, 512:1024], in_=d1[:, :], func=AF.Exp),
                     reads=[(dkk, 1)], pwrites=[Ek])
                return E, Ek

            def stageB(g, E, Ek):
                hsl = slice(g * 8, (g + 1) * 8)
                gs = slice(g * 512, (g + 1) * 512)
                E3 = E.rearrange("p (h l) -> p h l", h=8)
                MT, MTk = mrot.next()
                MT3 = MT.rearrange("p (h l) -> p h l", h=8)
                S.op("dve", lambda e, E3=E3, MT3=MT3, g=g: e.tensor_tensor(
                    out=MT3, in0=E3, in1=cbm[:, g * 128:(g + 1) * 128].unsqueeze(1).broadcast_to([128, 8, 128]), op=ALU.mult),
                    reads=[Ek, "cbm"], writes=[MTk])
                S.op("dve", lambda e, E3=E3, gs=gs: e.tensor_tensor(
                    out=xst[:, gs].rearrange("p (h d) -> p h d", h=8), in0=xs[:, gs].rearrange("p (h d) -> p h d", h=8),
                    in1=E3[:, :, 127:128].broadcast_to([128, 8, 64]), op=ALU.mult),
                    reads=[Ek, "xs"], writes=[("xst", g % 2)])
                ib, ibk = psb[4], ("pb", 4)
                nb_, nbk = psb[5], ("pb", 5)
                sn, snk = psb[6], ("pb", 6)
                for hh in range(8):
                    hs = slice(g * 512 + hh * 64, g * 512 + (hh + 1) * 64)
                    hhs = slice(hh * 64, (hh + 1) * 64)
                    S.op("pe", lambda e, MT3=MT3, hh=hh, hs=hs, hhs=hhs: e.matmul(
                        ib[:, hhs], lhsT=MT3[:, hh, :], rhs=xs[:, hs], start=True, stop=True),
                        reads=[MTk, "xs"], writes=[ibk] if hh == 0 else (), pwrites=() if hh == 0 else [ibk])
                S.op("pe", lambda e, g=g, gs=gs, bct3=bct3: e.matmul(nb_[:, :], lhsT=bct3[:, 4 + g, :], rhs=state_bf[:, gs], start=True, stop=True),
                     reads=[bctk, ("state_bf", g)], writes=[nbk])
                S.op("pe", lambda e, g=g, gs=gs, bt=bt: e.matmul(sn[:, :], lhsT=bt[:, g * 128:(g + 1) * 128], rhs=xst[:, gs], start=True, stop=True),
                     reads=[btk, ("xst", g % 2)], writes=[snk])
                tmpy, tmk = tmpys[g % 2], ("tmpy", g % 2)
                S.op("dve", lambda e, tmpy=tmpy, hsl=hsl: e.tensor_tensor(
                    out=tmpy.rearrange("p (h d) -> p h d", h=8), in0=nb_[:, :].rearrange("p (h d) -> p h d", h=8),
                    in1=ecs[:, hsl].unsqueeze(2).broadcast_to([128, 8, 64]), op=ALU.mult),
                    reads=[nbk, "ecs"], writes=[tmk])
                S.op("dve", lambda e, gs=gs, tmpy=tmpy, ytok=ytok: e.tensor_tensor(out=ytok[:, gs], in0=ib[:, :], in1=tmpy, op=ALU.add),
                     reads=[ibk, tmk], pwrites=[ytk])
                S.op("pool", lambda e, gs=gs, hsl=hsl: e.tensor_tensor(
                    out=state[:, gs].rearrange("p (h d) -> p h d", h=8), in0=state[:, gs].rearrange("p (h d) -> p h d", h=8),
                    in1=etot[:, hsl].unsqueeze(2).broadcast_to([128, 8, 64]), op=ALU.mult),
                    reads=["etot"], writes=[("state", g)])
                S.op("dve", lambda e, gs=gs: e.tensor_tensor(out=state[:, gs], in0=sn[:, :], in1=state[:, gs], op=ALU.add),
                     reads=[snk], writes=[("state", g)])
                S.op("act", lambda e, gs=gs: e.activation(out=state_bf[:, gs], in_=state[:, gs], func=AF.Copy),
                     reads=[("state", g)], writes=[("state_bf", g)])

            nxt = stageA(0)
            for g in range(4):
                cur = nxt
                if g < 3:
                    nxt = stageA(g + 1)
                if g == 1:
                    S.op("pool", lambda e, xt=xt: e.tensor_tensor(
                        out=dxs.rearrange("p (h d) -> p h d", h=32), in0=xt.rearrange("p (h d) -> p h d", h=32),
                        in1=rp[:, RP_DSK:RP_DSK + 32].unsqueeze(2).broadcast_to([128, 32, 64]), op=ALU.mult),
                        reads=[xtk, "rp"], writes=["dxs"])
                stageB(g, *cur)
                if pending is not None:
                    tail_piece(pending, g)
            pending = {"c": c, "ytok": ytok, "ytk": ytk, "gst": gsts[c % 2], "gk": ("gst", c % 2)}
        zt, ztk = zrot.next()
        S.dma("sp", zt, z_s[31 * 128:32 * 128, :], reads=["z_s"], writes=[ztk])
        pending["zt"], pending["ztk"] = zt, ztk
        for g in range(4):
            tail_piece(pending, g)

    def layer1(seq):
        rms_to_hT(h1_s, "h1_s", CP_ODNORM)
        l1_inproj(seq)
        l1_ssd(seq)
        outproj(wb_od_out, WK["od_out"], h1_s, "h1_s", out_d[seq], "out", final=True)

    dbg = None
    if stop_after in ("hT", "yconv", "attn"):
        dbg = nc.dram_tensor("dbg", [16, 128, T], BF16, kind="ExternalOutput").ap()
    if stop_after == "conv":
        dbg = nc.dram_tensor("dbg", [8, 128, T], F32, kind="ExternalOutput").ap()

    for seq in range(nseq):
        x_seq = x_d[seq]
        rms_to_hT(x_seq, "x_in", CP_EVNORM)
        if stop_after == "hT":
            for k in range(8):
                S.dma("sp", dbg[k], hT[:, k, :], reads=["hT"], pwrites=["dbg"])
            break
        l0_conv(seq)
        if stop_after == "conv":
            S.dma("sp", dbg, conv_s, reads=[("conv_s", c) for c in range(8)], pwrites=["dbg"])
            break
        l0_ln(seq)
        if stop_after == "yconv":
            S.dma("sp", dbg[0:8], y_s[0:8], reads=[("y_s", c) for c in range(8)], pwrites=["dbg"])
            break
        l0_attn(seq)
        if stop_after == "attn":
            S.dma("sp", dbg, y_s, reads=[("y_s", c) for c in range(16)], pwrites=["dbg"])
            break
        if stop_after == "l0":
            outproj(wb_ev_out, WK["ev_out"], x_seq, "x_in", out_d[seq], "out", final=False)
            continue
        outproj(wb_ev_out, WK["ev_out"], x_seq, "x_in", h1_s, "h1_s", final=False)
        layer1(seq)
    S.barrier()
    S.emit()
    return nc


def host_consts(inp):
    f = np.float32
    cp = np.zeros((128, CP_N), f)

    def colv(v):
        return np.ascontiguousarray(np.asarray(v, f).reshape(-1, 128).T)
    cp[:, CP_EVNORM:CP_EVNORM + 8] = colv(inp["ev_norm_w"][0])
    dw = np.asarray(inp["ev_dw_w"][0], f)
    cp[:, CP_DWW:CP_DWW + 248] = dw.reshape(31, 8, 128).transpose(2, 1, 0).reshape(128, 248)
    cp[:, CP_DWB:CP_DWB + 8] = colv(inp["ev_dw_b"][0])
    cp[:, CP_LNW:CP_LNW + 8] = colv(inp["ev_ln_w"][0])
    cp[:, CP_LNB:CP_LNB + 8] = colv(inp["ev_ln_b"][0])
    cp[:, CP_ODNORM:CP_ODNORM + 8] = colv(inp["od_norm_w"][0])
    cw = np.asarray(inp["od_conv_w"][0], f)
    cp[:, CP_CW:CP_CW + 96] = cw.reshape(4, 24, 128).transpose(2, 1, 0).reshape(128, 96)
    cp[:, CP_CB:CP_CB + 24] = colv(inp["od_conv_b"][0])
    rp = np.zeros((128, RP_N), f)
    rp[:, RP_FNORM:RP_FNORM + 1024] = np.asarray(inp["final_norm_w"], f)[None, :]
    rp[:, RP_GNORM:RP_GNORM + 2048] = np.asarray(inp["od_gnorm_w"][0], f)[None, :]
    rp[:, RP_DSK:RP_DSK + 32] = np.asarray(inp["od_d"][0], f)[None, :]
    rp[:, RP_DTB:RP_DTB + 32] = np.asarray(inp["od_dt_bias"][0], f)[None, :]
    rp[:, RP_ALOG:RP_ALOG + 32] = np.asarray(inp["od_a_log"][0], f)[None, :]
    j = np.arange(128)
    cm = np.zeros((128, 7, 128), f)
    cm[:, 0, :] = np.eye(128, dtype=f)
    cm[:, 1, :] = 1.0
    cm[:, 2, :] = -(j[:, None] >= j[None, :]).astype(f)
    cm[:, 3, :] = 0.0
    cm[:, 4, :] = -1.0
    cm[:, 5, :] = (j[:, None] <= j[None, :]).astype(f)
    cm[:, 6, :] = (j[:, None] > j[None, :]).astype(f)
    mask0 = (j[:, None] < np.arange(512)[None, :]).astype(f)
    return cp, rp, cm, mask0


def make_in_maps(inputs, nseq=2, ncores=NCORES):
    cp, rp, cm, mask0 = host_consts(inputs)
    x = np.asarray(inputs["x"], np.float32)
    common = {
        "ev_w_in": np.ascontiguousarray(np.asarray(inputs["ev_w_in"], np.float32)[0]),
        "ev_w_out": np.ascontiguousarray(np.asarray(inputs["ev_w_out"], np.float32)[0]),
        "od_w_in": np.ascontiguousarray(np.asarray(inputs["od_w_in"], np.float32)[0]),
        "od_w_out": np.ascontiguousarray(np.asarray(inputs["od_w_out"], np.float32)[0]),
        "colpack": cp, "rowpack": rp, "cmats": cm, "mask0": mask0,
    }
    maps = []
    for c in range(ncores):
        m = dict(common)
        m["x"] = np.ascontiguousarray(x[c * nseq:(c + 1) * nseq])
        maps.append(m)
    return maps


def kernel(**inputs):
    nc = build(nseq=2)
    maps = make_in_maps(inputs, 2, NCORES)
    res = run_bass_kernel_spmd(nc, maps, core_ids=list(range(NCORES)))
    outs = [np.asarray(r["out"], np.float32) for r in res.results]
    return np.concatenate(outs, axis=0)
```
